# Optimizing a Trainium2 kernel written in Bass

```python
import math
import jax
import jax.numpy as jnp
from jax import lax
import numpy as np

D_MODEL = 1024
BATCH = 8
SEQ = 8192
DEPTH = 1
DEC_BATCH = 2
DEC_SEQ = 8192
PAST_LEN = 128

HEAD_DIM = 64
A_GROUPS = ((128, 1), (512, 4), (2048, 16))
A_HEADS_PER_GROUP = 4
A_HEADS = A_HEADS_PER_GROUP * len(A_GROUPS)
A_WIDTH = A_HEADS * HEAD_DIM
A_OUT = A_HEADS_PER_GROUP * HEAD_DIM
A_QBLOCK = 64
B_HEADS = 8
B_WIDTH = B_HEADS * HEAD_DIM
GRID_W = 64
NA_ROWS = 8
NA_COLS = 16
NA_QROWS = 8
NA_QCOLS = 16
ROPE_THETA = 500000.0
ROPE_DIMS = HEAD_DIM // 4
D_FF = 4 * D_MODEL
IN_COLS = 3 * A_WIDTH + 3 * B_WIDTH + 2 * D_MODEL
LN_EPS = 1e-5
DEEPNORM_ALPHA = (2.0 * DEPTH) ** 0.25
DEEPNORM_BETA = (8.0 * DEPTH) ** -0.25
NEG_INF = -1e30

kernel_name = "hybrid_dilated_neighbourhood_encoder"


def layer_norm(x, g, b):
    xf = x.astype(jnp.float32)
    mu = jnp.mean(xf, axis=-1, keepdims=True)
    var = jnp.mean(jnp.square(xf - mu), axis=-1, keepdims=True)
    y = (xf - mu) * lax.rsqrt(var + LN_EPS) * g.astype(jnp.float32) + b.astype(jnp.float32)
    return y.astype(x.dtype)


def partial_rope(x):
    s = x.shape[1]
    half = ROPE_DIMS // 2
    inv = ROPE_THETA ** (-jnp.arange(half, dtype=jnp.float32) / half)
    ang = jnp.arange(s, dtype=jnp.float32)[:, None] * inv[None, :]
    cos = jnp.cos(ang)[None, :, None, :]
    sin = jnp.sin(ang)[None, :, None, :]
    xr = x[..., :ROPE_DIMS].astype(jnp.float32)
    x1, x2 = xr[..., :half], xr[..., half:]
    rot = jnp.concatenate([x1 * cos - x2 * sin, x1 * sin + x2 * cos], axis=-1).astype(x.dtype)
    return jnp.concatenate([rot, x[..., ROPE_DIMS:]], axis=-1)


def dilated_window_attention(q, k, v, window, dilation):
    b, s, h, dh = q.shape
    p = window // (2 * dilation)
    l = s // dilation
    qb = math.gcd(A_QBLOCK, l)
    nb = l // qb
    span = qb + 2 * p

    def to_sub(t):
        return t.reshape(b, l, dilation, h, dh).transpose(0, 2, 1, 3, 4)

    qs = to_sub(q).reshape(b, dilation, nb, qb, h, dh)
    pad = ((0, 0), (0, 0), (p, p), (0, 0), (0, 0))
    ks = jnp.pad(to_sub(k), pad)
    vs = jnp.pad(to_sub(v), pad)
    idx = jnp.arange(nb)[:, None] * qb + jnp.arange(span)[None, :]
    kb = jnp.take(ks, idx, axis=2)
    vb = jnp.take(vs, idx, axis=2)
    scores = jnp.einsum('brnqhd,brnkhd->brnhqk', qs, kb,
                        preferred_element_type=jnp.float32) * (dh ** -0.5)
    off = jnp.arange(span)[None, :] - p - jnp.arange(qb)[:, None]
    key_pos = idx - p
    in_seq = (key_pos >= 0) & (key_pos < l)
    valid = (jnp.abs(off) <= p)[None, :, :] & in_seq[:, None, :]
    scores = jnp.where(valid[None, None, :, None, :, :], scores, NEG_INF)
    m = jnp.max(scores, axis=-1, keepdims=True)
    e = jnp.exp(scores - m)
    den = jnp.sum(e, axis=-1, keepdims=True)
    o = jnp.einsum('brnhqk,brnkhd->brnqhd', e / den, vb.astype(jnp.float32))
    lse = (m + jnp.log(den))[..., 0]
    o = o.reshape(b, dilation, l, h, dh).transpose(0, 2, 1, 3, 4).reshape(b, s, h, dh)
    lse = lse.transpose(0, 1, 2, 4, 3).reshape(b, dilation, l, h).transpose(0, 2, 1, 3).reshape(b, s, h)
    return o, lse


def mixer_a(qa, ka, va):
    qa = partial_rope(qa)
    ka = partial_rope(ka)
    outs, lses = [], []
    for g, (window, dilation) in enumerate(A_GROUPS):
        sl = slice(g * A_HEADS_PER_GROUP, (g + 1) * A_HEADS_PER_GROUP)
        o, lse = dilated_window_attention(qa[:, :, sl], ka[:, :, sl], va[:, :, sl], window, dilation)
        outs.append(o)
        lses.append(lse)
    o = jnp.stack(outs, axis=0)
    w = jax.nn.softmax(jnp.stack(lses, axis=0), axis=0)
    return jnp.sum(w[..., None] * o, axis=0)


def neighbourhood_attention(q, k, v, rpb):
    b, s, h, dh = q.shape
    rows = s // GRID_W
    kh, kw = min(NA_ROWS, rows), NA_COLS
    qr, qc = math.gcd(NA_QROWS, rows), NA_QCOLS
    ksr, ksc = min(qr + kh - 1, rows), min(qc + kw - 1, GRID_W)
    nrb, ncb = rows // qr, GRID_W // qc
    r_idx = jnp.arange(rows)
    c_idx = jnp.arange(GRID_W)
    win_r = jnp.clip(r_idx - kh // 2, 0, rows - kh)
    win_c = jnp.clip(c_idx - kw // 2, 0, GRID_W - kw)
    key_r = jnp.minimum(win_r[::qr], rows - ksr)[:, None] + jnp.arange(ksr)[None, :]
    key_c = jnp.minimum(win_c[::qc], GRID_W - ksc)[:, None] + jnp.arange(ksc)[None, :]

    def gather_kv(t):
        t = t.reshape(b, rows, GRID_W, h, dh)
        t = jnp.take(jnp.take(t, key_r, axis=1), key_c, axis=3)
        return t.transpose(0, 1, 3, 2, 4, 5, 6).reshape(b, nrb, ncb, ksr * ksc, h, dh)

    kg = gather_kv(k)
    vg = gather_kv(v)
    qg = q.reshape(b, nrb, qr, ncb, qc, h, dh).transpose(0, 1, 3, 2, 4, 5, 6).reshape(b, nrb, ncb, qr * qc, h, dh)

    q_r = r_idx.reshape(nrb, qr)
    q_c = c_idx.reshape(ncb, qc)
    wr = win_r.reshape(nrb, qr)[:, :, None]
    wc = win_c.reshape(ncb, qc)[:, :, None]
    kr = key_r[:, None, :]
    kc = key_c[:, None, :]
    ok_r = (kr >= wr) & (kr < wr + kh)
    ok_c = (kc >= wc) & (kc < wc + kw)
    dr = jnp.clip(kr - q_r[:, :, None] + NA_ROWS - 1, 0, 2 * NA_ROWS - 2)
    dc = jnp.clip(kc - q_c[:, :, None] + NA_COLS - 1, 0, 2 * NA_COLS - 2)
    bias = rpb.astype(jnp.float32)[:, dr[:, :, :, None, None, None], dc[None, None, None, :, :, :]]
    ok = ok_r[:, :, :, None, None, None] & ok_c[None, None, None, :, :, :]
    bias = jnp.where(ok[None], bias, NEG_INF)
    bias = bias.transpose(1, 4, 0, 2, 5, 3, 6).reshape(nrb, ncb, h, qr * qc, ksr * ksc)

    scores = jnp.einsum('bnmqhd,bnmkhd->bnmhqk', qg, kg,
                        preferred_element_type=jnp.float32) * (dh ** -0.5) + bias[None]
    probs = jax.nn.softmax(scores, axis=-1)
    o = jnp.einsum('bnmhqk,bnmkhd->bnmqhd', probs, vg.astype(jnp.float32))
    o = o.reshape(b, nrb, ncb, qr, qc, h, dh).transpose(0, 1, 3, 2, 4, 5, 6)
    return o.reshape(b, s, h * dh)


def encoder_layer(x, w_in, b_gate, w_branch_a, w_branch_b, w_out, rpb,
                  ln1_g, ln1_b, w_ff1, b_ff1, w_ff2, b_ff2, ln2_g, ln2_b):
    bsz, seq, _ = x.shape
    proj = jnp.einsum('bsd,df->bsf', x, w_in)
    a, c = A_WIDTH, B_WIDTH
    cuts = [a, 2 * a, 3 * a, 3 * a + c, 3 * a + 2 * c, 3 * a + 3 * c, 3 * a + 3 * c + D_MODEL]
    qa, ka, va, qb, kb, vb, ga, gb = jnp.split(proj, cuts, axis=-1)
    ya = mixer_a(qa.reshape(bsz, seq, A_HEADS, HEAD_DIM),
                 ka.reshape(bsz, seq, A_HEADS, HEAD_DIM),
                 va.reshape(bsz, seq, A_HEADS, HEAD_DIM))
    ya = ya.reshape(bsz, seq, A_OUT).astype(x.dtype) @ w_branch_a
    yb = neighbourhood_attention(qb.reshape(bsz, seq, B_HEADS, HEAD_DIM),
                                 kb.reshape(bsz, seq, B_HEADS, HEAD_DIM),
                                 vb.reshape(bsz, seq, B_HEADS, HEAD_DIM), rpb)
    yb = yb.astype(x.dtype) @ w_branch_b
    merged = jax.nn.sigmoid(ga + b_gate[0]) * ya + jax.nn.sigmoid(gb + b_gate[1]) * yb
    x = layer_norm(DEEPNORM_ALPHA * x + merged @ w_out, ln1_g, ln1_b)
    hid = jnp.square(jax.nn.relu(x @ w_ff1 + b_ff1))
    x = layer_norm(DEEPNORM_ALPHA * x + hid @ w_ff2 + b_ff2, ln2_g, ln2_b)
    return x


def setup_inputs(seed: int = 0) -> dict:
    key = jax.random.key(seed)
    ks = jax.random.split(key, 16)

    def nrm(k, shape, scale):
        return jax.random.normal(k, shape, jnp.float32) * scale

    return {
        'x_prompt': nrm(ks[0], (BATCH, SEQ, D_MODEL), 1.0),
        'x_sample': nrm(ks[1], (DEC_BATCH, DEC_SEQ, D_MODEL), 1.0),
        'w_in': nrm(ks[2], (DEPTH, D_MODEL, IN_COLS), D_MODEL ** -0.5),
        'b_gate': nrm(ks[3], (DEPTH, 2, D_MODEL), 0.1),
        'w_branch_a': nrm(ks[4], (DEPTH, A_OUT, D_MODEL), A_OUT ** -0.5),
        'w_branch_b': nrm(ks[5], (DEPTH, B_WIDTH, D_MODEL), B_WIDTH ** -0.5),
        'w_out': nrm(ks[6], (DEPTH, D_MODEL, D_MODEL), DEEPNORM_BETA * D_MODEL ** -0.5),
        'rel_pos_bias': nrm(ks[7], (DEPTH, B_HEADS, 2 * NA_ROWS - 1, 2 * NA_COLS - 1), 0.1),
        'ln1_g': 1.0 + nrm(ks[8], (DEPTH, D_MODEL), 0.02),
        'ln1_b': nrm(ks[9], (DEPTH, D_MODEL), 0.02),
        'w_ff1': nrm(ks[10], (DEPTH, D_MODEL, D_FF), D_MODEL ** -0.5),
        'b_ff1': nrm(ks[11], (DEPTH, D_FF), 0.02),
        'w_ff2': nrm(ks[12], (DEPTH, D_FF, D_MODEL), DEEPNORM_BETA * D_FF ** -0.5),
        'b_ff2': nrm(ks[13], (DEPTH, D_MODEL), 0.02),
        'ln2_g': 1.0 + nrm(ks[14], (DEPTH, D_MODEL), 0.02),
        'ln2_b': nrm(ks[15], (DEPTH, D_MODEL), 0.02),
    }


def reference(x_prompt, x_sample, w_in, b_gate, w_branch_a, w_branch_b, w_out, rel_pos_bias,
              ln1_g, ln1_b, w_ff1, b_ff1, w_ff2, b_ff2, ln2_g, ln2_b):
    y_prompt = x_prompt
    y_sample = x_sample
    for layer in range(DEPTH):
        p = (w_in[layer], b_gate[layer], w_branch_a[layer], w_branch_b[layer], w_out[layer],
             rel_pos_bias[layer], ln1_g[layer], ln1_b[layer], w_ff1[layer], b_ff1[layer],
             w_ff2[layer], b_ff2[layer], ln2_g[layer], ln2_b[layer])
        y_prompt = encoder_layer(y_prompt, *p)
        y_sample = encoder_layer(y_sample, *p)
    return (y_prompt, y_sample)
```

```python
import contextlib
import numpy as np
import concourse.bass as bass
import concourse.mybir as mybir
from concourse.bass_utils import run_bass_kernel_spmd

F32 = mybir.dt.float32
BF16 = mybir.dt.bfloat16
AF = mybir.ActivationFunctionType
ALU = mybir.AluOpType

ENGS = ("pe", "act", "dve", "pool", "sp")
NEG = -30000.0
ALPHA = 2.0 ** 0.25
LN_EPS = 1e-5
T = 512
NKV = 24
NTOK = NKV * T
QT_A = 16
QT_B = 4


class Res:
    __slots__ = ("name", "w", "rs")

    def __init__(self, name):
        self.name = name
        self.w = None
        self.rs = []


class Instr:
    __slots__ = ("eng", "fn", "deps", "signal", "dma", "sem", "val")

    def __init__(self, eng, fn, dma):
        self.eng = eng
        self.fn = fn
        self.deps = []
        self.signal = False
        self.dma = dma
        self.sem = None
        self.val = None


class Prog:
    def __init__(self, nc):
        self.nc = nc
        self.lists = {e: [] for e in ENGS}

    def op(self, eng, fn, reads=(), writes=(), dma_key=None):
        ins = Instr(eng, fn, dma_key)
        deps = []
        for r in reads:
            if r.w is not None:
                deps.append(r.w)
        for w in writes:
            if w.w is not None:
                deps.append(w.w)
            deps.extend(w.rs)
        seen = set()
        for d in deps:
            if id(d) in seen:
                continue
            seen.add(id(d))
            if d.eng == eng and d.dma is None and ins.dma is None and eng == "pe":
                continue
            ins.deps.append(d)
        for r in reads:
            r.rs.append(ins)
        for w in writes:
            w.w = ins
            w.rs = []
        self.lists[eng].append(ins)
        return ins

    def finalize(self, final_waits=()):
        nc = self.nc
        for e in ENGS:
            for ins in self.lists[e]:
                for d in ins.deps:
                    d.signal = True
        with contextlib.ExitStack() as stack:
            eng_sem = {e: stack.enter_context(nc.semaphore(f"prog_{e}")) for e in ENGS}
            dma_h = {}
            for e in ENGS:
                for ins in self.lists[e]:
                    if ins.dma is not None and ins.dma not in dma_h:
                        dma_h[ins.dma] = [stack.enter_context(nc.semaphore(f"dma_{len(dma_h)}")), 0]
            for e in ENGS:
                cnt = 0
                for ins in self.lists[e]:
                    if ins.dma is not None:
                        h = dma_h[ins.dma]
                        h[1] += 16
                        ins.sem, ins.val = h[0], h[1]
                    elif ins.signal:
                        cnt += 1
                        ins.sem, ins.val = eng_sem[e], cnt
            block = stack.enter_context(nc.Block())
            engobj = {"pe": block.tensor, "act": block.scalar, "dve": block.vector,
                      "pool": block.gpsimd, "sp": block.sync}

            def make_body(e):
                def body(engine):
                    waited = {}
                    for ins in self.lists[e]:
                        for d in ins.deps:
                            key = id(d.sem)
                            if waited.get(key, 0) >= d.val:
                                continue
                            waited[key] = d.val
                            engine.wait_ge(d.sem, d.val)
                        bi = ins.fn(engine)
                        if ins.dma is not None:
                            bi.then_inc(ins.sem, 16)
                        elif ins.signal:
                            bi.then_inc(ins.sem, 1)
                    if e == "sp":
                        for d in final_waits:
                            if waited.get(id(d.sem), 0) >= d.val:
                                continue
                            waited[id(d.sem)] = d.val
                            engine.wait_ge(d.sem, d.val)
                return body

            for e in ENGS:
                engobj[e](make_body(e))


def _b_entries(ttype):
    out = {}
    for kcr in range(-2, 6):
        for qr in range(8):
            a = 2 * kcr
            r = qr
            if ttype == "int":
                wr = r - 4
                ok = lambda row: wr <= row < wr + 8
            elif ttype == "first":
                wr = max(r - 4, 0)
                ok = lambda row: row >= 0 and wr <= row < wr + 8
            else:
                wr = min(r - 4, 0)
                ok = lambda row: row < 8 and wr <= row < wr + 8
            v0, v1 = int(ok(a)), int(ok(a + 1))
            if v0 or v1:
                out[(kcr, qr)] = (r - a, v0, v1)
    return out


def _b_slots():
    inter = sorted(set(_b_entries("int").values()))
    rest = set()
    for tt in ("first", "last"):
        rest |= set(_b_entries(tt).values())
    rest = sorted(rest - set(inter))
    return inter + rest


B_SLOTS = _b_slots()
NBS = len(B_SLOTS)


def _b_runs(mode):
    items = []
    if mode in ("int", "first", "last"):
        ent = _b_entries(mode)
        for (kcr, qr), e in ent.items():
            items.append((kcr, qr, B_SLOTS.index(e), None))
    else:
        eI = _b_entries("int")
        eX = _b_entries("first" if mode == "q0" else "last")
        gI, gX = ("L", "F") if mode == "q0" else ("R", "Z")
        for kcr in range(-2, 6):
            for qr in range(8):
                a, b = eI.get((kcr, qr)), eX.get((kcr, qr))
                if a is not None and a == b:
                    items.append((kcr, qr, B_SLOTS.index(a), None))
                else:
                    if a is not None:
                        items.append((kcr, qr, B_SLOTS.index(a), gI))
                    if b is not None:
                        items.append((kcr, qr, B_SLOTS.index(b), gX))
    items.sort(key=lambda t: (t[0], str(t[3]), t[1]))
    runs = []
    for (kcr, qr, sl, g) in items:
        if runs:
            k0, q0, n0, s0, g0 = runs[-1]
            if k0 == kcr and g0 == g and q0 + n0 == qr and s0 + n0 == sl:
                runs[-1] = (k0, q0, n0 + 1, s0, g0)
                continue
        runs.append((kcr, qr, 1, sl, g))
    return runs


GATE_COL = {None: 0, "L": 1, "F": 2, "R": 3, "Z": 4, "S64": 6, "S32": 7, "S96": 8, "L64": 9, "L32": 10, "R96": 11}
NAM = 384 + 512 + 2048
VROW = 3 * 384 + 768

C_QA, C_KA, C_VA, C_QB, C_KB, C_VB, C_GA, C_GB = 0, 768, 1536, 2304, 2816, 3328, 3840, 4864


def _fm_chunk(w, col0):
    k = w.shape[0] // 128
    return w[:, col0:col0 + 128].reshape(k, 128, 128).transpose(1, 0, 2)


def _rhs_layout(w):
    k = w.shape[0] // 128
    return w.reshape(k, 128, w.shape[1]).transpose(1, 0, 2)


def _host_weights(w_in, w_a, w_b, w_o, w1, w2):
    f = np.float32
    chunks = []
    for c0 in [C_KA + 128 * i for i in range(6)] + [C_KB + 128 * i for i in range(4)]:
        chunks.append(_fm_chunk(w_in, c0))
    for c0 in [C_QA + 128 * i for i in range(6)] + [C_QB + 128 * i for i in range(4)]:
        chunks.append(_fm_chunk(w_in, c0))
    for c0 in [C_GA + 128 * i for i in range(8)] + [C_GB + 128 * i for i in range(8)]:
        chunks.append(_fm_chunk(w_in, c0))
    wfm = np.stack(chunks, axis=1).reshape(128, -1)
    wv = np.concatenate([_rhs_layout(w_in[:, C_VA + 256 * g:C_VA + 256 * (g + 1)]).reshape(128, -1) for g in range(3)]
                        + [_rhs_layout(w_in[:, C_VB:C_VB + 512]).reshape(128, -1)], axis=1)
    wa = _rhs_layout(w_a).reshape(128, -1)
    wb = _rhs_layout(w_b)
    wb = np.concatenate([wb[:, :, 0:512].reshape(128, -1), wb[:, :, 512:1024].reshape(128, -1)], axis=1)
    wo = _rhs_layout(w_o)
    pieces = []
    for ch in range(2):
        for kh in range(2):
            pieces.append(wo[:, 4 * kh:4 * kh + 4, 512 * ch:512 * ch + 512].reshape(128, -1))
    wo = np.concatenate(pieces, axis=1)
    w1c = np.stack([_fm_chunk(w1, 128 * i) for i in range(32)], axis=1).reshape(128, -1)
    w2l = _rhs_layout(w2)
    pieces = []
    for hh in range(2):
        for ch in range(2):
            for kq in range(4):
                k0 = 16 * hh + 4 * kq
                pieces.append(w2l[:, k0:k0 + 4, 512 * ch:512 * ch + 512].reshape(128, -1))
    w2p = np.concatenate(pieces, axis=1)
    wall = np.concatenate([wfm, wv, wa, wb, wo, w1c, w2p], axis=1).astype(f)
    return np.ascontiguousarray(wall)


OFF_WFM = 0
OFF_WV = 36 * 1024
OFF_WA = OFF_WV + 3 * 2048 + 4096
OFF_WB = OFF_WA + 2048
OFF_WO = OFF_WB + 4096
OFF_W1 = OFF_WO + 8192
OFF_W2 = OFF_W1 + 32 * 1024
W_TOT = OFF_W2 + 32 * 1024


def _host_consts(b_gate, rpb, ln1_g, ln1_b, b_ff1, b_ff2, ln2_g, ln2_b):
    f = np.float32
    cols = np.zeros((128, 64), f)
    cols[:, 0:8] = b_gate[0].reshape(8, 128).T
    cols[:, 8:16] = b_gate[1].reshape(8, 128).T
    cols[:, 16:48] = b_ff1.reshape(32, 128).T
    cols[:, 48:56] = ln1_g.reshape(8, 128).T
    cols[:, 56:64] = ln1_b.reshape(8, 128).T
    rep = np.stack([np.broadcast_to(v[None, :], (128, 1024)) for v in (ln1_g, ln1_b, b_ff2, ln2_g, ln2_b)], axis=1)
    rep = np.ascontiguousarray(rep.reshape(128, 5 * 1024)).astype(f)
    i = np.arange(128)[:, None]
    j = np.arange(128)[None, :]
    ms = []
    for d in (-1, 0, 1):
        ms.append((np.abs(128 * d + i - j) <= 64).astype(f))
    mq = np.arange(32)[None, :]
    ma3 = np.tile((i >= mq).astype(f), (1, 16))
    ms.append(ma3)
    t = np.arange(512)[None, :]
    for c in range(4):
        ms.append((((i % 16) == (t % 16)) & ((8 * c + i // 16) <= (t // 16))).astype(f))
    amask = np.concatenate(ms, axis=1)
    kcol = np.arange(64)[:, None]
    qc = np.arange(64)[None, :]
    wc = np.clip(qc - 8, 0, 48)
    okc = (kcol >= wc) & (kcol < wc + 16)
    dc = np.clip(kcol - qc + 15, 0, 30)
    tabs = np.full((128, 8, NBS * 64), NEG, f)
    for s, (delta, v0, v1) in enumerate(B_SLOTS):
        for krl, v in ((0, v0), (1, v1)):
            if not v:
                continue
            dr = krl - delta + 7
            assert 0 <= dr <= 14
            for h in range(8):
                vals = rpb[h, dr][dc]
                tabs[krl * 64:(krl + 1) * 64, h, s * 64:(s + 1) * 64] = np.where(okc, vals, NEG)
    tabs = np.ascontiguousarray(tabs.reshape(128, -1))
    ident = np.eye(128, dtype=f)
    perm = np.zeros((128, 128), f)
    for m in range(128):
        d = m % 64
        if d < 8:
            perm[m + 8, m] = 1.0
        elif d < 16:
            perm[m - 8, m] = 1.0
    misc = np.concatenate([ident, perm], axis=1)
    return cols, rep, amask, tabs, misc


def _rope_tables(q):
    f = np.float32
    inv = (np.float32(500000.0) ** (-np.arange(8, dtype=f) / np.float32(8))).astype(f)
    pos = np.zeros((NKV, T), f)
    for kt in range(16):
        pos[kt] = kt * T + np.arange(T)
    for u in range(8):
        pos[16 + u] = q * 2048 - 1024 + u * T + np.arange(T)
    ang = pos[:, None, :] * inv[None, :, None]
    cs, sn = np.cos(ang).astype(f), np.sin(ang).astype(f)
    tab = np.zeros((NKV, 128, 2, T), f)
    tab[:, :, 0, :] = 1.0
    for half in (0, 64):
        tab[:, half + 0:half + 8, 0, :] = cs
        tab[:, half + 8:half + 16, 0, :] = cs
        tab[:, half + 0:half + 8, 1, :] = -sn
        tab[:, half + 8:half + 16, 1, :] = sn
    return tab


def build_program(n_q_a=QT_A, n_q_b=QT_B, n_kv=NKV, debug=False):
    nc = bass.Bass("TRN2", target_bir_lowering=False)
    xs = nc.dram_tensor("xs", [NTOK, 1024], F32, kind="ExternalInput").ap()
    wall = nc.dram_tensor("wall", [128, W_TOT], F32, kind="ExternalInput").ap()
    ccols = nc.dram_tensor("ccols", [128, 64], F32, kind="ExternalInput").ap()
    crep = nc.dram_tensor("crep", [128, 5 * 1024], F32, kind="ExternalInput").ap()
    camask = nc.dram_tensor("camask", [128, NAM], F32, kind="ExternalInput").ap()
    ctabs = nc.dram_tensor("ctabs", [128, 8 * NBS * 64], F32, kind="ExternalInput").ap()
    cmisc = nc.dram_tensor("cmisc", [128, 256], F32, kind="ExternalInput").ap()
    cgate = nc.dram_tensor("cgate", [128, 16], F32, kind="ExternalInput").ap()
    crope = nc.dram_tensor("crope", [NKV, 128, 2 * T], F32, kind="ExternalInput").ap()
    ys = nc.dram_tensor("ys", [(QT_A + QT_B) * T, 1024], F32, kind="ExternalOutput").ap()
    wbf = nc.dram_tensor("wbf", [128, W_TOT], BF16, kind="Internal").ap()
    kscr = nc.dram_tensor("kscr", [10, 128, NTOK], BF16, kind="Internal").ap()
    vscr = nc.dram_tensor("vscr", [NTOK, VROW], BF16, kind="Internal").ap()

    P = Prog(nc)
    with contextlib.ExitStack() as st:
        def sb(name, shape, dt):
            return st.enter_context(nc.sbuf_tensor(name, shape, dt))

        NS, NL = 4, 3
        wS = sb("wS", [128, NS, 1024], BF16)
        wL = sb("wL", [128, NL, 2048], BF16)
        R_wS = [Res(f"wS{i}") for i in range(NS)]
        R_wL = [Res(f"wL{i}") for i in range(NL)]
        xst = sb("xst", [128, 2, 1024], F32)
        R_xst = [Res("xst0"), Res("xst1")]
        xres = sb("xres", [128, 4, 1024], F32)
        R_xres = [Res(f"xres{i}") for i in range(4)]
        arena = sb("arena", [128, 18, T], BF16)
        R_ar = [Res(f"ar{i}") for i in range(18)]
        mrg = sb("mrg", [128, 8, T], BF16)
        R_mrg = [Res(f"mrg{i}") for i in range(8)]
        OA = sb("OA", [128, 2, T], BF16)
        OB = sb("OB", [128, 4, T], BF16)
        R_OA = [Res("OA0"), Res("OA1")]
        R_OB = [Res(f"OB{i}") for i in range(4)]
        acc = sb("acc", [128, T], F32)
        R_acc = Res("acc")
        rcp = sb("rcp", [128, T], F32)
        R_rcp = [Res("rcp0"), Res("rcp1")]
        K1w = sb("K1w", [128, 2, 768], BF16)
        K2w = sb("K2w", [128, 2, 1536], BF16)
        K3w = sb("K3w", [128, 2, 2560], BF16)
        KBw = sb("KBw", [128, 4, 1024], BF16)
        R_K = [Res("K1w"), Res("K2w"), Res("K3w"), Res("KBw")]
        V1w = sb("V1w", [128, 6 * 384], BF16)
        Vbig = sb("Vbig", [128, 28 * 384], BF16)
        V2w = Vbig[:, 0:12 * 384]
        V3w = Vbig[:, 12 * 384:28 * 384]
        V3b = sb("V3b", [128, 4 * 384], BF16)
        VBw = sb("VBw", [128, 8 * 768], BF16)
        R_V = [Res("V1w"), Res("V2w"), Res("V3w"), Res("VBw"), Res("V3b")]
        NP = 5
        NSB = 5
        OBANKS = [5, 6, 7]
        DUMMY_BANK = 4
        N_WARM = 0
        Pt = sb("Pt", [128, NP, T], BF16)
        R_Pt = [Res(f"Pt{i}") for i in range(NP)]
        ft = sb("ft", [128, 3, T], F32)
        R_ft = [Res(f"ft{i}") for i in range(3)]
        sg = sb("sg", [128, 2, T], BF16)
        R_sg = [Res("sg0"), Res("sg1")]
        amask = sb("amask", [128, NAM], BF16)
        amask4 = sb("amask4", [128, 3, T], BF16)
        amaskr = sb("amaskr", [128, 384], BF16)
        R_amask = Res("amask")
        R_amask4 = Res("amask4")
        tabB = sb("tabB", [128, 8, NBS * 64], BF16)
        R_tabB = Res("tabB")
        reps = sb("reps", [128, 4, 1024], F32)
        R_reps = Res("reps")
        rope = sb("rope", [128, 2, T], F32)
        R_rope = Res("rope")
        cols = sb("cols", [128, 64], F32)
        R_cols = Res("cols")
        gate = sb("gate", [128, 16], F32)
        R_gate = Res("gate")
        ident = sb("ident", [128, 128], F32)
        permb = sb("permb", [128, 128], BF16)
        R_misc = Res("misc")
        stats = sb("stats", [128, 4, 2, 6], F32)
        mv = sb("mv", [128, 4, 2], F32)
        rstd = sb("rstd", [128, 4], F32)
        R_stats = [Res(f"st{i}") for i in range(4)]
        R_rstd = Res("rstd")
        kst = arena[:, 8:18, :]
        R_kst = R_ar[8:18]
        vst = Vbig[:, 0:4 * VROW].rearrange("p (c n) -> p c n", n=VROW)
        R_vst = [Res(f"vst{i}") for i in range(4)]
        psum = [st.enter_context(nc.psum_tensor(f"ps{i}", [128, T], F32)) for i in range(8)]
        R_ps = [Res(f"ps{i}") for i in range(8)]
        R_wbf = Res("wbf")
        R_scrK = Res("scrK")
        R_scrV = Res("scrV")
        R_kscr = [Res(f"kscr{kt}") for kt in range(NKV)]
        R_vscr = [Res(f"vscr{kt}") for kt in range(NKV)]

        cnt = {"dma": 0, "S": 0, "L": 0, "P": 0, "ev": 0, "mul": 0, "ob": 0}
        dbg_dmas = []

        def dump(name, ap, shape, dt, reads):
            if not debug:
                return
            d = nc.dram_tensor("dbg_" + name, list(shape), dt, kind="ExternalOutput").ap()
            dbg_dmas.append(P.op("sp", lambda e: e.dma_start(out=d, in_=ap), reads=reads, dma_key="dbg_" + name))

        def dkey(prefix):
            cnt["dma"] += 1
            return f"{prefix}{cnt['dma'] % 6}"

        for a0 in range(0, W_TOT, 16384):
            a1 = min(a0 + 16384, W_TOT)
            P.op("pool", lambda e, a0=a0, a1=a1: e.dma_start(
                out=wbf[:, a0:a1].rearrange("p (n f) -> p n f", f=2048),
                in_=wall[:, a0:a1].rearrange("p (n f) -> p n f", f=2048)),
                writes=[R_wbf], dma_key="wcast")
        P.op("sp", lambda e: e.dma_start(out=cols[:], in_=ccols), writes=[R_cols], dma_key="c0")
        P.op("sp", lambda e: e.dma_start(out=gate[:], in_=cgate), writes=[R_gate], dma_key="c1")
        P.op("sp", lambda e: e.dma_start(out=ident[:], in_=cmisc[:, 0:128]), writes=[R_misc], dma_key="c2")
        P.op("pool", lambda e: e.dma_start(out=permb[:], in_=cmisc[:, 128:256]), writes=[R_misc], dma_key="c3")
        P.op("pool", lambda e: e.dma_start(out=amask[:].rearrange("p (a b) -> p a b", b=128), in_=camask.rearrange("p (a b) -> p a b", b=128)), writes=[R_amask], dma_key="c4")
        for d_ in range(3):
            for rep_ in range(4):
                P.op("pool", lambda e, d_=d_, rep_=rep_: e.tensor_copy(out=amask4[:, d_, rep_ * 128:(rep_ + 1) * 128], in_=amask[:, d_ * 128:(d_ + 1) * 128]),
                     reads=[R_amask], writes=[R_amask4])
        for j_ in range(3):
            P.op("pool", lambda e, j_=j_: e.tensor_copy(out=amaskr[:, j_ * 128:(j_ + 1) * 128], in_=amask[:, (2 - j_) * 128:(3 - j_) * 128]),
                 reads=[R_amask], writes=[R_amask4])
        P.op("sp", lambda e: e.dma_start(out=reps[:, 0, :], in_=crep[:, 0:1024]), writes=[R_reps], dma_key="c5_0")
        P.op("sp", lambda e: e.dma_start(out=reps[:, 1, :], in_=crep[:, 1024:2048]), writes=[R_reps], dma_key="c5_1")
        P.op("sp", lambda e: e.dma_start(out=xres[:, 0, :], in_=crep[:, 2048:3072]), writes=[R_xres[0]], dma_key="c5_2")
        P.op("sp", lambda e: e.dma_start(out=reps[:, 2:4, :].rearrange("p a b -> p (a b)"), in_=crep[:, 3072:5120]), writes=[R_reps], dma_key="c5_3")
        P.op("dve", lambda e: e.tensor_scalar(out=reps[:, 0, :], in0=reps[:, 0, :], scalar1=ALPHA, scalar2=None, op0=ALU.mult), reads=[R_reps], writes=[R_reps])
        P.op("dve", lambda e: e.scalar_tensor_tensor(out=reps[:, 1, :], in0=reps[:, 1, :], scalar=ALPHA, in1=xres[:, 0, :], op0=ALU.mult, op1=ALU.add),
             reads=[R_reps, R_xres[0]], writes=[R_reps])
        for h in range(8):
            w = NBS * 64
            r = R_xres[1 + (h % 2)]
            P.op("sp", lambda e, h=h, w=w: e.dma_start(out=xres[:, 1 + (h % 2), 0:w], in_=ctabs[:, h * w:(h + 1) * w]), writes=[r], dma_key=f"tb{h % 2}")
            P.op("act", lambda e, h=h, w=w: e.activation(out=tabB[:, h, :], in_=xres[:, 1 + (h % 2), 0:w], func=AF.Exp), reads=[r], writes=[R_tabB])
        for (w_, r_) in ((K1w, R_K[0]), (K2w, R_K[1]), (K3w, R_K[2]), (KBw, R_K[3])):
            P.op("pool", lambda e, w_=w_: e.memset(w_[:].rearrange("p a b -> p (a b)"), 0.0), writes=[r_])
        for (w_, r_) in ((V1w, [R_V[0]]), (Vbig, [R_V[1], R_V[2]]), (VBw, [R_V[3]]), (V3b, [R_V[4]])):
            P.op("pool", lambda e, w_=w_: e.memset(w_[:], 0.0), writes=r_)
        for sl in range(4):
            P.op("pool", lambda e, sl=sl: e.memset(vst[:, sl, :].rearrange("p (a b c) -> p a b c", b=3, c=64)[:, :, 1, :], 1.0),
                 reads=[R_V[1], R_V[2]], writes=[R_vst[sl]])

        def load_S(off):
            i = cnt["S"] % NS
            cnt["S"] += 1
            P.op("sp", lambda e, i=i, off=off: e.dma_start(out=wS[:, i, :], in_=wbf[:, off:off + 1024]),
                 reads=[R_wbf], writes=[R_wS[i]], dma_key=f"wS{i}")
            return wS[:, i, :].rearrange("p (k c) -> p k c", c=128), R_wS[i]

        def load_L(off):
            i = cnt["L"] % NL
            cnt["L"] += 1
            P.op("sp", lambda e, i=i, off=off: e.dma_start(out=wL[:, i, :], in_=wbf[:, off:off + 2048]),
                 reads=[R_wbf], writes=[R_wL[i]], dma_key=f"wL{i}")
            return wL[:, i, :], R_wL[i]

        def mm(out, lhsT, rhs, start, stop, reads, writes, skip=False):
            P.op("pe", lambda e: e.matmul(out, lhsT=lhsT, rhs=rhs, start=start, stop=stop, skip_group_check=skip),
                 reads=reads, writes=writes)

        def evac_copy(out, in_, reads, writes):
            k = cnt["ev"] % 2
            cnt["ev"] += 1
            if k == 0:
                P.op("act", lambda e: e.activation(out=out, in_=in_, func=AF.Copy), reads=reads, writes=writes)
            else:
                P.op("dve", lambda e: e.tensor_copy(out=out, in_=in_), reads=reads, writes=writes)

        qt_list = [("A", i) for i in range(n_q_a)] + [("B", u) for u in range(n_q_b)]
        xseq = list(range(n_kv)) + [(i if j_ == "A" else 18 + i) for (j_, i) in qt_list]
        xstate = {"i": 0}

        mrg32 = mrg[:].rearrange("p a b -> p (a b)").bitcast(F32).rearrange("p (s n) -> p s n", n=1024)
        ring1 = [(xres[:, i, :], [R_xres[i]]) for i in range(4)] + [(xst[:, i, :], [R_xst[i]]) for i in range(2)]
        ring2 = [(xst[:, 0, :], [R_xst[0]]), (xst[:, 1, :], [R_xst[1]]), (mrg32[:, 0, :], list(R_mrg[0:4])), (mrg32[:, 1, :], list(R_mrg[4:8]))]
        n1 = 4 * n_kv

        def xslot(g):
            if g < n1:
                return ring1[g % 6], g % 6
            k_ = (g - n1) % 4
            return ring2[k_], (4 + k_ if k_ < 2 else 4 + k_)
        xstate["req"] = 0
        xocc = {}

        def xrequest_upto(g, own=False):
            total = 4 * len(xseq)
            while xstate["req"] < total:
                r = xstate["req"]
                (ap_, res_), sid = xslot(r)
                prev = xocc.get(sid)
                if prev is not None and prev >= g:
                    break
                if sid >= 6 and not (own and (r // 4) == (g // 4)):
                    break
                xocc[sid] = r
                kt_, tc = xseq[r // 4], r % 4
                P.op("sp", lambda e, ap_=ap_, kt_=kt_, tc=tc: e.dma_start(out=ap_, in_=xs[kt_ * T + tc * 128: kt_ * T + (tc + 1) * 128, :]),
                     writes=res_, dma_key="x" + res_[0].name)
                xstate["req"] += 1

        def make_xT(kt):
            idx = xstate["i"]
            xstate["i"] += 1
            assert xseq[idx] == kt
            xrequest_upto(4 * idx, own=True)
            for rnd in range(2):
                for tc in range(4):
                    g = 4 * idx + tc
                    (ap_, res_), sid = xslot(g)
                    for f4 in range(4):
                        fc = 4 * rnd + f4
                        P.op("pe", lambda e, tc=tc, ap_=ap_, fc=fc, f4=f4: e.transpose(out=psum[f4][:, tc * 128:(tc + 1) * 128], in_=ap_[:, fc * 128:(fc + 1) * 128], identity=ident[:]),
                             reads=res_ + [R_misc], writes=[R_ps[f4]])
                for f4 in range(4):
                    fc = 4 * rnd + f4
                    if idx >= n_kv:
                        P.op("act", lambda e, fc=fc, f4=f4: e.activation(out=arena[:, fc, :], in_=psum[f4][:], func=AF.Copy), reads=[R_ps[f4]], writes=[R_ar[fc]])
                    else:
                        evac_copy(arena[:, fc, :], psum[f4][:], [R_ps[f4]], [R_ar[fc]])
            xrequest_upto(4 * (idx + 1))

        def rope_chunk(src_bank, dst_ap, dst_res, bankB):
            i = cnt["P"] % NP
            cnt["P"] += 1
            P.op("act", lambda e: e.activation(out=Pt[:, i, :], in_=psum[src_bank][:], func=AF.Copy), reads=[R_ps[src_bank]], writes=[R_Pt[i]])

            def fin():
                mm(psum[bankB][:], permb[:], Pt[:, i, :], True, True, [R_Pt[i], R_misc], [R_ps[bankB]])
                P.op("dve", lambda e: e.tensor_tensor(out=ft[:, 0, :], in0=psum[bankB][:], in1=rope[:, 1, :], op=ALU.mult), reads=[R_ps[bankB], R_rope], writes=[R_ft[0]])
                P.op("dve", lambda e: e.tensor_tensor(out=ft[:, 1, :], in0=psum[src_bank][:], in1=rope[:, 0, :], op=ALU.mult), reads=[R_ps[src_bank], R_rope], writes=[R_ft[1]])
                P.op("pool", lambda e: e.tensor_tensor(out=dst_ap, in0=ft[:, 0, :], in1=ft[:, 1, :], op=ALU.add), reads=[R_ft[0], R_ft[1]], writes=[dst_res])
            return fin

        def proj_chunks(w_base, dst_fn, resident=None):
            pend = None
            for c in range(10):
                if resident is not None:
                    wv_, rw = resident[c]
                else:
                    wv_, rw = load_S(OFF_WFM + (w_base + c) * 1024)
                b = c % 4
                for k in range(8):
                    mm(psum[b][:], wv_[:, k, :], arena[:, k, :], k == 0, k == 7, [rw, R_ar[k]], [R_ps[b]])
                if pend is not None:
                    pend()
                    pend = None
                dst_ap, dst_res = dst_fn(c)
                if c < 6:
                    pend = rope_chunk(b, dst_ap, dst_res, 4 + (c % 4))
                else:
                    evac_copy(dst_ap, psum[b][:], [R_ps[b]], [dst_res])
            if pend is not None:
                pend()

        def load_rope(kt):
            P.op("sp", lambda e: e.dma_start(out=rope[:].rearrange("p a b -> p (a b)"), in_=crope[kt]), writes=[R_rope], dma_key="rope")

        k3f = K3w[:].rearrange("p a b -> p (a b)")
        kbf = KBw[:].rearrange("p a b -> p (a b)")
        k2f = K2w[:].rearrange("p a b -> p (a b)")
        mrgf = mrg[:].rearrange("p a b -> p (a b)")
        kres_slots = [(k3f[:, i * 1024:(i + 1) * 1024], R_K[2]) for i in range(5)] + \
                     [(kbf[:, i * 1024:(i + 1) * 1024], R_K[3]) for i in range(4)] + [(k2f[:, 0:1024], R_K[1])]
        wk_res = []
        for c in range(10):
            ap_, r_ = kres_slots[c]
            P.op("sp", lambda e, ap_=ap_, c=c: e.dma_start(out=ap_, in_=wbf[:, OFF_WFM + c * 1024:OFF_WFM + (c + 1) * 1024]),
                 reads=[R_wbf], writes=[r_], dma_key=f"p1k{c}")
            wk_res.append((ap_.rearrange("p (k c) -> p k c", c=128), r_))
        vres_slots = [(VBw[:, 0:2048], [R_V[3]]), (VBw[:, 2048:4096], [R_V[3]]), (VBw[:, 4096:6144], [R_V[3]]),
                      (V1w[:, 0:2048], [R_V[0]]), (mrgf[:, 0:2048], list(R_mrg[0:4]))]
        wv_res = []
        for pc in range(5):
            ap_, r_ = vres_slots[pc]
            P.op("sp", lambda e, ap_=ap_, pc=pc: e.dma_start(out=ap_, in_=wbf[:, OFF_WV + pc * 2048:OFF_WV + (pc + 1) * 2048]),
                 reads=[R_wbf], writes=r_, dma_key=f"p1v{pc}")
            wv_res.append((ap_, r_))
        for kt in range(n_kv):
            make_xT(kt)
            load_rope(kt)
            proj_chunks(0, lambda c: (kst[:, c, :], R_kst[c]), resident=wk_res)
            P.op("pool", lambda e, kt=kt: e.dma_start(out=kscr[:, :, kt * T:(kt + 1) * T].rearrange("c p t -> p c t"), in_=kst),
                 reads=list(R_kst), writes=[R_scrK], dma_key="scrK")
            for g in range(4):
                ncol = 512 if g == 3 else 256
                woff = OFF_WV + (g * 2048 if g < 3 else 3 * 2048)
                pieces = [wv_res[g]] if g < 3 else [wv_res[3], wv_res[4]]
                for ch in range(4):
                    b = 4 + (ch % 2) + 2 * (g % 2)
                    for k in range(8):
                        lt = arena[:, k, ch * 128:(ch + 1) * 128]
                        if g < 3:
                            wl, rw = pieces[0]
                            rhs = wl.rearrange("p (k c) -> p k c", c=256)[:, k, :]
                        else:
                            wl, rw = pieces[k // 4]
                            rhs = wl.rearrange("p (k c) -> p k c", c=512)[:, k % 4, :]
                        mm(psum[b][:, 0:ncol], lt, rhs, k == 0, k == 7, list(rw) + [R_ar[k]], [R_ps[b]])
                    npair = ncol // 128
                    c0 = g * 384
                    sl = ch
                    dst = vst[:, sl, c0:c0 + npair * 192].rearrange("p (a b c) -> p a b c", b=3, c=64)[:, :, 0:3:2, :]
                    src = psum[b][:, 0:ncol].rearrange("p (a b c) -> p a b c", b=2, c=64)
                    evac_copy(dst, src, [R_ps[b]], [R_vst[sl]])
            P.op("pool", lambda e, kt=kt: e.dma_start(out=vscr[kt * T:(kt + 1) * T, :].rearrange("(c p) n -> p c n", p=128), in_=vst),
                 reads=list(R_vst), writes=[R_scrV], dma_key="scrV")

        out_dmas = []
        qtiles = [("A", i) for i in range(n_q_a)] + [("B", u) for u in range(n_q_b)]

        def tile_params(job, ti):
            if job == "A":
                return dict(kt=ti, kt_lo=0, kt_hi=16, bmode="first" if ti == 0 else ("last" if ti == 15 else "int"), orow=ti * T,
                            kgate=lambda ktile: None)
            return dict(kt=18 + ti, kt_lo=16, kt_hi=24, bmode="q0" if ti == 0 else ("q3" if ti == 3 else "int"), orow=(QT_A + ti) * T,
                        kgate=lambda ktile: "L" if ktile < 18 else ("R" if ktile > 21 else None))

        def window_loads(job, ti):
            tp_ = tile_params(job, ti)
            kt, kt_lo, kt_hi = tp_["kt"], tp_["kt_lo"], tp_["kt_hi"]
            tok0 = kt * T
            lo_tok, hi_tok = kt_lo * T, kt_hi * T
            thunks = []

            def kload(win, rw, chs, t_lo, t_hi):
                a, b_ = max(t_lo, lo_tok), min(t_hi, hi_tok)
                src = kscr[chs[0]:chs[-1] + 1, :, a:b_].rearrange("c p t -> p c t")
                thunks.append(lambda WQ: P.op(WQ, lambda e: e.dma_start(out=win[:, :, a - t_lo:b_ - t_lo], in_=src),
                                              reads=[R_scrK, R_scrV], writes=[rw], dma_key="w" + rw.name))
            kload(K1w, R_K[0], [0, 1], tok0 - 128, tok0 + 640)
            kload(K2w, R_K[1], [2, 3], tok0 - 512, tok0 + 1024)
            kload(K3w, R_K[2], [4, 5], tok0 - 1024, tok0 + 1536)
            kload(KBw, R_K[3], [6, 7, 8, 9], tok0 - 256, tok0 + 768)

            def vgather(win, rv, wcol0, col0, ncol, row0, dims, p_lo=0, p_hi=128, pstride=1):
                if p_hi <= p_lo:
                    return
                nch = 1
                for (_, c_) in dims:
                    nch *= c_
                src = bass.AP(tensor=vscr.tensor, offset=(row0 + pstride * p_lo) * VROW + col0,
                              ap=[[pstride * VROW, p_hi - p_lo]] + [[rs * VROW, c_] for (rs, c_) in dims] + [[1, ncol]])
                dst = win[p_lo:p_hi, wcol0:wcol0 + nch * ncol]
                if len(dims) == 1:
                    dst = dst.rearrange("p (a n) -> p a n", n=ncol)
                elif len(dims) == 2:
                    dst = dst.rearrange("p (a b n) -> p a b n", b=dims[1][1], n=ncol)
                thunks.append(lambda WQ: P.op(WQ, lambda e: e.dma_start(out=dst, in_=src), reads=[R_scrK, R_scrV], writes=[rv], dma_key="w" + rv.name))

            def tile_ok(ktl):
                return kt_lo <= ktl < kt_hi
            ccs = [4 * kt - 1 + s_ for s_ in range(6) if tile_ok((4 * kt - 1 + s_) // 4)]
            vgather(V1w, R_V[0], (ccs[0] - (4 * kt - 1)) * 384, 0, 384, ccs[0] * 128, [(128, len(ccs))])
            for d in range(3):
                if tile_ok(kt - 1 + d):
                    vgather(V2w, R_V[1], 4 * d * 384, 384, 384, (kt - 1 + d) * T, [(1, 4)], pstride=4)
            base3 = tok0 - 1024
            i_lo = max(0, (lo_tok - base3) // 16)
            i_hi = min(128, (hi_tok - base3) // 16)
            for r0 in range(0, 16, 4):
                vgather(V3w, R_V[2], r0 * 384, 768, 384, base3 + r0, [(1, 4)], i_lo, i_hi, pstride=16)
            if tile_ok(kt + 2):
                vgather(V3b, R_V[4], 0, 768, 384, (kt + 2) * T, [(128, 4)])
            ccs = [4 * kt - 2 + s_ for s_ in range(8) if tile_ok((4 * kt - 2 + s_) // 4)]
            half_n = (len(ccs) + 1) // 2
            for part in (ccs[:half_n], ccs[half_n:]):
                if part:
                    vgather(VBw, R_V[3], (part[0] - (4 * kt - 2)) * 768, 1152, 768, part[0] * 128, [(128, len(part))])
            return thunks

        if qtiles:
            for th in window_loads(*qtiles[0]):
                th("sp")
        for qi, (job, ti) in enumerate(qtiles):
            tp_ = tile_params(job, ti)
            kt, kt_lo, kt_hi, bmode, orow, kgate = tp_["kt"], tp_["kt_lo"], tp_["kt_hi"], tp_["bmode"], tp_["orow"], tp_["kgate"]
            tok0 = kt * T

            def tile_ok(ktl, kt_lo=kt_lo, kt_hi=kt_hi):
                return kt_lo <= ktl < kt_hi

            make_xT(kt)
            load_rope(kt)
            proj_chunks(10, lambda c: (arena[:, 8 + c, :], R_ar[8 + c]))

            if qi == 0:
                dump("xT", arena[:, 0:8, :], [128, 8, T], BF16, R_ar[0:8])
                dump("Q", arena[:, 8:18, :], [128, 10, T], BF16, R_ar[8:18])
                dump("K3w", K3w[:], [128, 2, 2560], BF16, [R_K[2]])
                dump("KBw", KBw[:], [128, 4, 1024], BF16, [R_K[3]])
                dump("V1w", V1w[:], [128, 6 * 384], BF16, [R_V[0]])
                dump("V3w", V3w, [128, 16 * 384], BF16, [R_V[2]])
                dump("VBw", VBw[:], [128, 8 * 768], BF16, [R_V[3]])
                dump("tabB", tabB[:], [128, 8, NBS * 64], BF16, [R_tabB])
            DEPTH = 4
            pipe = []

            def vaug(win, chunk_off, pair, half):
                o = chunk_off + pair * 192 + 64 * half
                return win[:, o:o + 128]

            post_sched = []
            POST_LAG = 2

            def run_due(force=False):
                keep = []
                for item in post_sched:
                    if force or item[0] <= 0:
                        item[1]()
                    else:
                        keep.append(item)
                post_sched[:] = keep

            def pop_one():
                fn = pipe.pop(0)
                fn()
                for item in post_sched:
                    item[0] -= 1
                run_due()

            def submit(subs, mask_ops, obank, first_flag, post=None, sres=(), vres=()):
                i = cnt["P"] % NP
                cnt["P"] += 1
                sbk = i % NSB
                for j, (kap, qap, c0, n, g, vap, oc0) in enumerate(subs):
                    mm(psum[sbk][:, c0:c0 + n], kap, qap, j == 0, False, list(sres), [R_ps[sbk]], skip=True)
                j = 0
                while j < len(subs):
                    j2 = j
                    while j2 + 1 < len(subs) and subs[j2 + 1][4] == subs[j][4] and subs[j2 + 1][2] == subs[j2][2] + subs[j2][3]:
                        j2 += 1
                    c0 = subs[j][2]
                    c1 = subs[j2][2] + subs[j2][3]
                    gc = GATE_COL[subs[j][4]]
                    P.op("act", lambda e, i=i, sbk=sbk, c0=c0, c1=c1, gc=gc: e.activation(out=Pt[:, i, c0:c1], in_=psum[sbk][:, c0:c1], func=AF.Exp, scale=0.125, bias=gate[:, gc:gc + 1]),
                         reads=[R_ps[sbk], R_gate], writes=[R_Pt[i]])
                    j = j2 + 1
                for (c0, n, map_, mres) in mask_ops:
                    eng = "pool" if cnt["mul"] % 4 == 3 else "dve"
                    cnt["mul"] += 1
                    P.op(eng, lambda e, i=i, c0=c0, n=n, map_=map_: e.tensor_tensor(out=Pt[:, i, c0:c0 + n], in0=Pt[:, i, c0:c0 + n], in1=map_, op=ALU.mult),
                         reads=[R_Pt[i], mres], writes=[R_Pt[i]])

                for _ in range(N_WARM):
                    P.op("pe", lambda e: e.matmul(psum[DUMMY_BANK][:], lhsT=permb[:], rhs=amask4[:, 0, :], start=True, stop=True), reads=[], writes=[R_ps[DUMMY_BANK]])

                def pv(i=i, subs=subs, first_flag=first_flag, post=post):
                    ff = first_flag
                    for (kap, qap, c0, n, g, vap, oc0) in subs:
                        mm(psum[obank][:, oc0:oc0 + n], vap, Pt[:, i, c0:c0 + n], ff, False, list(vres) + [R_Pt[i]], [R_ps[obank]], skip=True)
                        ff = False
                    if post is not None:
                        for k_, st_ in enumerate(post()):
                            post_sched.append([POST_LAG + k_, st_])
                pipe.append(pv)
                while len(pipe) > DEPTH:
                    pop_one()

            def next_obank():
                b_ = OBANKS[cnt["ob"] % len(OBANKS)]
                cnt["ob"] += 1
                return b_

            if job == "A":
                g3gate = {0: "S64", 1: "S32", 15: "S96"}.get(ti)
            else:
                g3gate = {0: "L64", 1: "L32", 3: "R96"}.get(ti)

            for hs in range(4):
                half = hs % 2
                pr = slice(64 * half, 64 * half + 64)
                po = slice(64 * (1 - half), 64 * (1 - half) + 64)
                qch = hs // 2
                acc_ap, acc_res = (acc[:], R_acc) if hs % 2 == 0 else (ft[:, 2, :], R_ft[2])
                ob1 = next_obank()
                blocks = []
                for s_ in range(6):
                    cc = 4 * kt - 1 + s_
                    ktl = cc // 4
                    if not tile_ok(ktl):
                        continue
                    q_lo, q_hi = max(0, s_ - 2), min(3, s_)
                    n = 128 * (q_hi - q_lo + 1)
                    j_lo = q_lo - s_ + 2
                    koff = s_ * 128
                    subs = [(K1w[pr, qch, koff:koff + 128], arena[pr, 8 + qch, q_lo * 128:q_lo * 128 + n], 0, n, kgate(ktl),
                             vaug(V1w, s_ * 384, qch, half), q_lo * 128)]
                    blocks.append((subs, [(0, n, amaskr[:, j_lo * 128:j_lo * 128 + n], R_amask4)], [R_K[0], R_ar[8 + qch]], [R_V[0]]))
                if tile_ok(kt + 2):
                    for c_ in range(4):
                        koff = 2048 + c_ * 128
                        q0_ = 128 * c_
                        subs = [(K3w[pr, qch, koff:koff + 128], arena[pr, 12 + qch, q0_:T], 0, T - q0_, kgate(kt + 2), vaug(V3b, c_ * 384, qch, half), q0_)]
                        blocks.append((subs, [(0, T - q0_, amask[:, 896 + 512 * c_ + q0_:896 + 512 * (c_ + 1)], R_amask)], [R_K[2], R_ar[12 + qch]], [R_V[4]]))

                def post1(ob1=ob1, acc_ap=acc_ap, acc_res=acc_res):
                    return [lambda: P.op("act", lambda e: e.activation(out=acc_ap, in_=psum[ob1][:], func=AF.Copy), reads=[R_ps[ob1]], writes=[acc_res])]
                for bi, (subs, mops, sres, vres) in enumerate(blocks):
                    submit(subs, mops, ob1, bi == 0, post1 if bi == len(blocks) - 1 else None, sres, vres)
                ob2 = next_obank()
                blocks = []
                for d in (-1, 0, 1):
                    ktl = kt + d
                    if not tile_ok(ktl):
                        continue
                    kb0 = (ktl * T) - (tok0 - 512)
                    subs = []
                    for r4 in range(4):
                        sl = 4 * (d + 1) + r4
                        subs.append((K2w[pr, qch, kb0 + r4:kb0 + T:4], arena[pr, 10 + qch, r4:T:4], r4 * 128, 128, kgate(ktl),
                                     vaug(V2w, sl * 384, qch, half), r4 * 128))
                    blocks.append((subs, [(0, T, amask4[:, d + 1, :], R_amask4)], [R_K[1], R_ar[10 + qch]], [R_V[1]]))

                def post2(ob2=ob2, acc_ap=acc_ap, acc_res=acc_res):
                    return [lambda: P.op("dve", lambda e: e.tensor_tensor(out=acc_ap.rearrange("p (m r) -> p m r", r=4), in0=acc_ap.rearrange("p (m r) -> p m r", r=4),
                                                                          in1=psum[ob2][:].rearrange("p (r m) -> p m r", r=4), op=ALU.add),
                                         reads=[R_ps[ob2], acc_res], writes=[acc_res])]
                for bi, (subs, mops, sres, vres) in enumerate(blocks):
                    submit(subs, mops, ob2, bi == 0, post2 if bi == len(blocks) - 1 else None, sres, vres)
                ob3 = next_obank()
                subs = []
                for r in range(16):
                    subs.append((K3w[pr, qch, r:2048:16], arena[pr, 12 + qch, r:T:16], r * 32, 32, g3gate, vaug(V3w, r * 384, qch, half), r * 32))

                def post3(ob3=ob3, acc_ap=acc_ap, acc_res=acc_res, pr=pr, po=po, qch=qch, half=half):
                    def st0():
                        P.op("dve", lambda e: e.tensor_tensor(out=acc_ap.rearrange("p (m r) -> p m r", r=16), in0=acc_ap.rearrange("p (m r) -> p m r", r=16),
                                                              in1=psum[ob3][:].rearrange("p (r m) -> p m r", r=16), op=ALU.add),
                             reads=[R_ps[ob3], acc_res], writes=[acc_res])

                    def st1():
                        P.op("act", lambda e: e.activation(out=rcp[pr, :], in_=acc_ap[po, :], func=AF.Ln), reads=[acc_res], writes=[R_rcp[half]])
                        P.op("act", lambda e: e.activation(out=rcp[pr, :], in_=rcp[pr, :], func=AF.Exp, scale=-1.0), reads=[R_rcp[half]], writes=[R_rcp[half]])

                    def st2():
                        P.op("dve", lambda e: e.tensor_tensor(out=OA[pr, qch, :], in0=acc_ap[pr, :], in1=rcp[pr, :], op=ALU.mult),
                             reads=[acc_res, R_rcp[half]], writes=[R_OA[qch]])
                    return [st0, st1, st2]
                submit(subs, [(0, T, amask[:, 384:896], R_amask)], ob3, True, post3, [R_K[2], R_ar[12 + qch]], [R_V[2]])

            runs = _b_runs(bmode)
            for h in range(8):
                half = h % 2
                pr = slice(64 * half, 64 * half + 64)
                po = slice(64 * (1 - half), 64 * (1 - half) + 64)
                kch = h // 2
                ob = next_obank()
                blocks = []
                for (kcr, qr0, nr, sl0, g) in runs:
                    cc = 4 * kt + kcr
                    ktl = cc // 4
                    if not tile_ok(ktl):
                        continue
                    gname = g if g is not None else kgate(ktl)
                    koff = (kcr + 2) * 128
                    n = nr * 64
                    subs = [(KBw[pr, kch, koff:koff + 128], arena[pr, 14 + kch, qr0 * 64:qr0 * 64 + n], 0, n, gname,
                             vaug(VBw, (kcr + 2) * 768, kch, half), qr0 * 64)]
                    blocks.append((subs, [(0, n, tabB[:, h, sl0 * 64:sl0 * 64 + n], R_tabB)]))

                def postB(ob=ob, pr=pr, po=po, kch=kch, half=half):
                    def st0():
                        P.op("act", lambda e: e.activation(out=rcp[pr, :], in_=psum[ob][po, :], func=AF.Ln), reads=[R_ps[ob]], writes=[R_rcp[half]])
                        P.op("act", lambda e: e.activation(out=rcp[pr, :], in_=rcp[pr, :], func=AF.Exp, scale=-1.0), reads=[R_rcp[half]], writes=[R_rcp[half]])

                    def st1():
                        P.op("dve", lambda e: e.tensor_tensor(out=OB[pr, kch, :], in0=psum[ob][pr, :], in1=rcp[pr, :], op=ALU.mult),
                             reads=[R_ps[ob], R_rcp[half]], writes=[R_OB[kch]])
                    return [st0, st1]
                for bi, (subs, mops) in enumerate(blocks):
                    submit(subs, mops, ob, bi == 0, postB if bi == len(blocks) - 1 else None, [R_K[3], R_ar[14 + kch]], [R_V[3]])
            while pipe:
                pop_one()
            run_due(force=True)

            if qi == 0:
                dump("OA", OA[:], [128, 2, T], BF16, R_OA)
                dump("OB", OB[:], [128, 4, T], BF16, R_OB)
            wa_ap, rwa = load_L(OFF_WA)
            wb0, rwb0 = load_L(OFF_WB)
            wb1, rwb1 = load_L(OFF_WB + 2048)
            wa3 = wa_ap.rearrange("p (k c) -> p k c", c=1024)
            for jc in range(8):
                bs = 0 if jc % 2 == 0 else 4
                if jc % 2 == 0:
                    t0_ap, t0_res, t1_ap, t1_res = ft[:, 0, :], [R_ft[0]], ft[:, 1, :], [R_ft[1]]
                else:
                    t0_ap, t0_res, t1_ap, t1_res = ft[:, 2, :], [R_ft[2]], rcp[:], list(R_rcp)
                for k in range(2):
                    mm(psum[bs][:], wa3[:, k, jc * 128:(jc + 1) * 128], OA[:, k, :], k == 0, k == 1, [rwa, R_OA[k]], [R_ps[bs]])
                wbp, rwb = (wb0, rwb0) if jc < 4 else (wb1, rwb1)
                wb3 = wbp.rearrange("p (k c) -> p k c", c=512)
                for k in range(4):
                    mm(psum[bs + 1][:], wb3[:, k, (jc % 4) * 128:(jc % 4 + 1) * 128], OB[:, k, :], k == 0, k == 3, [rwb, R_OB[k]], [R_ps[bs + 1]])
                for gi in range(2):
                    wg, rwg = load_S(OFF_WFM + (20 + 8 * gi + jc) * 1024)
                    for k in range(8):
                        mm(psum[bs + 2 + gi][:], wg[:, k, :], arena[:, k, :], k == 0, k == 7, [rwg, R_ar[k]], [R_ps[bs + 2 + gi]])
                    P.op("act", lambda e, gi=gi, jc=jc, bs=bs: e.activation(out=sg[:, gi, :], in_=psum[bs + 2 + gi][:], func=AF.Sigmoid, bias=cols[:, 8 * gi + jc:8 * gi + jc + 1]),
                         reads=[R_ps[bs + 2 + gi], R_cols], writes=[R_sg[gi]])
                P.op("dve", lambda e, bs=bs, t0_ap=t0_ap: e.tensor_tensor(out=t0_ap, in0=psum[bs][:], in1=sg[:, 0, :], op=ALU.mult), reads=[R_ps[bs], R_sg[0]], writes=t0_res)
                P.op("dve", lambda e, bs=bs, t1_ap=t1_ap: e.tensor_tensor(out=t1_ap, in0=psum[bs + 1][:], in1=sg[:, 1, :], op=ALU.mult), reads=[R_ps[bs + 1], R_sg[1]], writes=t1_res)
                P.op("pool", lambda e, jc=jc, t0_ap=t0_ap, t1_ap=t1_ap: e.tensor_tensor(out=mrg[:, jc, :], in0=t0_ap, in1=t1_ap, op=ALU.add), reads=t0_res + t1_res, writes=[R_mrg[jc]])
                if jc == 0:
                    wthunks = window_loads(*qtiles[qi + 1]) if qi + 1 < len(qtiles) else []
                nth = (len(wthunks) + 6) // 7
                for th in wthunks[jc * nth:(jc + 1) * nth] if jc < 7 else wthunks[7 * nth:]:
                    th("sp")

            if qi == 0:
                dump("mrg", mrg[:], [128, 8, T], BF16, R_mrg)
            for tc in range(4):
                P.op("pool", lambda e, tc=tc, tok0=tok0: e.dma_start(out=xres[:, tc, :], in_=xs[tok0 + tc * 128:tok0 + (tc + 1) * 128, :]),
                     writes=[R_xres[tc]], dma_key=f"xres{tc}")

            def layer_norm_all(stats_done=False):
                for tc in range(4):
                    for hf in range(2):
                        if stats_done:
                            continue
                        P.op("dve", lambda e, hf=hf, tc=tc: e.bn_stats(out=stats[:, tc, hf, :], in_=xres[:, tc, hf * 512:(hf + 1) * 512]), reads=[R_xres[tc]], writes=[R_stats[tc]])
                    P.op("dve", lambda e, tc=tc: e.bn_aggr(out=mv[:, tc, :], in_=stats[:, tc].rearrange("p a b -> p (a b)")), reads=[R_stats[tc]], writes=[R_stats[tc]])
                P.op("act", lambda e: e.activation(out=rstd[:, 0:4], in_=mv[:, :, 1], func=AF.Sqrt, bias=gate[:, 5:6]), reads=list(R_stats) + [R_gate], writes=[R_rstd])
                P.op("dve", lambda e: e.reciprocal(out=rstd[:, 0:4], in_=rstd[:, 0:4]), reads=[R_rstd], writes=[R_rstd])
                for tc in range(4):
                    P.op("dve", lambda e, tc=tc: e.tensor_scalar(out=xres[:, tc, :], in0=xres[:, tc, :], scalar1=mv[:, tc, 0:1], scalar2=rstd[:, tc:tc + 1], op0=ALU.subtract, op1=ALU.mult),
                         reads=[R_xres[tc], R_stats[tc], R_rstd], writes=[R_xres[tc]])

            for chh in range(2):
                for kh in range(2):
                    wl, rw = load_L(OFF_WO + (2 * chh + kh) * 2048)
                    for tc in range(4):
                        b = 4 + tc
                        for k4 in range(4):
                            k = 4 * kh + k4
                            rhs = wl.rearrange("p (k c) -> p k c", c=512)[:, k4, :]
                            mm(psum[b][:], mrg[:, k, tc * 128:(tc + 1) * 128], rhs, k == 0, k == 7, [rw, R_mrg[k]], [R_ps[b]])
                for tc in range(4):
                    b = 4 + tc
                    P.op("dve", lambda e, tc=tc, b=b, chh=chh: e.scalar_tensor_tensor(out=xres[:, tc, chh * 512:(chh + 1) * 512], in0=xres[:, tc, chh * 512:(chh + 1) * 512],
                                                                                    scalar=ALPHA, in1=psum[b][:], op0=ALU.mult, op1=ALU.add),
                         reads=[R_xres[tc], R_ps[b]], writes=[R_xres[tc]])
                    P.op("dve", lambda e, tc=tc, chh=chh: e.bn_stats(out=stats[:, tc, chh, :], in_=xres[:, tc, chh * 512:(chh + 1) * 512]), reads=[R_xres[tc]], writes=[R_stats[tc]])
            layer_norm_all(stats_done=True)
            if qi == 0:
                dump("z1", xres[:], [128, 4, 1024], F32, R_xres)
            for tc in range(4):
                for fc in range(8):
                    b = fc
                    P.op("pe", lambda e, tc=tc, fc=fc, b=b: e.transpose(out=psum[b][:, tc * 128:(tc + 1) * 128], in_=xres[:, tc, fc * 128:(fc + 1) * 128], identity=ident[:]),
                         reads=[R_xres[tc], R_misc], writes=[R_ps[b]])
            for fc in range(8):
                if fc % 2 == 0:
                    P.op("act", lambda e, fc=fc: e.activation(out=mrg[:, fc, :], in_=psum[fc][:], func=AF.Identity, scale=cols[:, 48 + fc:49 + fc], bias=cols[:, 56 + fc:57 + fc]),
                         reads=[R_ps[fc], R_cols], writes=[R_mrg[fc]])
                else:
                    P.op("dve", lambda e, fc=fc: e.tensor_scalar(out=mrg[:, fc, :], in0=psum[fc][:], scalar1=cols[:, 48 + fc:49 + fc], scalar2=cols[:, 56 + fc:57 + fc], op0=ALU.mult, op1=ALU.add),
                         reads=[R_ps[fc], R_cols], writes=[R_mrg[fc]])
            for tc in range(4):
                P.op("pool", lambda e, tc=tc: e.tensor_tensor(out=xres[:, tc, :], in0=xres[:, tc, :], in1=reps[:, 0, :], op=ALU.mult), reads=[R_xres[tc], R_reps], writes=[R_xres[tc]])
                P.op("pool", lambda e, tc=tc: e.tensor_tensor(out=xres[:, tc, :], in0=xres[:, tc, :], in1=reps[:, 1, :], op=ALU.add), reads=[R_xres[tc], R_reps], writes=[R_xres[tc]])

            if qi == 0:
                dump("x1T", mrg[:], [128, 8, T], BF16, R_mrg)
                dump("x1res", xres[:], [128, 4, 1024], F32, R_xres)
            for hh in range(2):
                for kk in range(16):
                    hk = 16 * hh + kk
                    w1_, rw = load_S(OFF_W1 + hk * 1024)
                    b = kk % 2
                    for k in range(8):
                        mm(psum[b][:], w1_[:, k, :], mrg[:, k, :], k == 0, k == 7, [rw, R_mrg[k]], [R_ps[b]])
                    fi = 2 if kk % 2 == 0 else 0
                    P.op("dve", lambda e, b=b, hk=hk, fi=fi: e.tensor_scalar(out=ft[:, fi, :], in0=psum[b][:], scalar1=cols[:, 16 + hk:17 + hk], scalar2=0.0, op0=ALU.add, op1=ALU.max),
                         reads=[R_ps[b], R_cols], writes=[R_ft[fi]])
                    P.op("pool", lambda e, kk=kk, fi=fi: e.tensor_tensor(out=arena[:, kk, :], in0=ft[:, fi, :], in1=ft[:, fi, :], op=ALU.mult), reads=[R_ft[fi]], writes=[R_ar[kk]])
                for chh in range(2):
                    for kq in range(4):
                        wl, rw = load_L(OFF_W2 + ((hh * 2 + chh) * 4 + kq) * 2048)
                        for tc in range(4):
                            b = 4 + tc
                            for k4 in range(4):
                                k = 4 * kq + k4
                                rhs = wl.rearrange("p (k c) -> p k c", c=512)[:, k4, :]
                                mm(psum[b][:], arena[:, k, tc * 128:(tc + 1) * 128], rhs, k == 0, k == 15, [rw, R_ar[k]], [R_ps[b]])
                    for tc in range(4):
                        b = 4 + tc
                        P.op("dve", lambda e, tc=tc, b=b, chh=chh: e.tensor_tensor(out=xres[:, tc, chh * 512:(chh + 1) * 512], in0=psum[b][:], in1=xres[:, tc, chh * 512:(chh + 1) * 512], op=ALU.add),
                             reads=[R_xres[tc], R_ps[b]], writes=[R_xres[tc]])
                        if hh == 1:
                            P.op("dve", lambda e, tc=tc, chh=chh: e.bn_stats(out=stats[:, tc, chh, :], in_=xres[:, tc, chh * 512:(chh + 1) * 512]), reads=[R_xres[tc]], writes=[R_stats[tc]])
            if qi == 0:
                dump("h2", xres[:], [128, 4, 1024], F32, R_xres)
            layer_norm_all(stats_done=True)
            for tc in range(4):
                P.op("pool", lambda e, tc=tc: e.tensor_tensor(out=xres[:, tc, :], in0=xres[:, tc, :], in1=reps[:, 2, :], op=ALU.mult), reads=[R_xres[tc], R_reps], writes=[R_xres[tc]])
                P.op("pool", lambda e, tc=tc: e.tensor_tensor(out=xres[:, tc, :], in0=xres[:, tc, :], in1=reps[:, 3, :], op=ALU.add), reads=[R_xres[tc], R_reps], writes=[R_xres[tc]])
                od = P.op("pool", lambda e, tc=tc, orow=orow: e.dma_start(out=ys[orow + tc * 128:orow + (tc + 1) * 128, :], in_=xres[:, tc, :]),
                          reads=[R_xres[tc]], dma_key=f"out{tc}")
                out_dmas.append(od)

        if debug:
            nk = n_kv * T
            dump("kscr", kscr[:, :, 0:nk], [10, 128, nk], BF16, [R_scrK])
            dump("vscr", vscr[0:nk, :], [nk, VROW], BF16, [R_scrV])
        P.finalize(final_waits=out_dmas + dbg_dmas)
    return nc


_CACHE = {}


def kernel(x_prompt, x_sample, w_in, b_gate, w_branch_a, w_branch_b, w_out, rel_pos_bias,
           ln1_g, ln1_b, w_ff1, b_ff1, w_ff2, b_ff2, ln2_g, ln2_b):
    f = np.float32
    x_prompt = np.asarray(x_prompt, f)
    x_sample = np.asarray(x_sample, f)
    wall = _host_weights(np.asarray(w_in[0], f), np.asarray(w_branch_a[0], f), np.asarray(w_branch_b[0], f),
                         np.asarray(w_out[0], f), np.asarray(w_ff1[0], f), np.asarray(w_ff2[0], f))
    cols, rep, amask, tabs, misc = _host_consts(np.asarray(b_gate[0], f), np.asarray(rel_pos_bias[0], f),
                                                np.asarray(ln1_g[0], f), np.asarray(ln1_b[0], f), np.asarray(b_ff1[0], f),
                                                np.asarray(b_ff2[0], f), np.asarray(ln2_g[0], f), np.asarray(ln2_b[0], f))
    in_maps = []
    for c in range(8):
        b, q = c // 4, c % 4
        xs = np.zeros((NTOK, 1024), f)
        xs[:8192] = x_prompt[c]
        lo, hi = q * 2048 - 1024, q * 2048 + 3072
        s_lo, s_hi = max(lo, 0), min(hi, 8192)
        xs[8192 + (s_lo - lo):8192 + (s_hi - lo)] = x_sample[b, s_lo:s_hi]
        g = np.zeros((128, 16), f)
        g[:, 1] = 0.0 if q > 0 else NEG
        g[:, 2] = 0.0 if q == 0 else NEG
        g[:, 3] = 0.0 if q < 3 else NEG
        g[:, 4] = 0.0 if q == 3 else NEG
        g[:, 5] = LN_EPS
        g[0:64, 6] = NEG
        g[0:32, 7] = NEG
        g[96:128, 8] = NEG
        if q == 0:
            g[0:64, 9] = NEG
            g[0:32, 10] = NEG
        if q == 3:
            g[96:128, 11] = NEG
        rope = _rope_tables(q).reshape(NKV, 128, 2 * T)
        in_maps.append({"xs": xs, "wall": wall, "ccols": cols, "crep": rep, "camask": amask, "ctabs": tabs,
                        "cmisc": misc, "cgate": g, "crope": np.ascontiguousarray(rope)})
    if "nc" not in _CACHE:
        _CACHE["nc"] = build_program()
    res = run_bass_kernel_spmd(_CACHE["nc"], in_maps, core_ids=list(range(8)))
    y_prompt = np.zeros((8, 8192, 1024), f)
    y_sample = np.zeros((2, 8192, 1024), f)
    for c in range(8):
        ysc = res.results[c]["ys"]
        y_prompt[c] = ysc[:8192]
        b, q = c // 4, c % 4
        y_sample[b, q * 2048:(q + 1) * 2048] = ysc[8192:]
    return (y_prompt, y_sample)
```

```python
import contextlib
import numpy as np
import concourse.bass as bass
import concourse.mybir as mybir
from concourse.bass_utils import run_bass_kernel_spmd

F32 = mybir.dt.float32
BF16 = mybir.dt.bfloat16
AF = mybir.ActivationFunctionType
ALU = mybir.AluOpType

ENGS = ("pe", "act", "dve", "pool", "sp")
NEG = -30000.0
ALPHA = 2.0 ** 0.25
LN_EPS = 1e-5
T = 512
NKV = 24
NTOK = NKV * T
QT_A = 16
QT_B = 4


class Res:
    __slots__ = ("name", "w", "rs")

    def __init__(self, name):
        self.name = name
        self.w = None
        self.rs = []


class Instr:
    __slots__ = ("eng", "fn", "deps", "signal", "dma", "sem", "val")

    def __init__(self, eng, fn, dma):
        self.eng = eng
        self.fn = fn
        self.deps = []
        self.signal = False
        self.dma = dma
        self.sem = None
        self.val = None


class Prog:
    def __init__(self, nc):
        self.nc = nc
        self.lists = {e: [] for e in ENGS}

    def op(self, eng, fn, reads=(), writes=(), dma_key=None):
        ins = Instr(eng, fn, dma_key)
        deps = []
        for r in reads:
            if r.w is not None:
                deps.append(r.w)
        for w in writes:
            if w.w is not None:
                deps.append(w.w)
            deps.extend(w.rs)
        seen = set()
        for d in deps:
            if id(d) in seen:
                continue
            seen.add(id(d))
            if d.eng == eng and d.dma is None and ins.dma is None and eng == "pe":
                continue
            ins.deps.append(d)
        for r in reads:
            r.rs.append(ins)
        for w in writes:
            w.w = ins
            w.rs = []
        self.lists[eng].append(ins)
        return ins

    def finalize(self, final_waits=()):
        nc = self.nc
        for e in ENGS:
            for ins in self.lists[e]:
                for d in ins.deps:
                    d.signal = True
        with contextlib.ExitStack() as stack:
            eng_sem = {e: stack.enter_context(nc.semaphore(f"prog_{e}")) for e in ENGS}
            dma_h = {}
            for e in ENGS:
                for ins in self.lists[e]:
                    if ins.dma is not None and ins.dma not in dma_h:
                        dma_h[ins.dma] = [stack.enter_context(nc.semaphore(f"dma_{len(dma_h)}")), 0]
            for e in ENGS:
                cnt = 0
                for ins in self.lists[e]:
                    if ins.dma is not None:
                        h = dma_h[ins.dma]
                        h[1] += 16
                        ins.sem, ins.val = h[0], h[1]
                    elif ins.signal:
                        cnt += 1
                        ins.sem, ins.val = eng_sem[e], cnt
            block = stack.enter_context(nc.Block())
            engobj = {"pe": block.tensor, "act": block.scalar, "dve": block.vector,
                      "pool": block.gpsimd, "sp": block.sync}

            def make_body(e):
                def body(engine):
                    waited = {}
                    for ins in self.lists[e]:
                        for d in ins.deps:
                            key = id(d.sem)
                            if waited.get(key, 0) >= d.val:
                                continue
                            waited[key] = d.val
                            engine.wait_ge(d.sem, d.val)
                        bi = ins.fn(engine)
                        if ins.dma is not None:
                            bi.then_inc(ins.sem, 16)
                        elif ins.signal:
                            bi.then_inc(ins.sem, 1)
                    if e == "sp":
                        for d in final_waits:
                            if waited.get(id(d.sem), 0) >= d.val:
                                continue
                            waited[id(d.sem)] = d.val
                            engine.wait_ge(d.sem, d.val)
                return body

            for e in ENGS:
                engobj[e](make_body(e))


def _b_entries(ttype):
    out = {}
    for kcr in range(-2, 6):
        for qr in range(8):
            a = 2 * kcr
            r = qr
            if ttype == "int":
                wr = r - 4
                ok = lambda row: wr <= row < wr + 8
            elif ttype == "first":
                wr = max(r - 4, 0)
                ok = lambda row: row >= 0 and wr <= row < wr + 8
            else:
                wr = min(r - 4, 0)
                ok = lambda row: row < 8 and wr <= row < wr + 8
            v0, v1 = int(ok(a)), int(ok(a + 1))
            if v0 or v1:
                out[(kcr, qr)] = (r - a, v0, v1)
    return out


def _b_slots():
    inter = sorted(set(_b_entries("int").values()))
    rest = set()
    for tt in ("first", "last"):
        rest |= set(_b_entries(tt).values())
    rest = sorted(rest - set(inter))
    return inter + rest


B_SLOTS = _b_slots()
NBS = len(B_SLOTS)


def _b_runs(mode):
    items = []
    if mode in ("int", "first", "last"):
        ent = _b_entries(mode)
        for (kcr, qr), e in ent.items():
            items.append((kcr, qr, B_SLOTS.index(e), None))
    else:
        eI = _b_entries("int")
        eX = _b_entries("first" if mode == "q0" else "last")
        gI, gX = ("L", "F") if mode == "q0" else ("R", "Z")
        for kcr in range(-2, 6):
            for qr in range(8):
                a, b = eI.get((kcr, qr)), eX.get((kcr, qr))
                if a is not None and a == b:
                    items.append((kcr, qr, B_SLOTS.index(a), None))
                else:
                    if a is not None:
                        items.append((kcr, qr, B_SLOTS.index(a), gI))
                    if b is not None:
                        items.append((kcr, qr, B_SLOTS.index(b), gX))
    items.sort(key=lambda t: (t[0], str(t[3]), t[1]))
    runs = []
    for (kcr, qr, sl, g) in items:
        if runs:
            k0, q0, n0, s0, g0 = runs[-1]
            if k0 == kcr and g0 == g and q0 + n0 == qr and s0 + n0 == sl:
                runs[-1] = (k0, q0, n0 + 1, s0, g0)
                continue
        runs.append((kcr, qr, 1, sl, g))
    return runs


GATE_COL = {None: 0, "L": 1, "F": 2, "R": 3, "Z": 4, "S64": 6, "S32": 7, "S96": 8, "L64": 9, "L32": 10, "R96": 11}
NAM = 384 + 512 + 2048
VROW = 3 * 384 + 768

C_QA, C_KA, C_VA, C_QB, C_KB, C_VB, C_GA, C_GB = 0, 768, 1536, 2304, 2816, 3328, 3840, 4864


def _fm_chunk(w, col0):
    k = w.shape[0] // 128
    return w[:, col0:col0 + 128].reshape(k, 128, 128).transpose(1, 0, 2)


def _rhs_layout(w):
    k = w.shape[0] // 128
    return w.reshape(k, 128, w.shape[1]).transpose(1, 0, 2)


def _host_weights(w_in, w_a, w_b, w_o, w1, w2):
    f = np.float32
    chunks = []
    for c0 in [C_KA + 128 * i for i in range(6)] + [C_KB + 128 * i for i in range(4)]:
        chunks.append(_fm_chunk(w_in, c0))
    for c0 in [C_QA + 128 * i for i in range(6)] + [C_QB + 128 * i for i in range(4)]:
        chunks.append(_fm_chunk(w_in, c0))
    for c0 in [C_GA + 128 * i for i in range(8)] + [C_GB + 128 * i for i in range(8)]:
        chunks.append(_fm_chunk(w_in, c0))
    wfm = np.stack(chunks, axis=1).reshape(128, -1)
    wv = np.concatenate([_rhs_layout(w_in[:, C_VA + 256 * g:C_VA + 256 * (g + 1)]).reshape(128, -1) for g in range(3)]
                        + [_rhs_layout(w_in[:, C_VB:C_VB + 512]).reshape(128, -1)], axis=1)
    wa = _rhs_layout(w_a).reshape(128, -1)
    wb = _rhs_layout(w_b)
    wb = np.concatenate([wb[:, :, 0:512].reshape(128, -1), wb[:, :, 512:1024].reshape(128, -1)], axis=1)
    wo = _rhs_layout(w_o)
    pieces = []
    for ch in range(2):
        for kh in range(2):
            pieces.append(wo[:, 4 * kh:4 * kh + 4, 512 * ch:512 * ch + 512].reshape(128, -1))
    wo = np.concatenate(pieces, axis=1)
    w1c = np.stack([_fm_chunk(w1, 128 * i) for i in range(32)], axis=1).reshape(128, -1)
    w2l = _rhs_layout(w2)
    pieces = []
    for hh in range(2):
        for ch in range(2):
            for kq in range(4):
                k0 = 16 * hh + 4 * kq
                pieces.append(w2l[:, k0:k0 + 4, 512 * ch:512 * ch + 512].reshape(128, -1))
    w2p = np.concatenate(pieces, axis=1)
    wall = np.concatenate([wfm, wv, wa, wb, wo, w1c, w2p], axis=1).astype(f)
    return np.ascontiguousarray(wall)


OFF_WFM = 0
OFF_WV = 36 * 1024
OFF_WA = OFF_WV + 3 * 2048 + 4096
OFF_WB = OFF_WA + 2048
OFF_WO = OFF_WB + 4096
OFF_W1 = OFF_WO + 8192
OFF_W2 = OFF_W1 + 32 * 1024
W_TOT = OFF_W2 + 32 * 1024


def _host_consts(b_gate, rpb, ln1_g, ln1_b, b_ff1, b_ff2, ln2_g, ln2_b):
    f = np.float32
    cols = np.zeros((128, 64), f)
    cols[:, 0:8] = b_gate[0].reshape(8, 128).T
    cols[:, 8:16] = b_gate[1].reshape(8, 128).T
    cols[:, 16:48] = b_ff1.reshape(32, 128).T
    cols[:, 48:56] = ln1_g.reshape(8, 128).T
    cols[:, 56:64] = ln1_b.reshape(8, 128).T
    rep = np.stack([np.broadcast_to(v[None, :], (128, 1024)) for v in (ln1_g, ln1_b, b_ff2, ln2_g, ln2_b)], axis=1)
    rep = np.ascontiguousarray(rep.reshape(128, 5 * 1024)).astype(f)
    i = np.arange(128)[:, None]
    j = np.arange(128)[None, :]
    ms = []
    for d in (-1, 0, 1):
        ms.append((np.abs(128 * d + i - j) <= 64).astype(f))
    mq = np.arange(32)[None, :]
    ma3 = np.tile((i >= mq).astype(f), (1, 16))
    ms.append(ma3)
    t = np.arange(512)[None, :]
    for c in range(4):
        ms.append((((i % 16) == (t % 16)) & ((8 * c + i // 16) <= (t // 16))).astype(f))
    amask = np.concatenate(ms, axis=1)
    kcol = np.arange(64)[:, None]
    qc = np.arange(64)[None, :]
    wc = np.clip(qc - 8, 0, 48)
    okc = (kcol >= wc) & (kcol < wc + 16)
    dc = np.clip(kcol - qc + 15, 0, 30)
    tabs = np.full((128, 8, NBS * 64), NEG, f)
    for s, (delta, v0, v1) in enumerate(B_SLOTS):
        for krl, v in ((0, v0), (1, v1)):
            if not v:
                continue
            dr = krl - delta + 7
            assert 0 <= dr <= 14
            for h in range(8):
                vals = rpb[h, dr][dc]
                tabs[krl * 64:(krl + 1) * 64, h, s * 64:(s + 1) * 64] = np.where(okc, vals, NEG)
    tabs = np.ascontiguousarray(tabs.reshape(128, -1))
    ident = np.eye(128, dtype=f)
    perm = np.zeros((128, 128), f)
    for m in range(128):
        d = m % 64
        if d < 8:
            perm[m + 8, m] = 1.0
        elif d < 16:
            perm[m - 8, m] = 1.0
    misc = np.concatenate([ident, perm], axis=1)
    return cols, rep, amask, tabs, misc


def _rope_tables(q):
    f = np.float32
    inv = (np.float32(500000.0) ** (-np.arange(8, dtype=f) / np.float32(8))).astype(f)
    pos = np.zeros((NKV, T), f)
    for kt in range(16):
        pos[kt] = kt * T + np.arange(T)
    for u in range(8):
        pos[16 + u] = q * 2048 - 1024 + u * T + np.arange(T)
    ang = pos[:, None, :] * inv[None, :, None]
    cs, sn = np.cos(ang).astype(f), np.sin(ang).astype(f)
    tab = np.zeros((NKV, 128, 2, T), f)
    tab[:, :, 0, :] = 1.0
    for half in (0, 64):
        tab[:, half + 0:half + 8, 0, :] = cs
        tab[:, half + 8:half + 16, 0, :] = cs
        tab[:, half + 0:half + 8, 1, :] = -sn
        tab[:, half + 8:half + 16, 1, :] = sn
    return tab


def build_program(n_q_a=QT_A, n_q_b=QT_B, n_kv=NKV, debug=False):
    nc = bass.Bass("TRN2", target_bir_lowering=False)
    xs = nc.dram_tensor("xs", [NTOK, 1024], F32, kind="ExternalInput").ap()
    wall = nc.dram_tensor("wall", [128, W_TOT], F32, kind="ExternalInput").ap()
    ccols = nc.dram_tensor("ccols", [128, 64], F32, kind="ExternalInput").ap()
    crep = nc.dram_tensor("crep", [128, 5 * 1024], F32, kind="ExternalInput").ap()
    camask = nc.dram_tensor("camask", [128, NAM], F32, kind="ExternalInput").ap()
    ctabs = nc.dram_tensor("ctabs", [128, 8 * NBS * 64], F32, kind="ExternalInput").ap()
    cmisc = nc.dram_tensor("cmisc", [128, 256], F32, kind="ExternalInput").ap()
    cgate = nc.dram_tensor("cgate", [128, 16], F32, kind="ExternalInput").ap()
    crope = nc.dram_tensor("crope", [NKV, 128, 2 * T], F32, kind="ExternalInput").ap()
    ys = nc.dram_tensor("ys", [(QT_A + QT_B) * T, 1024], F32, kind="ExternalOutput").ap()
    wbf = nc.dram_tensor("wbf", [128, W_TOT], BF16, kind="Internal").ap()
    kscr = nc.dram_tensor("kscr", [10, 128, NTOK], BF16, kind="Internal").ap()
    vscr = nc.dram_tensor("vscr", [NTOK, VROW], BF16, kind="Internal").ap()

    P = Prog(nc)
    with contextlib.ExitStack() as st:
        def sb(name, shape, dt):
            return st.enter_context(nc.sbuf_tensor(name, shape, dt))

        NS, NL = 4, 3
        wS = sb("wS", [128, NS, 1024], BF16)
        wL = sb("wL", [128, NL, 2048], BF16)
        R_wS = [Res(f"wS{i}") for i in range(NS)]
        R_wL = [Res(f"wL{i}") for i in range(NL)]
        xst = sb("xst", [128, 2, 1024], F32)
        R_xst = [Res("xst0"), Res("xst1")]
        xres = sb("xres", [128, 4, 1024], F32)
        R_xres = [Res(f"xres{i}") for i in range(4)]
        arena = sb("arena", [128, 18, T], BF16)
        R_ar = [Res(f"ar{i}") for i in range(18)]
        mrg = sb("mrg", [128, 8, T], BF16)
        R_mrg = [Res(f"mrg{i}") for i in range(8)]
        OA = sb("OA", [128, 2, T], BF16)
        OB = sb("OB", [128, 4, T], BF16)
        R_OA = [Res("OA0"), Res("OA1")]
        R_OB = [Res(f"OB{i}") for i in range(4)]
        acc = sb("acc", [128, T], F32)
        R_acc = Res("acc")
        rcp = sb("rcp", [128, T], F32)
        R_rcp = [Res("rcp0"), Res("rcp1")]
        K1w = sb("K1w", [128, 2, 768], BF16)
        K2w = sb("K2w", [128, 2, 1536], BF16)
        K3w = sb("K3w", [128, 2, 2560], BF16)
        KBw = sb("KBw", [128, 4, 1024], BF16)
        R_K = [Res("K1w"), Res("K2w"), Res("K3w"), Res("KBw")]
        V1w = sb("V1w", [128, 6 * 384], BF16)
        Vbig = sb("Vbig", [128, 28 * 384], BF16)
        V2w = Vbig[:, 0:12 * 384]
        V3w = Vbig[:, 12 * 384:28 * 384]
        V3b = sb("V3b", [128, 4 * 384], BF16)
        VBw = sb("VBw", [128, 8 * 768], BF16)
        R_V = [Res("V1w"), Res("V2w"), Res("V3w"), Res("VBw"), Res("V3b")]
        NP = 5
        NSB = 5
        OBANKS = [5, 6, 7]
        DUMMY_BANK = 4
        N_WARM = 0
        Pt = sb("Pt", [128, NP, T], BF16)
        R_Pt = [Res(f"Pt{i}") for i in range(NP)]
        ft = sb("ft", [128, 3, T], F32)
        R_ft = [Res(f"ft{i}") for i in range(3)]
        sg = sb("sg", [128, 2, T], BF16)
        R_sg = [Res("sg0"), Res("sg1")]
        amask = sb("amask", [128, NAM], BF16)
        amask4 = sb("amask4", [128, 3, T], BF16)
        amaskr = sb("amaskr", [128, 384], BF16)
        R_amask = Res("amask")
        R_amask4 = Res("amask4")
        tabB = sb("tabB", [128, 8, NBS * 64], BF16)
        R_tabB = Res("tabB")
        reps = sb("reps", [128, 4, 1024], F32)
        R_reps = Res("reps")
        rope = sb("rope", [128, 2, T], F32)
        R_rope = Res("rope")
        cols = sb("cols", [128, 64], F32)
        R_cols = Res("cols")
        gate = sb("gate", [128, 16], F32)
        R_gate = Res("gate")
        ident = sb("ident", [128, 128], F32)
        permb = sb("permb", [128, 128], BF16)
        R_misc = Res("misc")
        stats = sb("stats", [128, 4, 2, 6], F32)
        mv = sb("mv", [128, 4, 2], F32)
        rstd = sb("rstd", [128, 4], F32)
        R_stats = [Res(f"st{i}") for i in range(4)]
        R_rstd = Res("rstd")
        kst = arena[:, 8:18, :]
        R_kst = R_ar[8:18]
        vst = Vbig[:, 0:4 * VROW].rearrange("p (c n) -> p c n", n=VROW)
        R_vst = [Res(f"vst{i}") for i in range(4)]
        psum = [st.enter_context(nc.psum_tensor(f"ps{i}", [128, T], F32)) for i in range(8)]
        R_ps = [Res(f"ps{i}") for i in range(8)]
        R_wbf = Res("wbf")
        R_scrK = Res("scrK")
        R_scrV = Res("scrV")
        R_kscr = [Res(f"kscr{kt}") for kt in range(NKV)]
        R_vscr = [Res(f"vscr{kt}") for kt in range(NKV)]

        cnt = {"dma": 0, "S": 0, "L": 0, "P": 0, "ev": 0, "mul": 0, "ob": 0}
        dbg_dmas = []

        def dump(name, ap, shape, dt, reads):
            if not debug:
                return
            d = nc.dram_tensor("dbg_" + name, list(shape), dt, kind="ExternalOutput").ap()
            dbg_dmas.append(P.op("sp", lambda e: e.dma_start(out=d, in_=ap), reads=reads, dma_key="dbg_" + name))

        def dkey(prefix):
            cnt["dma"] += 1
            return f"{prefix}{cnt['dma'] % 6}"

        for a0 in range(0, W_TOT, 16384):
            a1 = min(a0 + 16384, W_TOT)
            P.op("pool", lambda e, a0=a0, a1=a1: e.dma_start(
                out=wbf[:, a0:a1].rearrange("p (n f) -> p n f", f=2048),
                in_=wall[:, a0:a1].rearrange("p (n f) -> p n f", f=2048)),
                writes=[R_wbf], dma_key="wcast")
        P.op("sp", lambda e: e.dma_start(out=cols[:], in_=ccols), writes=[R_cols], dma_key="c0")
        P.op("sp", lambda e: e.dma_start(out=gate[:], in_=cgate), writes=[R_gate], dma_key="c1")
        P.op("sp", lambda e: e.dma_start(out=ident[:], in_=cmisc[:, 0:128]), writes=[R_misc], dma_key="c2")
        P.op("pool", lambda e: e.dma_start(out=permb[:], in_=cmisc[:, 128:256]), writes=[R_misc], dma_key="c3")
        P.op("pool", lambda e: e.dma_start(out=amask[:].rearrange("p (a b) -> p a b", b=128), in_=camask.rearrange("p (a b) -> p a b", b=128)), writes=[R_amask], dma_key="c4")
        for d_ in range(3):
            for rep_ in range(4):
                P.op("pool", lambda e, d_=d_, rep_=rep_: e.tensor_copy(out=amask4[:, d_, rep_ * 128:(rep_ + 1) * 128], in_=amask[:, d_ * 128:(d_ + 1) * 128]),
                     reads=[R_amask], writes=[R_amask4])
        for j_ in range(3):
            P.op("pool", lambda e, j_=j_: e.tensor_copy(out=amaskr[:, j_ * 128:(j_ + 1) * 128], in_=amask[:, (2 - j_) * 128:(3 - j_) * 128]),
                 reads=[R_amask], writes=[R_amask4])
        P.op("sp", lambda e: e.dma_start(out=reps[:, 0, :], in_=crep[:, 0:1024]), writes=[R_reps], dma_key="c5_0")
        P.op("sp", lambda e: e.dma_start(out=reps[:, 1, :], in_=crep[:, 1024:2048]), writes=[R_reps], dma_key="c5_1")
        P.op("sp", lambda e: e.dma_start(out=xres[:, 0, :], in_=crep[:, 2048:3072]), writes=[R_xres[0]], dma_key="c5_2")
        P.op("sp", lambda e: e.dma_start(out=reps[:, 2:4, :].rearrange("p a b -> p (a b)"), in_=crep[:, 3072:5120]), writes=[R_reps], dma_key="c5_3")
        P.op("dve", lambda e: e.tensor_scalar(out=reps[:, 0, :], in0=reps[:, 0, :], scalar1=ALPHA, scalar2=None, op0=ALU.mult), reads=[R_reps], writes=[R_reps])
        P.op("dve", lambda e: e.scalar_tensor_tensor(out=reps[:, 1, :], in0=reps[:, 1, :], scalar=ALPHA, in1=xres[:, 0, :], op0=ALU.mult, op1=ALU.add),
             reads=[R_reps, R_xres[0]], writes=[R_reps])
        for h in range(8):
            w = NBS * 64
            r = R_xres[1 + (h % 2)]
            P.op("sp", lambda e, h=h, w=w: e.dma_start(out=xres[:, 1 + (h % 2), 0:w], in_=ctabs[:, h * w:(h + 1) * w]), writes=[r], dma_key=f"tb{h % 2}")
            P.op("act", lambda e, h=h, w=w: e.activation(out=tabB[:, h, :], in_=xres[:, 1 + (h % 2), 0:w], func=AF.Exp), reads=[r], writes=[R_tabB])
        for (w_, r_) in ((K1w, R_K[0]), (K2w, R_K[1]), (K3w, R_K[2]), (KBw, R_K[3])):
            P.op("pool", lambda e, w_=w_: e.memset(w_[:].rearrange("p a b -> p (a b)"), 0.0), writes=[r_])
        for (w_, r_) in ((V1w, [R_V[0]]), (Vbig, [R_V[1], R_V[2]]), (VBw, [R_V[3]]), (V3b, [R_V[4]])):
            P.op("pool", lambda e, w_=w_: e.memset(w_[:], 0.0), writes=r_)
        for sl in range(4):
            P.op("pool", lambda e, sl=sl: e.memset(vst[:, sl, :].rearrange("p (a b c) -> p a b c", b=3, c=64)[:, :, 1, :], 1.0),
                 reads=[R_V[1], R_V[2]], writes=[R_vst[sl]])

        def load_S(off):
            i = cnt["S"] % NS
            cnt["S"] += 1
            P.op("sp", lambda e, i=i, off=off: e.dma_start(out=wS[:, i, :], in_=wbf[:, off:off + 1024]),
                 reads=[R_wbf], writes=[R_wS[i]], dma_key=f"wS{i}")
            return wS[:, i, :].rearrange("p (k c) -> p k c", c=128), R_wS[i]

        def load_L(off):
            i = cnt["L"] % NL
            cnt["L"] += 1
            P.op("sp", lambda e, i=i, off=off: e.dma_start(out=wL[:, i, :], in_=wbf[:, off:off + 2048]),
                 reads=[R_wbf], writes=[R_wL[i]], dma_key=f"wL{i}")
            return wL[:, i, :], R_wL[i]

        def mm(out, lhsT, rhs, start, stop, reads, writes, skip=False):
            P.op("pe", lambda e: e.matmul(out, lhsT=lhsT, rhs=rhs, start=start, stop=stop, skip_group_check=skip),
                 reads=reads, writes=writes)

        def evac_copy(out, in_, reads, writes):
            k = cnt["ev"] % 2
            cnt["ev"] += 1
            if k == 0:
                P.op("act", lambda e: e.activation(out=out, in_=in_, func=AF.Copy), reads=reads, writes=writes)
            else:
                P.op("dve", lambda e: e.tensor_copy(out=out, in_=in_), reads=reads, writes=writes)

        qt_list = [("A", i) for i in range(n_q_a)] + [("B", u) for u in range(n_q_b)]
        xseq = list(range(n_kv)) + [(i if j_ == "A" else 18 + i) for (j_, i) in qt_list]
        xstate = {"i": 0}

        mrg32 = mrg[:].rearrange("p a b -> p (a b)").bitcast(F32).rearrange("p (s n) -> p s n", n=1024)
        ring1 = [(xres[:, i, :], [R_xres[i]]) for i in range(4)] + [(xst[:, i, :], [R_xst[i]]) for i in range(2)]
        ring2 = [(xst[:, 0, :], [R_xst[0]]), (xst[:, 1, :], [R_xst[1]]), (mrg32[:, 0, :], list(R_mrg[0:4])), (mrg32[:, 1, :], list(R_mrg[4:8]))]
        n1 = 4 * n_kv

        def xslot(g):
            if g < n1:
                return ring1[g % 6], g % 6
            k_ = (g - n1) % 4
            return ring2[k_], (4 + k_ if k_ < 2 else 4 + k_)
        xstate["req"] = 0
        xocc = {}

        def xrequest_upto(g, own=False):
            total = 4 * len(xseq)
            while xstate["req"] < total:
                r = xstate["req"]
                (ap_, res_), sid = xslot(r)
                prev = xocc.get(sid)
                if prev is not None and prev >= g:
                    break
                if sid >= 6 and not (own and (r // 4) == (g // 4)):
                    break
                xocc[sid] = r
                kt_, tc = xseq[r // 4], r % 4
                P.op("sp", lambda e, ap_=ap_, kt_=kt_, tc=tc: e.dma_start(out=ap_, in_=xs[kt_ * T + tc * 128: kt_ * T + (tc + 1) * 128, :]),
                     writes=res_, dma_key="x" + res_[0].name)
                xstate["req"] += 1

        def make_xT(kt):
            idx = xstate["i"]
            xstate["i"] += 1
            assert xseq[idx] == kt
            xrequest_upto(4 * idx, own=True)
            for rnd in range(2):
                for tc in range(4):
                    g = 4 * idx + tc
                    (ap_, res_), sid = xslot(g)
                    for f4 in range(4):
                        fc = 4 * rnd + f4
                        P.op("pe", lambda e, tc=tc, ap_=ap_, fc=fc, f4=f4: e.transpose(out=psum[f4][:, tc * 128:(tc + 1) * 128], in_=ap_[:, fc * 128:(fc + 1) * 128], identity=ident[:]),
                             reads=res_ + [R_misc], writes=[R_ps[f4]])
                for f4 in range(4):
                    fc = 4 * rnd + f4
                    if idx >= n_kv:
                        P.op("act", lambda e, fc=fc, f4=f4: e.activation(out=arena[:, fc, :], in_=psum[f4][:], func=AF.Copy), reads=[R_ps[f4]], writes=[R_ar[fc]])
                    else:
                        evac_copy(arena[:, fc, :], psum[f4][:], [R_ps[f4]], [R_ar[fc]])
            xrequest_upto(4 * (idx + 1))

        def rope_chunk(src_bank, dst_ap, dst_res, bankB, add_eng="pool"):
            i = cnt["P"] % NP
            cnt["P"] += 1
            P.op("act", lambda e: e.activation(out=Pt[:, i, :], in_=psum[src_bank][:], func=AF.Copy), reads=[R_ps[src_bank]], writes=[R_Pt[i]])

            def fin():
                mm(psum[bankB][:], permb[:], Pt[:, i, :], True, True, [R_Pt[i], R_misc], [R_ps[bankB]])
                P.op("dve", lambda e: e.tensor_tensor(out=ft[:, 0, :], in0=psum[bankB][:], in1=rope[:, 1, :], op=ALU.mult), reads=[R_ps[bankB], R_rope], writes=[R_ft[0]])
                P.op("dve", lambda e: e.tensor_tensor(out=ft[:, 1, :], in0=psum[src_bank][:], in1=rope[:, 0, :], op=ALU.mult), reads=[R_ps[src_bank], R_rope], writes=[R_ft[1]])
                P.op(add_eng, lambda e: e.tensor_tensor(out=dst_ap, in0=ft[:, 0, :], in1=ft[:, 1, :], op=ALU.add), reads=[R_ft[0], R_ft[1]], writes=[dst_res])
            return fin

        def proj_chunks(w_base, dst_fn, resident=None):
            pend = None
            for c in range(10):
                if resident is not None:
                    wv_, rw = resident[c]
                else:
                    wv_, rw = load_S(OFF_WFM + (w_base + c) * 1024)
                b = c % 4
                for k in range(8):
                    mm(psum[b][:], wv_[:, k, :], arena[:, k, :], k == 0, k == 7, [rw, R_ar[k]], [R_ps[b]])
                if pend is not None:
                    pend()
                    pend = None
                dst_ap, dst_res = dst_fn(c)
                if c < 6:
                    pend = rope_chunk(b, dst_ap, dst_res, 4 + (c % 4), add_eng=("pool" if resident is not None else "dve"))
                else:
                    evac_copy(dst_ap, psum[b][:], [R_ps[b]], [dst_res])
            if pend is not None:
                pend()

        def load_rope(kt):
            P.op("sp", lambda e: e.dma_start(out=rope[:].rearrange("p a b -> p (a b)"), in_=crope[kt]), writes=[R_rope], dma_key="rope")

        k3f = K3w[:].rearrange("p a b -> p (a b)")
        kbf = KBw[:].rearrange("p a b -> p (a b)")
        k2f = K2w[:].rearrange("p a b -> p (a b)")
        mrgf = mrg[:].rearrange("p a b -> p (a b)")
        kres_slots = [(k3f[:, i * 1024:(i + 1) * 1024], R_K[2]) for i in range(5)] + \
                     [(kbf[:, i * 1024:(i + 1) * 1024], R_K[3]) for i in range(4)] + [(k2f[:, 0:1024], R_K[1])]
        wk_res = []
        for c in range(10):
            ap_, r_ = kres_slots[c]
            P.op("sp", lambda e, ap_=ap_, c=c: e.dma_start(out=ap_, in_=wbf[:, OFF_WFM + c * 1024:OFF_WFM + (c + 1) * 1024]),
                 reads=[R_wbf], writes=[r_], dma_key=f"p1k{c}")
            wk_res.append((ap_.rearrange("p (k c) -> p k c", c=128), r_))
        vres_slots = [(VBw[:, 0:2048], [R_V[3]]), (VBw[:, 2048:4096], [R_V[3]]), (VBw[:, 4096:6144], [R_V[3]]),
                      (V1w[:, 0:2048], [R_V[0]]), (mrgf[:, 0:2048], list(R_mrg[0:4]))]
        wv_res = []
        for pc in range(5):
            ap_, r_ = vres_slots[pc]
            P.op("sp", lambda e, ap_=ap_, pc=pc: e.dma_start(out=ap_, in_=wbf[:, OFF_WV + pc * 2048:OFF_WV + (pc + 1) * 2048]),
                 reads=[R_wbf], writes=r_, dma_key=f"p1v{pc}")
            wv_res.append((ap_, r_))
        for kt in range(n_kv):
            make_xT(kt)
            load_rope(kt)
            proj_chunks(0, lambda c: (kst[:, c, :], R_kst[c]), resident=wk_res)
            P.op("pool", lambda e, kt=kt: e.dma_start(out=kscr[:, :, kt * T:(kt + 1) * T].rearrange("c p t -> p c t"), in_=kst),
                 reads=list(R_kst), writes=[R_scrK], dma_key="scrK")
            for g in range(4):
                ncol = 512 if g == 3 else 256
                woff = OFF_WV + (g * 2048 if g < 3 else 3 * 2048)
                pieces = [wv_res[g]] if g < 3 else [wv_res[3], wv_res[4]]
                for ch in range(4):
                    b = 4 + (ch % 2) + 2 * (g % 2)
                    for k in range(8):
                        lt = arena[:, k, ch * 128:(ch + 1) * 128]
                        if g < 3:
                            wl, rw = pieces[0]
                            rhs = wl.rearrange("p (k c) -> p k c", c=256)[:, k, :]
                        else:
                            wl, rw = pieces[k // 4]
                            rhs = wl.rearrange("p (k c) -> p k c", c=512)[:, k % 4, :]
                        mm(psum[b][:, 0:ncol], lt, rhs, k == 0, k == 7, list(rw) + [R_ar[k]], [R_ps[b]])
                    npair = ncol // 128
                    c0 = g * 384
                    sl = ch
                    dst = vst[:, sl, c0:c0 + npair * 192].rearrange("p (a b c) -> p a b c", b=3, c=64)[:, :, 0:3:2, :]
                    src = psum[b][:, 0:ncol].rearrange("p (a b c) -> p a b c", b=2, c=64)
                    evac_copy(dst, src, [R_ps[b]], [R_vst[sl]])
            P.op("pool", lambda e, kt=kt: e.dma_start(out=vscr[kt * T:(kt + 1) * T, :].rearrange("(c p) n -> p c n", p=128), in_=vst),
                 reads=list(R_vst), writes=[R_scrV], dma_key="scrV")

        out_dmas = []
        qtiles = [("A", i) for i in range(n_q_a)] + [("B", u) for u in range(n_q_b)]

        def tile_params(job, ti):
            if job == "A":
                return dict(kt=ti, kt_lo=0, kt_hi=16, bmode="first" if ti == 0 else ("last" if ti == 15 else "int"), orow=ti * T,
                            kgate=lambda ktile: None)
            return dict(kt=18 + ti, kt_lo=16, kt_hi=24, bmode="q0" if ti == 0 else ("q3" if ti == 3 else "int"), orow=(QT_A + ti) * T,
                        kgate=lambda ktile: "L" if ktile < 18 else ("R" if ktile > 21 else None))

        def window_loads(job, ti):
            tp_ = tile_params(job, ti)
            kt, kt_lo, kt_hi = tp_["kt"], tp_["kt_lo"], tp_["kt_hi"]
            tok0 = kt * T
            lo_tok, hi_tok = kt_lo * T, kt_hi * T
            thunks = []

            def kload(win, rw, chs, t_lo, t_hi):
                a, b_ = max(t_lo, lo_tok), min(t_hi, hi_tok)
                src = kscr[chs[0]:chs[-1] + 1, :, a:b_].rearrange("c p t -> p c t")
                thunks.append(lambda WQ: P.op(WQ, lambda e: e.dma_start(out=win[:, :, a - t_lo:b_ - t_lo], in_=src),
                                              reads=[R_scrK, R_scrV], writes=[rw], dma_key="w" + rw.name))
            kload(K1w, R_K[0], [0, 1], tok0 - 128, tok0 + 640)
            kload(K2w, R_K[1], [2, 3], tok0 - 512, tok0 + 1024)
            kload(K3w, R_K[2], [4, 5], tok0 - 1024, tok0 + 1536)
            kload(KBw, R_K[3], [6, 7, 8, 9], tok0 - 256, tok0 + 768)

            def vgather(win, rv, wcol0, col0, ncol, row0, dims, p_lo=0, p_hi=128, pstride=1):
                if p_hi <= p_lo:
                    return
                nch = 1
                for (_, c_) in dims:
                    nch *= c_
                src = bass.AP(tensor=vscr.tensor, offset=(row0 + pstride * p_lo) * VROW + col0,
                              ap=[[pstride * VROW, p_hi - p_lo]] + [[rs * VROW, c_] for (rs, c_) in dims] + [[1, ncol]])
                dst = win[p_lo:p_hi, wcol0:wcol0 + nch * ncol]
                if len(dims) == 1:
                    dst = dst.rearrange("p (a n) -> p a n", n=ncol)
                elif len(dims) == 2:
                    dst = dst.rearrange("p (a b n) -> p a b n", b=dims[1][1], n=ncol)
                thunks.append(lambda WQ: P.op(WQ, lambda e: e.dma_start(out=dst, in_=src), reads=[R_scrK, R_scrV], writes=[rv], dma_key="w" + rv.name))

            def tile_ok(ktl):
                return kt_lo <= ktl < kt_hi
            ccs = [4 * kt - 1 + s_ for s_ in range(6) if tile_ok((4 * kt - 1 + s_) // 4)]
            vgather(V1w, R_V[0], (ccs[0] - (4 * kt - 1)) * 384, 0, 384, ccs[0] * 128, [(128, len(ccs))])
            for d in range(3):
                if tile_ok(kt - 1 + d):
                    vgather(V2w, R_V[1], 4 * d * 384, 384, 384, (kt - 1 + d) * T, [(1, 4)], pstride=4)
            base3 = tok0 - 1024
            i_lo = max(0, (lo_tok - base3) // 16)
            i_hi = min(128, (hi_tok - base3) // 16)
            for r0 in range(0, 16, 4):
                vgather(V3w, R_V[2], r0 * 384, 768, 384, base3 + r0, [(1, 4)], i_lo, i_hi, pstride=16)
            if tile_ok(kt + 2):
                vgather(V3b, R_V[4], 0, 768, 384, (kt + 2) * T, [(128, 4)])
            ccs = [4 * kt - 2 + s_ for s_ in range(8) if tile_ok((4 * kt - 2 + s_) // 4)]
            half_n = (len(ccs) + 1) // 2
            for part in (ccs[:half_n], ccs[half_n:]):
                if part:
                    vgather(VBw, R_V[3], (part[0] - (4 * kt - 2)) * 768, 1152, 768, part[0] * 128, [(128, len(part))])
            return thunks

        if qtiles:
            for th in window_loads(*qtiles[0]):
                th("sp")
        for qi, (job, ti) in enumerate(qtiles):
            tp_ = tile_params(job, ti)
            kt, kt_lo, kt_hi, bmode, orow, kgate = tp_["kt"], tp_["kt_lo"], tp_["kt_hi"], tp_["bmode"], tp_["orow"], tp_["kgate"]
            tok0 = kt * T

            def tile_ok(ktl, kt_lo=kt_lo, kt_hi=kt_hi):
                return kt_lo <= ktl < kt_hi

            make_xT(kt)
            load_rope(kt)
            proj_chunks(10, lambda c: (arena[:, 8 + c, :], R_ar[8 + c]))

            if qi == 0:
                dump("xT", arena[:, 0:8, :], [128, 8, T], BF16, R_ar[0:8])
                dump("Q", arena[:, 8:18, :], [128, 10, T], BF16, R_ar[8:18])
                dump("K3w", K3w[:], [128, 2, 2560], BF16, [R_K[2]])
                dump("KBw", KBw[:], [128, 4, 1024], BF16, [R_K[3]])
                dump("V1w", V1w[:], [128, 6 * 384], BF16, [R_V[0]])
                dump("V3w", V3w, [128, 16 * 384], BF16, [R_V[2]])
                dump("VBw", VBw[:], [128, 8 * 768], BF16, [R_V[3]])
                dump("tabB", tabB[:], [128, 8, NBS * 64], BF16, [R_tabB])
            DEPTH = 4
            pipe = []

            def vaug(win, chunk_off, pair, half):
                o = chunk_off + pair * 192 + 64 * half
                return win[:, o:o + 128]

            post_sched = []
            POST_LAG = 2

            def run_due(force=False):
                keep = []
                for item in post_sched:
                    if force or item[0] <= 0:
                        item[1]()
                    else:
                        keep.append(item)
                post_sched[:] = keep

            def pop_one():
                fn = pipe.pop(0)
                fn()
                for item in post_sched:
                    item[0] -= 1
                run_due()

            def submit(subs, mask_ops, obank, first_flag, post=None, sres=(), vres=()):
                i = cnt["P"] % NP
                cnt["P"] += 1
                sbk = i % NSB
                for j, (kap, qap, c0, n, g, vap, oc0) in enumerate(subs):
                    mm(psum[sbk][:, c0:c0 + n], kap, qap, j == 0, False, list(sres), [R_ps[sbk]], skip=True)
                j = 0
                while j < len(subs):
                    j2 = j
                    while j2 + 1 < len(subs) and subs[j2 + 1][4] == subs[j][4] and subs[j2 + 1][2] == subs[j2][2] + subs[j2][3]:
                        j2 += 1
                    c0 = subs[j][2]
                    c1 = subs[j2][2] + subs[j2][3]
                    gc = GATE_COL[subs[j][4]]
                    P.op("act", lambda e, i=i, sbk=sbk, c0=c0, c1=c1, gc=gc: e.activation(out=Pt[:, i, c0:c1], in_=psum[sbk][:, c0:c1], func=AF.Exp, scale=0.125, bias=gate[:, gc:gc + 1]),
                         reads=[R_ps[sbk], R_gate], writes=[R_Pt[i]])
                    j = j2 + 1
                for (c0, n, map_, mres) in mask_ops:
                    eng = "pool" if cnt["mul"] % 4 == 3 else "dve"
                    cnt["mul"] += 1
                    P.op(eng, lambda e, i=i, c0=c0, n=n, map_=map_: e.tensor_tensor(out=Pt[:, i, c0:c0 + n], in0=Pt[:, i, c0:c0 + n], in1=map_, op=ALU.mult),
                         reads=[R_Pt[i], mres], writes=[R_Pt[i]])

                for _ in range(N_WARM):
                    P.op("pe", lambda e: e.matmul(psum[DUMMY_BANK][:], lhsT=permb[:], rhs=amask4[:, 0, :], start=True, stop=True), reads=[], writes=[R_ps[DUMMY_BANK]])

                def pv(i=i, subs=subs, first_flag=first_flag, post=post):
                    ff = first_flag
                    for (kap, qap, c0, n, g, vap, oc0) in subs:
                        mm(psum[obank][:, oc0:oc0 + n], vap, Pt[:, i, c0:c0 + n], ff, False, list(vres) + [R_Pt[i]], [R_ps[obank]], skip=True)
                        ff = False
                    if post is not None:
                        for k_, st_ in enumerate(post()):
                            post_sched.append([POST_LAG + k_, st_])
                pipe.append(pv)
                while len(pipe) > DEPTH:
                    pop_one()

            def next_obank():
                b_ = OBANKS[cnt["ob"] % len(OBANKS)]
                cnt["ob"] += 1
                return b_

            if job == "A":
                g3gate = {0: "S64", 1: "S32", 15: "S96"}.get(ti)
            else:
                g3gate = {0: "L64", 1: "L32", 3: "R96"}.get(ti)

            for hs in range(4):
                half = hs % 2
                pr = slice(64 * half, 64 * half + 64)
                po = slice(64 * (1 - half), 64 * (1 - half) + 64)
                qch = hs // 2
                acc_ap, acc_res = (acc[:], R_acc) if hs % 2 == 0 else (ft[:, 2, :], R_ft[2])
                ob1 = next_obank()
                blocks = []
                for s_ in range(6):
                    cc = 4 * kt - 1 + s_
                    ktl = cc // 4
                    if not tile_ok(ktl):
                        continue
                    q_lo, q_hi = max(0, s_ - 2), min(3, s_)
                    n = 128 * (q_hi - q_lo + 1)
                    j_lo = q_lo - s_ + 2
                    koff = s_ * 128
                    subs = [(K1w[pr, qch, koff:koff + 128], arena[pr, 8 + qch, q_lo * 128:q_lo * 128 + n], 0, n, kgate(ktl),
                             vaug(V1w, s_ * 384, qch, half), q_lo * 128)]
                    blocks.append((subs, [(0, n, amaskr[:, j_lo * 128:j_lo * 128 + n], R_amask4)], [R_K[0], R_ar[8 + qch]], [R_V[0]]))
                if tile_ok(kt + 2):
                    for c_ in range(4):
                        koff = 2048 + c_ * 128
                        q0_ = 128 * c_
                        subs = [(K3w[pr, qch, koff:koff + 128], arena[pr, 12 + qch, q0_:T], 0, T - q0_, kgate(kt + 2), vaug(V3b, c_ * 384, qch, half), q0_)]
                        blocks.append((subs, [(0, T - q0_, amask[:, 896 + 512 * c_ + q0_:896 + 512 * (c_ + 1)], R_amask)], [R_K[2], R_ar[12 + qch]], [R_V[4]]))

                def post1(ob1=ob1, acc_ap=acc_ap, acc_res=acc_res):
                    return [lambda: P.op("act", lambda e: e.activation(out=acc_ap, in_=psum[ob1][:], func=AF.Copy), reads=[R_ps[ob1]], writes=[acc_res])]
                for bi, (subs, mops, sres, vres) in enumerate(blocks):
                    submit(subs, mops, ob1, bi == 0, post1 if bi == len(blocks) - 1 else None, sres, vres)
                ob2 = next_obank()
                blocks = []
                for d in (-1, 0, 1):
                    ktl = kt + d
                    if not tile_ok(ktl):
                        continue
                    kb0 = (ktl * T) - (tok0 - 512)
                    subs = []
                    for r4 in range(4):
                        sl = 4 * (d + 1) + r4
                        subs.append((K2w[pr, qch, kb0 + r4:kb0 + T:4], arena[pr, 10 + qch, r4:T:4], r4 * 128, 128, kgate(ktl),
                                     vaug(V2w, sl * 384, qch, half), r4 * 128))
                    blocks.append((subs, [(0, T, amask4[:, d + 1, :], R_amask4)], [R_K[1], R_ar[10 + qch]], [R_V[1]]))

                def post2(ob2=ob2, acc_ap=acc_ap, acc_res=acc_res):
                    return [lambda: P.op("dve", lambda e: e.tensor_tensor(out=acc_ap.rearrange("p (m r) -> p m r", r=4), in0=acc_ap.rearrange("p (m r) -> p m r", r=4),
                                                                          in1=psum[ob2][:].rearrange("p (r m) -> p m r", r=4), op=ALU.add),
                                         reads=[R_ps[ob2], acc_res], writes=[acc_res])]
                for bi, (subs, mops, sres, vres) in enumerate(blocks):
                    submit(subs, mops, ob2, bi == 0, post2 if bi == len(blocks) - 1 else None, sres, vres)
                ob3 = next_obank()
                subs = []
                for r in range(16):
                    subs.append((K3w[pr, qch, r:2048:16], arena[pr, 12 + qch, r:T:16], r * 32, 32, g3gate, vaug(V3w, r * 384, qch, half), r * 32))

                def post3(ob3=ob3, acc_ap=acc_ap, acc_res=acc_res, pr=pr, po=po, qch=qch, half=half):
                    def st0():
                        P.op("dve", lambda e: e.tensor_tensor(out=acc_ap.rearrange("p (m r) -> p m r", r=16), in0=acc_ap.rearrange("p (m r) -> p m r", r=16),
                                                              in1=psum[ob3][:].rearrange("p (r m) -> p m r", r=16), op=ALU.add),
                             reads=[R_ps[ob3], acc_res], writes=[acc_res])

                    def st1():
                        P.op("act", lambda e: e.activation(out=rcp[pr, :], in_=acc_ap[po, :], func=AF.Ln), reads=[acc_res], writes=[R_rcp[half]])
                        P.op("act", lambda e: e.activation(out=rcp[pr, :], in_=rcp[pr, :], func=AF.Exp, scale=-1.0), reads=[R_rcp[half]], writes=[R_rcp[half]])

                    def st2():
                        P.op("dve", lambda e: e.tensor_tensor(out=OA[pr, qch, :], in0=acc_ap[pr, :], in1=rcp[pr, :], op=ALU.mult),
                             reads=[acc_res, R_rcp[half]], writes=[R_OA[qch]])
                    return [st0, st1, st2]
                submit(subs, [(0, T, amask[:, 384:896], R_amask)], ob3, True, post3, [R_K[2], R_ar[12 + qch]], [R_V[2]])

            runs = _b_runs(bmode)
            for h in range(8):
                half = h % 2
                pr = slice(64 * half, 64 * half + 64)
                po = slice(64 * (1 - half), 64 * (1 - half) + 64)
                kch = h // 2
                ob = next_obank()
                blocks = []
                for (kcr, qr0, nr, sl0, g) in runs:
                    cc = 4 * kt + kcr
                    ktl = cc // 4
                    if not tile_ok(ktl):
                        continue
                    gname = g if g is not None else kgate(ktl)
                    koff = (kcr + 2) * 128
                    n = nr * 64
                    subs = [(KBw[pr, kch, koff:koff + 128], arena[pr, 14 + kch, qr0 * 64:qr0 * 64 + n], 0, n, gname,
                             vaug(VBw, (kcr + 2) * 768, kch, half), qr0 * 64)]
                    blocks.append((subs, [(0, n, tabB[:, h, sl0 * 64:sl0 * 64 + n], R_tabB)]))

                def postB(ob=ob, pr=pr, po=po, kch=kch, half=half):
                    def st0():
                        P.op("act", lambda e: e.activation(out=rcp[pr, :], in_=psum[ob][po, :], func=AF.Ln), reads=[R_ps[ob]], writes=[R_rcp[half]])
                        P.op("act", lambda e: e.activation(out=rcp[pr, :], in_=rcp[pr, :], func=AF.Exp, scale=-1.0), reads=[R_rcp[half]], writes=[R_rcp[half]])

                    def st1():
                        P.op("dve", lambda e: e.tensor_tensor(out=OB[pr, kch, :], in0=psum[ob][pr, :], in1=rcp[pr, :], op=ALU.mult),
                             reads=[R_ps[ob], R_rcp[half]], writes=[R_OB[kch]])
                    return [st0, st1]
                for bi, (subs, mops) in enumerate(blocks):
                    submit(subs, mops, ob, bi == 0, postB if bi == len(blocks) - 1 else None, [R_K[3], R_ar[14 + kch]], [R_V[3]])
            while pipe:
                pop_one()
            run_due(force=True)

            if qi == 0:
                dump("OA", OA[:], [128, 2, T], BF16, R_OA)
                dump("OB", OB[:], [128, 4, T], BF16, R_OB)
            wa_ap, rwa = load_L(OFF_WA)
            wb0, rwb0 = load_L(OFF_WB)
            wb1, rwb1 = load_L(OFF_WB + 2048)
            wa3 = wa_ap.rearrange("p (k c) -> p k c", c=1024)
            for jc in range(8):
                bs = 0 if jc % 2 == 0 else 4
                if jc % 2 == 0:
                    t0_ap, t0_res, t1_ap, t1_res = ft[:, 0, :], [R_ft[0]], ft[:, 1, :], [R_ft[1]]
                else:
                    t0_ap, t0_res, t1_ap, t1_res = ft[:, 2, :], [R_ft[2]], rcp[:], list(R_rcp)
                for k in range(2):
                    mm(psum[bs][:], wa3[:, k, jc * 128:(jc + 1) * 128], OA[:, k, :], k == 0, k == 1, [rwa, R_OA[k]], [R_ps[bs]])
                wbp, rwb = (wb0, rwb0) if jc < 4 else (wb1, rwb1)
                wb3 = wbp.rearrange("p (k c) -> p k c", c=512)
                for k in range(4):
                    mm(psum[bs + 1][:], wb3[:, k, (jc % 4) * 128:(jc % 4 + 1) * 128], OB[:, k, :], k == 0, k == 3, [rwb, R_OB[k]], [R_ps[bs + 1]])
                for gi in range(2):
                    wg, rwg = load_S(OFF_WFM + (20 + 8 * gi + jc) * 1024)
                    for k in range(8):
                        mm(psum[bs + 2 + gi][:], wg[:, k, :], arena[:, k, :], k == 0, k == 7, [rwg, R_ar[k]], [R_ps[bs + 2 + gi]])
                    P.op("act", lambda e, gi=gi, jc=jc, bs=bs: e.activation(out=sg[:, gi, :], in_=psum[bs + 2 + gi][:], func=AF.Sigmoid, bias=cols[:, 8 * gi + jc:8 * gi + jc + 1]),
                         reads=[R_ps[bs + 2 + gi], R_cols], writes=[R_sg[gi]])
                P.op("dve", lambda e, bs=bs, t0_ap=t0_ap: e.tensor_tensor(out=t0_ap, in0=psum[bs][:], in1=sg[:, 0, :], op=ALU.mult), reads=[R_ps[bs], R_sg[0]], writes=t0_res)
                P.op("dve", lambda e, bs=bs, t1_ap=t1_ap: e.tensor_tensor(out=t1_ap, in0=psum[bs + 1][:], in1=sg[:, 1, :], op=ALU.mult), reads=[R_ps[bs + 1], R_sg[1]], writes=t1_res)
                P.op("pool", lambda e, jc=jc, t0_ap=t0_ap, t1_ap=t1_ap: e.tensor_tensor(out=mrg[:, jc, :], in0=t0_ap, in1=t1_ap, op=ALU.add), reads=t0_res + t1_res, writes=[R_mrg[jc]])
                if jc == 0:
                    wthunks = window_loads(*qtiles[qi + 1]) if qi + 1 < len(qtiles) else []
                nth = (len(wthunks) + 6) // 7
                for th in wthunks[jc * nth:(jc + 1) * nth] if jc < 7 else wthunks[7 * nth:]:
                    th("sp")

            if qi == 0:
                dump("mrg", mrg[:], [128, 8, T], BF16, R_mrg)
            for tc in range(4):
                P.op("pool", lambda e, tc=tc, tok0=tok0: e.dma_start(out=xres[:, tc, :], in_=xs[tok0 + tc * 128:tok0 + (tc + 1) * 128, :]),
                     writes=[R_xres[tc]], dma_key=f"xres{tc}")

            def layer_norm_all(stats_done=False):
                for tc in range(4):
                    for hf in range(2):
                        if stats_done:
                            continue
                        P.op("dve", lambda e, hf=hf, tc=tc: e.bn_stats(out=stats[:, tc, hf, :], in_=xres[:, tc, hf * 512:(hf + 1) * 512]), reads=[R_xres[tc]], writes=[R_stats[tc]])
                    P.op("dve", lambda e, tc=tc: e.bn_aggr(out=mv[:, tc, :], in_=stats[:, tc].rearrange("p a b -> p (a b)")), reads=[R_stats[tc]], writes=[R_stats[tc]])
                P.op("act", lambda e: e.activation(out=rstd[:, 0:4], in_=mv[:, :, 1], func=AF.Sqrt, bias=gate[:, 5:6]), reads=list(R_stats) + [R_gate], writes=[R_rstd])
                P.op("dve", lambda e: e.reciprocal(out=rstd[:, 0:4], in_=rstd[:, 0:4]), reads=[R_rstd], writes=[R_rstd])
                for tc in range(4):
                    P.op("dve", lambda e, tc=tc: e.tensor_scalar(out=xres[:, tc, :], in0=xres[:, tc, :], scalar1=mv[:, tc, 0:1], scalar2=rstd[:, tc:tc + 1], op0=ALU.subtract, op1=ALU.mult),
                         reads=[R_xres[tc], R_stats[tc], R_rstd], writes=[R_xres[tc]])

            for chh in range(2):
                for kh in range(2):
                    wl, rw = load_L(OFF_WO + (2 * chh + kh) * 2048)
                    for tc in range(4):
                        b = 4 + tc
                        for k4 in range(4):
                            k = 4 * kh + k4
                            rhs = wl.rearrange("p (k c) -> p k c", c=512)[:, k4, :]
                            mm(psum[b][:], mrg[:, k, tc * 128:(tc + 1) * 128], rhs, k == 0, k == 7, [rw, R_mrg[k]], [R_ps[b]])
                for tc in range(4):
                    b = 4 + tc
                    P.op("dve", lambda e, tc=tc, b=b, chh=chh: e.scalar_tensor_tensor(out=xres[:, tc, chh * 512:(chh + 1) * 512], in0=xres[:, tc, chh * 512:(chh + 1) * 512],
                                                                                    scalar=ALPHA, in1=psum[b][:], op0=ALU.mult, op1=ALU.add),
                         reads=[R_xres[tc], R_ps[b]], writes=[R_xres[tc]])
                    P.op("dve", lambda e, tc=tc, chh=chh: e.bn_stats(out=stats[:, tc, chh, :], in_=xres[:, tc, chh * 512:(chh + 1) * 512]), reads=[R_xres[tc]], writes=[R_stats[tc]])
            layer_norm_all(stats_done=True)
            if qi == 0:
                dump("z1", xres[:], [128, 4, 1024], F32, R_xres)
            for tc in range(4):
                for fc in range(8):
                    b = fc
                    P.op("pe", lambda e, tc=tc, fc=fc, b=b: e.transpose(out=psum[b][:, tc * 128:(tc + 1) * 128], in_=xres[:, tc, fc * 128:(fc + 1) * 128], identity=ident[:]),
                         reads=[R_xres[tc], R_misc], writes=[R_ps[b]])
            for fc in range(8):
                if fc % 2 == 0:
                    P.op("act", lambda e, fc=fc: e.activation(out=mrg[:, fc, :], in_=psum[fc][:], func=AF.Identity, scale=cols[:, 48 + fc:49 + fc], bias=cols[:, 56 + fc:57 + fc]),
                         reads=[R_ps[fc], R_cols], writes=[R_mrg[fc]])
                else:
                    P.op("dve", lambda e, fc=fc: e.tensor_scalar(out=mrg[:, fc, :], in0=psum[fc][:], scalar1=cols[:, 48 + fc:49 + fc], scalar2=cols[:, 56 + fc:57 + fc], op0=ALU.mult, op1=ALU.add),
                         reads=[R_ps[fc], R_cols], writes=[R_mrg[fc]])
            for tc in range(4):
                P.op("pool", lambda e, tc=tc: e.tensor_tensor(out=xres[:, tc, :], in0=xres[:, tc, :], in1=reps[:, 0, :], op=ALU.mult), reads=[R_xres[tc], R_reps], writes=[R_xres[tc]])
                P.op("pool", lambda e, tc=tc: e.tensor_tensor(out=xres[:, tc, :], in0=xres[:, tc, :], in1=reps[:, 1, :], op=ALU.add), reads=[R_xres[tc], R_reps], writes=[R_xres[tc]])

            if qi == 0:
                dump("x1T", mrg[:], [128, 8, T], BF16, R_mrg)
                dump("x1res", xres[:], [128, 4, 1024], F32, R_xres)
            for hh in range(2):
                for kk in range(16):
                    hk = 16 * hh + kk
                    w1_, rw = load_S(OFF_W1 + hk * 1024)
                    b = kk % 2
                    for k in range(8):
                        mm(psum[b][:], w1_[:, k, :], mrg[:, k, :], k == 0, k == 7, [rw, R_mrg[k]], [R_ps[b]])
                    fi = 2 if kk % 2 == 0 else 0
                    P.op("dve", lambda e, b=b, hk=hk, fi=fi: e.tensor_scalar(out=ft[:, fi, :], in0=psum[b][:], scalar1=cols[:, 16 + hk:17 + hk], scalar2=0.0, op0=ALU.add, op1=ALU.max),
                         reads=[R_ps[b], R_cols], writes=[R_ft[fi]])
                    P.op("pool", lambda e, kk=kk, fi=fi: e.tensor_tensor(out=arena[:, kk, :], in0=ft[:, fi, :], in1=ft[:, fi, :], op=ALU.mult), reads=[R_ft[fi]], writes=[R_ar[kk]])
                for chh in range(2):
                    for kq in range(4):
                        wl, rw = load_L(OFF_W2 + ((hh * 2 + chh) * 4 + kq) * 2048)
                        for tc in range(4):
                            b = 4 + tc
                            for k4 in range(4):
                                k = 4 * kq + k4
                                rhs = wl.rearrange("p (k c) -> p k c", c=512)[:, k4, :]
                                mm(psum[b][:], arena[:, k, tc * 128:(tc + 1) * 128], rhs, k == 0, k == 15, [rw, R_ar[k]], [R_ps[b]])
                    for tc in range(4):
                        b = 4 + tc
                        P.op("dve", lambda e, tc=tc, b=b, chh=chh: e.tensor_tensor(out=xres[:, tc, chh * 512:(chh + 1) * 512], in0=psum[b][:], in1=xres[:, tc, chh * 512:(chh + 1) * 512], op=ALU.add),
                             reads=[R_xres[tc], R_ps[b]], writes=[R_xres[tc]])
                        if hh == 1:
                            P.op("dve", lambda e, tc=tc, chh=chh: e.bn_stats(out=stats[:, tc, chh, :], in_=xres[:, tc, chh * 512:(chh + 1) * 512]), reads=[R_xres[tc]], writes=[R_stats[tc]])
            if qi == 0:
                dump("h2", xres[:], [128, 4, 1024], F32, R_xres)
            layer_norm_all(stats_done=True)
            for tc in range(4):
                P.op("pool", lambda e, tc=tc: e.tensor_tensor(out=xres[:, tc, :], in0=xres[:, tc, :], in1=reps[:, 2, :], op=ALU.mult), reads=[R_xres[tc], R_reps], writes=[R_xres[tc]])
                P.op("pool", lambda e, tc=tc: e.tensor_tensor(out=xres[:, tc, :], in0=xres[:, tc, :], in1=reps[:, 3, :], op=ALU.add), reads=[R_xres[tc], R_reps], writes=[R_xres[tc]])
                od = P.op("pool", lambda e, tc=tc, orow=orow: e.dma_start(out=ys[orow + tc * 128:orow + (tc + 1) * 128, :], in_=xres[:, tc, :]),
                          reads=[R_xres[tc]], dma_key=f"out{tc}")
                out_dmas.append(od)

        if debug:
            nk = n_kv * T
            dump("kscr", kscr[:, :, 0:nk], [10, 128, nk], BF16, [R_scrK])
            dump("vscr", vscr[0:nk, :], [nk, VROW], BF16, [R_scrV])
        P.finalize(final_waits=out_dmas + dbg_dmas)
    return nc


_CACHE = {}


def kernel(x_prompt, x_sample, w_in, b_gate, w_branch_a, w_branch_b, w_out, rel_pos_bias,
           ln1_g, ln1_b, w_ff1, b_ff1, w_ff2, b_ff2, ln2_g, ln2_b):
    f = np.float32
    x_prompt = np.asarray(x_prompt, f)
    x_sample = np.asarray(x_sample, f)
    wall = _host_weights(np.asarray(w_in[0], f), np.asarray(w_branch_a[0], f), np.asarray(w_branch_b[0], f),
                         np.asarray(w_out[0], f), np.asarray(w_ff1[0], f), np.asarray(w_ff2[0], f))
    cols, rep, amask, tabs, misc = _host_consts(np.asarray(b_gate[0], f), np.asarray(rel_pos_bias[0], f),
                                                np.asarray(ln1_g[0], f), np.asarray(ln1_b[0], f), np.asarray(b_ff1[0], f),
                                                np.asarray(b_ff2[0], f), np.asarray(ln2_g[0], f), np.asarray(ln2_b[0], f))
    in_maps = []
    for c in range(8):
        b, q = c // 4, c % 4
        xs = np.zeros((NTOK, 1024), f)
        xs[:8192] = x_prompt[c]
        lo, hi = q * 2048 - 1024, q * 2048 + 3072
        s_lo, s_hi = max(lo, 0), min(hi, 8192)
        xs[8192 + (s_lo - lo):8192 + (s_hi - lo)] = x_sample[b, s_lo:s_hi]
        g = np.zeros((128, 16), f)
        g[:, 1] = 0.0 if q > 0 else NEG
        g[:, 2] = 0.0 if q == 0 else NEG
        g[:, 3] = 0.0 if q < 3 else NEG
        g[:, 4] = 0.0 if q == 3 else NEG
        g[:, 5] = LN_EPS
        g[0:64, 6] = NEG
        g[0:32, 7] = NEG
        g[96:128, 8] = NEG
        if q == 0:
            g[0:64, 9] = NEG
            g[0:32, 10] = NEG
        if q == 3:
            g[96:128, 11] = NEG
        rope = _rope_tables(q).reshape(NKV, 128, 2 * T)
        in_maps.append({"xs": xs, "wall": wall, "ccols": cols, "crep": rep, "camask": amask, "ctabs": tabs,
                        "cmisc": misc, "cgate": g, "crope": np.ascontiguousarray(rope)})
    if "nc" not in _CACHE:
        _CACHE["nc"] = build_program()
    res = run_bass_kernel_spmd(_CACHE["nc"], in_maps, core_ids=list(range(8)))
    y_prompt = np.zeros((8, 8192, 1024), f)
    y_sample = np.zeros((2, 8192, 1024), f)
    for c in range(8):
        ysc = res.results[c]["ys"]
        y_prompt[c] = ysc[:8192]
        b, q = c // 4, c % 4
        y_sample[b, q * 2048:(q + 1) * 2048] = ysc[8192:]
    return (y_prompt, y_sample)
```

```python
import contextlib
import numpy as np
import concourse.bass as bass
import concourse.mybir as mybir
from concourse.bass_utils import run_bass_kernel_spmd

F32 = mybir.dt.float32
BF16 = mybir.dt.bfloat16
AF = mybir.ActivationFunctionType
ALU = mybir.AluOpType

ENGS = ("pe", "act", "dve", "pool", "sp")
NEG = -30000.0
ALPHA = 2.0 ** 0.25
LN_EPS = 1e-5
T = 512
NKV = 24
NTOK = NKV * T
QT_A = 16
QT_B = 4


class Res:
    __slots__ = ("name", "w", "rs")

    def __init__(self, name):
        self.name = name
        self.w = None
        self.rs = []


class Instr:
    __slots__ = ("eng", "fn", "deps", "signal", "dma", "sem", "val")

    def __init__(self, eng, fn, dma):
        self.eng = eng
        self.fn = fn
        self.deps = []
        self.signal = False
        self.dma = dma
        self.sem = None
        self.val = None


class Prog:
    def __init__(self, nc):
        self.nc = nc
        self.lists = {e: [] for e in ENGS}

    def op(self, eng, fn, reads=(), writes=(), dma_key=None):
        ins = Instr(eng, fn, dma_key)
        deps = []
        for r in reads:
            if r.w is not None:
                deps.append(r.w)
        for w in writes:
            if w.w is not None:
                deps.append(w.w)
            deps.extend(w.rs)
        seen = set()
        for d in deps:
            if id(d) in seen:
                continue
            seen.add(id(d))
            if d.eng == eng and d.dma is None and ins.dma is None and eng == "pe":
                continue
            ins.deps.append(d)
        for r in reads:
            r.rs.append(ins)
        for w in writes:
            w.w = ins
            w.rs = []
        self.lists[eng].append(ins)
        return ins

    def finalize(self, final_waits=()):
        nc = self.nc
        for e in ENGS:
            for ins in self.lists[e]:
                for d in ins.deps:
                    d.signal = True
        with contextlib.ExitStack() as stack:
            eng_sem = {e: stack.enter_context(nc.semaphore(f"prog_{e}")) for e in ENGS}
            dma_h = {}
            for e in ENGS:
                for ins in self.lists[e]:
                    if ins.dma is not None and ins.dma not in dma_h:
                        dma_h[ins.dma] = [stack.enter_context(nc.semaphore(f"dma_{len(dma_h)}")), 0]
            for e in ENGS:
                cnt = 0
                for ins in self.lists[e]:
                    if ins.dma is not None:
                        h = dma_h[ins.dma]
                        h[1] += 16
                        ins.sem, ins.val = h[0], h[1]
                    elif ins.signal:
                        cnt += 1
                        ins.sem, ins.val = eng_sem[e], cnt
            block = stack.enter_context(nc.Block())
            engobj = {"pe": block.tensor, "act": block.scalar, "dve": block.vector,
                      "pool": block.gpsimd, "sp": block.sync}

            def make_body(e):
                def body(engine):
                    waited = {}
                    for ins in self.lists[e]:
                        for d in ins.deps:
                            key = id(d.sem)
                            if waited.get(key, 0) >= d.val:
                                continue
                            waited[key] = d.val
                            engine.wait_ge(d.sem, d.val)
                        bi = ins.fn(engine)
                        if ins.dma is not None:
                            bi.then_inc(ins.sem, 16)
                        elif ins.signal:
                            bi.then_inc(ins.sem, 1)
                    if e == "sp":
                        for d in final_waits:
                            if waited.get(id(d.sem), 0) >= d.val:
                                continue
                            waited[id(d.sem)] = d.val
                            engine.wait_ge(d.sem, d.val)
                return body

            for e in ENGS:
                engobj[e](make_body(e))


def _b_entries(ttype):
    out = {}
    for kcr in range(-2, 6):
        for qr in range(8):
            a = 2 * kcr
            r = qr
            if ttype == "int":
                wr = r - 4
                ok = lambda row: wr <= row < wr + 8
            elif ttype == "first":
                wr = max(r - 4, 0)
                ok = lambda row: row >= 0 and wr <= row < wr + 8
            else:
                wr = min(r - 4, 0)
                ok = lambda row: row < 8 and wr <= row < wr + 8
            v0, v1 = int(ok(a)), int(ok(a + 1))
            if v0 or v1:
                out[(kcr, qr)] = (r - a, v0, v1)
    return out


def _b_slots():
    inter = sorted(set(_b_entries("int").values()))
    rest = set()
    for tt in ("first", "last"):
        rest |= set(_b_entries(tt).values())
    rest = sorted(rest - set(inter))
    return inter + rest


B_SLOTS = _b_slots()
NBS = len(B_SLOTS)


def _b_runs(mode):
    items = []
    if mode in ("int", "first", "last"):
        ent = _b_entries(mode)
        for (kcr, qr), e in ent.items():
            items.append((kcr, qr, B_SLOTS.index(e), None))
    else:
        eI = _b_entries("int")
        eX = _b_entries("first" if mode == "q0" else "last")
        gI, gX = ("L", "F") if mode == "q0" else ("R", "Z")
        for kcr in range(-2, 6):
            for qr in range(8):
                a, b = eI.get((kcr, qr)), eX.get((kcr, qr))
                if a is not None and a == b:
                    items.append((kcr, qr, B_SLOTS.index(a), None))
                else:
                    if a is not None:
                        items.append((kcr, qr, B_SLOTS.index(a), gI))
                    if b is not None:
                        items.append((kcr, qr, B_SLOTS.index(b), gX))
    items.sort(key=lambda t: (t[0], str(t[3]), t[1]))
    runs = []
    for (kcr, qr, sl, g) in items:
        if runs:
            k0, q0, n0, s0, g0 = runs[-1]
            if k0 == kcr and g0 == g and q0 + n0 == qr and s0 + n0 == sl:
                runs[-1] = (k0, q0, n0 + 1, s0, g0)
                continue
        runs.append((kcr, qr, 1, sl, g))
    return runs


GATE_COL = {None: 0, "L": 1, "F": 2, "R": 3, "Z": 4, "S64": 6, "S32": 7, "S96": 8, "L64": 9, "L32": 10, "R96": 11}
NAM = 384 + 512 + 2048
VROW = 3 * 384 + 768

C_QA, C_KA, C_VA, C_QB, C_KB, C_VB, C_GA, C_GB = 0, 768, 1536, 2304, 2816, 3328, 3840, 4864


def _fm_chunk(w, col0):
    k = w.shape[0] // 128
    return w[:, col0:col0 + 128].reshape(k, 128, 128).transpose(1, 0, 2)


def _rhs_layout(w):
    k = w.shape[0] // 128
    return w.reshape(k, 128, w.shape[1]).transpose(1, 0, 2)


def _host_weights(w_in, w_a, w_b, w_o, w1, w2):
    f = np.float32
    chunks = []
    for c0 in [C_KA + 128 * i for i in range(6)] + [C_KB + 128 * i for i in range(4)]:
        chunks.append(_fm_chunk(w_in, c0))
    for c0 in [C_QA + 128 * i for i in range(6)] + [C_QB + 128 * i for i in range(4)]:
        chunks.append(_fm_chunk(w_in, c0))
    for c0 in [C_GA + 128 * i for i in range(8)] + [C_GB + 128 * i for i in range(8)]:
        chunks.append(_fm_chunk(w_in, c0))
    wfm = np.stack(chunks, axis=1).reshape(128, -1)
    wv = np.concatenate([_rhs_layout(w_in[:, C_VA + 256 * g:C_VA + 256 * (g + 1)]).reshape(128, -1) for g in range(3)]
                        + [_rhs_layout(w_in[:, C_VB:C_VB + 512]).reshape(128, -1)], axis=1)
    wa = _rhs_layout(w_a).reshape(128, -1)
    wb = _rhs_layout(w_b)
    wb = np.concatenate([wb[:, :, 0:512].reshape(128, -1), wb[:, :, 512:1024].reshape(128, -1)], axis=1)
    wo = _rhs_layout(w_o)
    pieces = []
    for ch in range(2):
        for kh in range(2):
            pieces.append(wo[:, 4 * kh:4 * kh + 4, 512 * ch:512 * ch + 512].reshape(128, -1))
    wo = np.concatenate(pieces, axis=1)
    w1c = np.stack([_fm_chunk(w1, 128 * i) for i in range(32)], axis=1).reshape(128, -1)
    w2l = _rhs_layout(w2)
    pieces = []
    for hh in range(2):
        for ch in range(2):
            for kq in range(4):
                k0 = 16 * hh + 4 * kq
                pieces.append(w2l[:, k0:k0 + 4, 512 * ch:512 * ch + 512].reshape(128, -1))
    w2p = np.concatenate(pieces, axis=1)
    wall = np.concatenate([wfm, wv, wa, wb, wo, w1c, w2p], axis=1).astype(f)
    return np.ascontiguousarray(wall)


OFF_WFM = 0
OFF_WV = 36 * 1024
OFF_WA = OFF_WV + 3 * 2048 + 4096
OFF_WB = OFF_WA + 2048
OFF_WO = OFF_WB + 4096
OFF_W1 = OFF_WO + 8192
OFF_W2 = OFF_W1 + 32 * 1024
W_TOT = OFF_W2 + 32 * 1024


def _host_consts(b_gate, rpb, ln1_g, ln1_b, b_ff1, b_ff2, ln2_g, ln2_b):
    f = np.float32
    cols = np.zeros((128, 64), f)
    cols[:, 0:8] = b_gate[0].reshape(8, 128).T
    cols[:, 8:16] = b_gate[1].reshape(8, 128).T
    cols[:, 16:48] = b_ff1.reshape(32, 128).T
    cols[:, 48:56] = ln1_g.reshape(8, 128).T
    cols[:, 56:64] = ln1_b.reshape(8, 128).T
    rep = np.stack([np.broadcast_to(v[None, :], (128, 1024)) for v in (ln1_g, ln1_b, b_ff2, ln2_g, ln2_b)], axis=1)
    rep = np.ascontiguousarray(rep.reshape(128, 5 * 1024)).astype(f)
    i = np.arange(128)[:, None]
    j = np.arange(128)[None, :]
    ms = []
    for d in (-1, 0, 1):
        ms.append((np.abs(128 * d + i - j) <= 64).astype(f))
    mq = np.arange(32)[None, :]
    ma3 = np.tile((i >= mq).astype(f), (1, 16))
    ms.append(ma3)
    t = np.arange(512)[None, :]
    for c in range(4):
        ms.append((((i % 16) == (t % 16)) & ((8 * c + i // 16) <= (t // 16))).astype(f))
    amask = np.concatenate(ms, axis=1)
    kcol = np.arange(64)[:, None]
    qc = np.arange(64)[None, :]
    wc = np.clip(qc - 8, 0, 48)
    okc = (kcol >= wc) & (kcol < wc + 16)
    dc = np.clip(kcol - qc + 15, 0, 30)
    tabs = np.full((128, 8, NBS * 64), NEG, f)
    for s, (delta, v0, v1) in enumerate(B_SLOTS):
        for krl, v in ((0, v0), (1, v1)):
            if not v:
                continue
            dr = krl - delta + 7
            assert 0 <= dr <= 14
            for h in range(8):
                vals = rpb[h, dr][dc]
                tabs[krl * 64:(krl + 1) * 64, h, s * 64:(s + 1) * 64] = np.where(okc, vals, NEG)
    tabs = np.ascontiguousarray(tabs.reshape(128, -1))
    ident = np.eye(128, dtype=f)
    perm = np.zeros((128, 128), f)
    for m in range(128):
        d = m % 64
        if d < 8:
            perm[m + 8, m] = 1.0
        elif d < 16:
            perm[m - 8, m] = 1.0
    misc = np.concatenate([ident, perm], axis=1)
    return cols, rep, amask, tabs, misc


def _rope_tables(q):
    f = np.float32
    inv = (np.float32(500000.0) ** (-np.arange(8, dtype=f) / np.float32(8))).astype(f)
    pos = np.zeros((NKV, T), f)
    for kt in range(16):
        pos[kt] = kt * T + np.arange(T)
    for u in range(8):
        pos[16 + u] = q * 2048 - 1024 + u * T + np.arange(T)
    ang = pos[:, None, :] * inv[None, :, None]
    cs, sn = np.cos(ang).astype(f), np.sin(ang).astype(f)
    tab = np.zeros((NKV, 128, 2, T), f)
    tab[:, :, 0, :] = 1.0
    for half in (0, 64):
        tab[:, half + 0:half + 8, 0, :] = cs
        tab[:, half + 8:half + 16, 0, :] = cs
        tab[:, half + 0:half + 8, 1, :] = -sn
        tab[:, half + 8:half + 16, 1, :] = sn
    return tab


def build_program(n_q_a=QT_A, n_q_b=QT_B, n_kv=NKV, debug=False):
    nc = bass.Bass("TRN2", target_bir_lowering=False)
    xs = nc.dram_tensor("xs", [NTOK, 1024], F32, kind="ExternalInput").ap()
    wall = nc.dram_tensor("wall", [128, W_TOT], F32, kind="ExternalInput").ap()
    ccols = nc.dram_tensor("ccols", [128, 64], F32, kind="ExternalInput").ap()
    crep = nc.dram_tensor("crep", [128, 5 * 1024], F32, kind="ExternalInput").ap()
    camask = nc.dram_tensor("camask", [128, NAM], F32, kind="ExternalInput").ap()
    ctabs = nc.dram_tensor("ctabs", [128, 8 * NBS * 64], F32, kind="ExternalInput").ap()
    cmisc = nc.dram_tensor("cmisc", [128, 256], F32, kind="ExternalInput").ap()
    cgate = nc.dram_tensor("cgate", [128, 16], F32, kind="ExternalInput").ap()
    crope = nc.dram_tensor("crope", [NKV, 128, 2 * T], F32, kind="ExternalInput").ap()
    ys = nc.dram_tensor("ys", [(QT_A + QT_B) * T, 1024], F32, kind="ExternalOutput").ap()
    wbf = nc.dram_tensor("wbf", [128, W_TOT], BF16, kind="Internal").ap()
    kscr = nc.dram_tensor("kscr", [10, 128, NTOK], BF16, kind="Internal").ap()
    vscr = nc.dram_tensor("vscr", [NTOK, VROW], BF16, kind="Internal").ap()

    P = Prog(nc)
    with contextlib.ExitStack() as st:
        def sb(name, shape, dt):
            return st.enter_context(nc.sbuf_tensor(name, shape, dt))

        NS, NL = 4, 3
        wS = sb("wS", [128, NS, 1024], BF16)
        wL = sb("wL", [128, NL, 2048], BF16)
        R_wS = [Res(f"wS{i}") for i in range(NS)]
        R_wL = [Res(f"wL{i}") for i in range(NL)]
        xst = sb("xst", [128, 2, 1024], F32)
        R_xst = [Res("xst0"), Res("xst1")]
        xres = sb("xres", [128, 4, 1024], F32)
        R_xres = [Res(f"xres{i}") for i in range(4)]
        arena = sb("arena", [128, 18, T], BF16)
        R_ar = [Res(f"ar{i}") for i in range(18)]
        mrg = sb("mrg", [128, 8, T], BF16)
        R_mrg = [Res(f"mrg{i}") for i in range(8)]
        OA = sb("OA", [128, 2, T], BF16)
        OB = sb("OB", [128, 4, T], BF16)
        R_OA = [Res("OA0"), Res("OA1")]
        R_OB = [Res(f"OB{i}") for i in range(4)]
        acc = sb("acc", [128, T], F32)
        R_acc = Res("acc")
        rcp = sb("rcp", [128, T], F32)
        R_rcp = [Res("rcp0"), Res("rcp1")]
        K1w = sb("K1w", [128, 2, 768], BF16)
        K2w = sb("K2w", [128, 2, 1536], BF16)
        K3w = sb("K3w", [128, 2, 2560], BF16)
        KBw = sb("KBw", [128, 4, 1024], BF16)
        R_K = [Res("K1w"), Res("K2w"), Res("K3w"), Res("KBw")]
        V1w = sb("V1w", [128, 6 * 384], BF16)
        Vbig = sb("Vbig", [128, 28 * 384], BF16)
        V2w = Vbig[:, 0:12 * 384]
        V3w = Vbig[:, 12 * 384:28 * 384]
        V3b = sb("V3b", [128, 4 * 384], BF16)
        VBw = sb("VBw", [128, 8 * 768], BF16)
        R_V = [Res("V1w"), Res("V2w"), Res("V3w"), Res("VBw"), Res("V3b")]
        NP = 5
        NSB = 5
        OBANKS = [5, 6, 7]
        DUMMY_BANK = 4
        N_WARM = 0
        Pt = sb("Pt", [128, NP, T], BF16)
        R_Pt = [Res(f"Pt{i}") for i in range(NP)]
        ft = sb("ft", [128, 3, T], F32)
        R_ft = [Res(f"ft{i}") for i in range(3)]
        sg = sb("sg", [128, 2, T], BF16)
        R_sg = [Res("sg0"), Res("sg1")]
        amask = sb("amask", [128, NAM], BF16)
        amask4 = sb("amask4", [128, 3, T], BF16)
        amaskr = sb("amaskr", [128, 384], BF16)
        R_amask = Res("amask")
        R_amask4 = Res("amask4")
        tabB = sb("tabB", [128, 8, NBS * 64], BF16)
        R_tabB = Res("tabB")
        reps = sb("reps", [128, 4, 1024], F32)
        R_reps = Res("reps")
        rope = sb("rope", [128, 2, T], F32)
        R_rope = Res("rope")
        cols = sb("cols", [128, 64], F32)
        R_cols = Res("cols")
        gate = sb("gate", [128, 16], F32)
        R_gate = Res("gate")
        ident = sb("ident", [128, 128], F32)
        permb = sb("permb", [128, 128], BF16)
        R_misc = Res("misc")
        stats = sb("stats", [128, 4, 2, 6], F32)
        mv = sb("mv", [128, 4, 2], F32)
        rstd = sb("rstd", [128, 4], F32)
        R_stats = [Res(f"st{i}") for i in range(4)]
        R_rstd = Res("rstd")
        kst = arena[:, 8:18, :]
        R_kst = R_ar[8:18]
        vst = Vbig[:, 0:4 * VROW].rearrange("p (c n) -> p c n", n=VROW)
        R_vst = [Res(f"vst{i}") for i in range(4)]
        psum = [st.enter_context(nc.psum_tensor(f"ps{i}", [128, T], F32)) for i in range(8)]
        R_ps = [Res(f"ps{i}") for i in range(8)]
        R_wbf = Res("wbf")
        R_scrK = Res("scrK")
        R_scrV = Res("scrV")
        R_kscr = [Res(f"kscr{kt}") for kt in range(NKV)]
        R_vscr = [Res(f"vscr{kt}") for kt in range(NKV)]

        cnt = {"dma": 0, "S": 0, "L": 0, "P": 0, "ev": 0, "mul": 0, "ob": 0}
        dbg_dmas = []

        def dump(name, ap, shape, dt, reads):
            if not debug:
                return
            d = nc.dram_tensor("dbg_" + name, list(shape), dt, kind="ExternalOutput").ap()
            dbg_dmas.append(P.op("sp", lambda e: e.dma_start(out=d, in_=ap), reads=reads, dma_key="dbg_" + name))

        def dkey(prefix):
            cnt["dma"] += 1
            return f"{prefix}{cnt['dma'] % 6}"

        for a0 in range(0, W_TOT, 16384):
            a1 = min(a0 + 16384, W_TOT)
            P.op("pool", lambda e, a0=a0, a1=a1: e.dma_start(
                out=wbf[:, a0:a1].rearrange("p (n f) -> p n f", f=2048),
                in_=wall[:, a0:a1].rearrange("p (n f) -> p n f", f=2048)),
                writes=[R_wbf], dma_key="wcast")
        P.op("sp", lambda e: e.dma_start(out=cols[:], in_=ccols), writes=[R_cols], dma_key="c0")
        P.op("sp", lambda e: e.dma_start(out=gate[:], in_=cgate), writes=[R_gate], dma_key="c1")
        P.op("sp", lambda e: e.dma_start(out=ident[:], in_=cmisc[:, 0:128]), writes=[R_misc], dma_key="c2")
        P.op("pool", lambda e: e.dma_start(out=permb[:], in_=cmisc[:, 128:256]), writes=[R_misc], dma_key="c3")
        P.op("pool", lambda e: e.dma_start(out=amask[:].rearrange("p (a b) -> p a b", b=128), in_=camask.rearrange("p (a b) -> p a b", b=128)), writes=[R_amask], dma_key="c4")
        for d_ in range(3):
            for rep_ in range(4):
                P.op("pool", lambda e, d_=d_, rep_=rep_: e.tensor_copy(out=amask4[:, d_, rep_ * 128:(rep_ + 1) * 128], in_=amask[:, d_ * 128:(d_ + 1) * 128]),
                     reads=[R_amask], writes=[R_amask4])
        for j_ in range(3):
            P.op("pool", lambda e, j_=j_: e.tensor_copy(out=amaskr[:, j_ * 128:(j_ + 1) * 128], in_=amask[:, (2 - j_) * 128:(3 - j_) * 128]),
                 reads=[R_amask], writes=[R_amask4])
        P.op("sp", lambda e: e.dma_start(out=reps[:, 0, :], in_=crep[:, 0:1024]), writes=[R_reps], dma_key="c5_0")
        P.op("sp", lambda e: e.dma_start(out=reps[:, 1, :], in_=crep[:, 1024:2048]), writes=[R_reps], dma_key="c5_1")
        P.op("sp", lambda e: e.dma_start(out=xres[:, 0, :], in_=crep[:, 2048:3072]), writes=[R_xres[0]], dma_key="c5_2")
        P.op("sp", lambda e: e.dma_start(out=reps[:, 2:4, :].rearrange("p a b -> p (a b)"), in_=crep[:, 3072:5120]), writes=[R_reps], dma_key="c5_3")
        P.op("dve", lambda e: e.tensor_scalar(out=reps[:, 0, :], in0=reps[:, 0, :], scalar1=ALPHA, scalar2=None, op0=ALU.mult), reads=[R_reps], writes=[R_reps])
        P.op("dve", lambda e: e.scalar_tensor_tensor(out=reps[:, 1, :], in0=reps[:, 1, :], scalar=ALPHA, in1=xres[:, 0, :], op0=ALU.mult, op1=ALU.add),
             reads=[R_reps, R_xres[0]], writes=[R_reps])
        for h in range(8):
            w = NBS * 64
            r = R_xres[1 + (h % 2)]
            P.op("sp", lambda e, h=h, w=w: e.dma_start(out=xres[:, 1 + (h % 2), 0:w], in_=ctabs[:, h * w:(h + 1) * w]), writes=[r], dma_key=f"tb{h % 2}")
            P.op("act", lambda e, h=h, w=w: e.activation(out=tabB[:, h, :], in_=xres[:, 1 + (h % 2), 0:w], func=AF.Exp), reads=[r], writes=[R_tabB])
        for (w_, r_) in ((K1w, R_K[0]), (K2w, R_K[1]), (K3w, R_K[2]), (KBw, R_K[3])):
            P.op("pool", lambda e, w_=w_: e.memset(w_[:].rearrange("p a b -> p (a b)"), 0.0), writes=[r_])
        for (w_, r_) in ((V1w, [R_V[0]]), (Vbig, [R_V[1], R_V[2]]), (VBw, [R_V[3]]), (V3b, [R_V[4]])):
            P.op("pool", lambda e, w_=w_: e.memset(w_[:], 0.0), writes=r_)
        for sl in range(4):
            P.op("pool", lambda e, sl=sl: e.memset(vst[:, sl, :].rearrange("p (a b c) -> p a b c", b=3, c=64)[:, :, 1, :], 1.0),
                 reads=[R_V[1], R_V[2]], writes=[R_vst[sl]])

        def load_S(off):
            i = cnt["S"] % NS
            cnt["S"] += 1
            P.op("sp", lambda e, i=i, off=off: e.dma_start(out=wS[:, i, :], in_=wbf[:, off:off + 1024]),
                 reads=[R_wbf], writes=[R_wS[i]], dma_key=f"wS{i}")
            return wS[:, i, :].rearrange("p (k c) -> p k c", c=128), R_wS[i]

        def load_L(off):
            i = cnt["L"] % NL
            cnt["L"] += 1
            P.op("sp", lambda e, i=i, off=off: e.dma_start(out=wL[:, i, :], in_=wbf[:, off:off + 2048]),
                 reads=[R_wbf], writes=[R_wL[i]], dma_key=f"wL{i}")
            return wL[:, i, :], R_wL[i]

        def mm(out, lhsT, rhs, start, stop, reads, writes, skip=False):
            P.op("pe", lambda e: e.matmul(out, lhsT=lhsT, rhs=rhs, start=start, stop=stop, skip_group_check=skip),
                 reads=reads, writes=writes)

        def evac_copy(out, in_, reads, writes):
            k = cnt["ev"] % 2
            cnt["ev"] += 1
            if k == 0:
                P.op("act", lambda e: e.activation(out=out, in_=in_, func=AF.Copy), reads=reads, writes=writes)
            else:
                P.op("dve", lambda e: e.tensor_copy(out=out, in_=in_), reads=reads, writes=writes)

        qt_list = [("A", i) for i in range(n_q_a)] + [("B", u) for u in range(n_q_b)]
        xseq = list(range(n_kv)) + [(i if j_ == "A" else 18 + i) for (j_, i) in qt_list]
        xstate = {"i": 0}

        mrg32 = mrg[:].rearrange("p a b -> p (a b)").bitcast(F32).rearrange("p (s n) -> p s n", n=1024)
        ring1 = [(xres[:, i, :], [R_xres[i]]) for i in range(4)] + [(xst[:, i, :], [R_xst[i]]) for i in range(2)]
        ring2 = [(xst[:, 0, :], [R_xst[0]]), (xst[:, 1, :], [R_xst[1]]), (mrg32[:, 0, :], list(R_mrg[0:4])), (mrg32[:, 1, :], list(R_mrg[4:8]))]
        n1 = 4 * n_kv

        def xslot(g):
            if g < n1:
                return ring1[g % 6], g % 6
            k_ = (g - n1) % 4
            return ring2[k_], (4 + k_ if k_ < 2 else 4 + k_)
        xstate["req"] = 0
        xocc = {}

        def xrequest_upto(g, own=False):
            total = 4 * len(xseq)
            while xstate["req"] < total:
                r = xstate["req"]
                (ap_, res_), sid = xslot(r)
                prev = xocc.get(sid)
                if prev is not None and prev >= g:
                    break
                if sid >= 6 and not (own and (r // 4) == (g // 4)):
                    break
                xocc[sid] = r
                kt_, tc = xseq[r // 4], r % 4
                P.op("sp", lambda e, ap_=ap_, kt_=kt_, tc=tc: e.dma_start(out=ap_, in_=xs[kt_ * T + tc * 128: kt_ * T + (tc + 1) * 128, :]),
                     writes=res_, dma_key="x" + res_[0].name)
                xstate["req"] += 1

        def make_xT(kt):
            idx = xstate["i"]
            xstate["i"] += 1
            assert xseq[idx] == kt
            xrequest_upto(4 * idx, own=True)
            for rnd in range(2):
                for tc in range(4):
                    g = 4 * idx + tc
                    (ap_, res_), sid = xslot(g)
                    for f4 in range(4):
                        fc = 4 * rnd + f4
                        P.op("pe", lambda e, tc=tc, ap_=ap_, fc=fc, f4=f4: e.transpose(out=psum[f4][:, tc * 128:(tc + 1) * 128], in_=ap_[:, fc * 128:(fc + 1) * 128], identity=ident[:]),
                             reads=res_ + [R_misc], writes=[R_ps[f4]])
                for f4 in range(4):
                    fc = 4 * rnd + f4
                    if idx >= n_kv:
                        P.op("act", lambda e, fc=fc, f4=f4: e.activation(out=arena[:, fc, :], in_=psum[f4][:], func=AF.Copy), reads=[R_ps[f4]], writes=[R_ar[fc]])
                    else:
                        evac_copy(arena[:, fc, :], psum[f4][:], [R_ps[f4]], [R_ar[fc]])
            xrequest_upto(4 * (idx + 1))

        def rope_chunk(src_bank, dst_ap, dst_res, bankB, add_eng="pool"):
            i = cnt["P"] % NP
            cnt["P"] += 1
            P.op("act", lambda e: e.activation(out=Pt[:, i, :], in_=psum[src_bank][:], func=AF.Copy), reads=[R_ps[src_bank]], writes=[R_Pt[i]])

            def fin():
                mm(psum[bankB][:], permb[:], Pt[:, i, :], True, True, [R_Pt[i], R_misc], [R_ps[bankB]])
                P.op("dve", lambda e: e.tensor_tensor(out=ft[:, 0, :], in0=psum[bankB][:], in1=rope[:, 1, :], op=ALU.mult), reads=[R_ps[bankB], R_rope], writes=[R_ft[0]])
                P.op("dve", lambda e: e.tensor_tensor(out=ft[:, 1, :], in0=psum[src_bank][:], in1=rope[:, 0, :], op=ALU.mult), reads=[R_ps[src_bank], R_rope], writes=[R_ft[1]])
                P.op(add_eng, lambda e: e.tensor_tensor(out=dst_ap, in0=ft[:, 0, :], in1=ft[:, 1, :], op=ALU.add), reads=[R_ft[0], R_ft[1]], writes=[dst_res])
            return fin

        def proj_chunks(w_base, dst_fn, resident=None):
            pend = None
            for c in range(10):
                if resident is not None:
                    wv_, rw = resident[c]
                else:
                    wv_, rw = load_S(OFF_WFM + (w_base + c) * 1024)
                b = c % 4
                for k in range(8):
                    mm(psum[b][:], wv_[:, k, :], arena[:, k, :], k == 0, k == 7, [rw, R_ar[k]], [R_ps[b]])
                if pend is not None:
                    pend()
                    pend = None
                dst_ap, dst_res = dst_fn(c)
                if c < 6:
                    pend = rope_chunk(b, dst_ap, dst_res, 4 + (c % 4), add_eng=("pool" if resident is not None else "dve"))
                else:
                    evac_copy(dst_ap, psum[b][:], [R_ps[b]], [dst_res])
            if pend is not None:
                pend()

        def load_rope(kt):
            P.op("sp", lambda e: e.dma_start(out=rope[:].rearrange("p a b -> p (a b)"), in_=crope[kt]), writes=[R_rope], dma_key="rope")

        k3f = K3w[:].rearrange("p a b -> p (a b)")
        kbf = KBw[:].rearrange("p a b -> p (a b)")
        k2f = K2w[:].rearrange("p a b -> p (a b)")
        mrgf = mrg[:].rearrange("p a b -> p (a b)")
        kres_slots = [(k3f[:, i * 1024:(i + 1) * 1024], R_K[2]) for i in range(5)] + \
                     [(kbf[:, i * 1024:(i + 1) * 1024], R_K[3]) for i in range(4)] + [(k2f[:, 0:1024], R_K[1])]
        wk_res = []
        for c in range(10):
            ap_, r_ = kres_slots[c]
            P.op("sp", lambda e, ap_=ap_, c=c: e.dma_start(out=ap_, in_=wbf[:, OFF_WFM + c * 1024:OFF_WFM + (c + 1) * 1024]),
                 reads=[R_wbf], writes=[r_], dma_key=f"p1k{c}")
            wk_res.append((ap_.rearrange("p (k c) -> p k c", c=128), r_))
        vres_slots = [(VBw[:, 0:2048], [R_V[3]]), (VBw[:, 2048:4096], [R_V[3]]), (VBw[:, 4096:6144], [R_V[3]]),
                      (V1w[:, 0:2048], [R_V[0]]), (mrgf[:, 0:2048], list(R_mrg[0:4]))]
        wv_res = []
        for pc in range(5):
            ap_, r_ = vres_slots[pc]
            P.op("sp", lambda e, ap_=ap_, pc=pc: e.dma_start(out=ap_, in_=wbf[:, OFF_WV + pc * 2048:OFF_WV + (pc + 1) * 2048]),
                 reads=[R_wbf], writes=r_, dma_key=f"p1v{pc}")
            wv_res.append((ap_, r_))
        for kt in range(n_kv):
            make_xT(kt)
            load_rope(kt)
            proj_chunks(0, lambda c: (kst[:, c, :], R_kst[c]), resident=wk_res)
            P.op("pool", lambda e, kt=kt: e.dma_start(out=kscr[:, :, kt * T:(kt + 1) * T].rearrange("c p t -> p c t"), in_=kst),
                 reads=list(R_kst), writes=[R_scrK], dma_key="scrK")
            for g in range(4):
                ncol = 512 if g == 3 else 256
                woff = OFF_WV + (g * 2048 if g < 3 else 3 * 2048)
                pieces = [wv_res[g]] if g < 3 else [wv_res[3], wv_res[4]]
                for ch in range(4):
                    b = 4 + (ch % 2) + 2 * (g % 2)
                    for k in range(8):
                        lt = arena[:, k, ch * 128:(ch + 1) * 128]
                        if g < 3:
                            wl, rw = pieces[0]
                            rhs = wl.rearrange("p (k c) -> p k c", c=256)[:, k, :]
                        else:
                            wl, rw = pieces[k // 4]
                            rhs = wl.rearrange("p (k c) -> p k c", c=512)[:, k % 4, :]
                        mm(psum[b][:, 0:ncol], lt, rhs, k == 0, k == 7, list(rw) + [R_ar[k]], [R_ps[b]])
                    npair = ncol // 128
                    c0 = g * 384
                    sl = ch
                    dst = vst[:, sl, c0:c0 + npair * 192].rearrange("p (a b c) -> p a b c", b=3, c=64)[:, :, 0:3:2, :]
                    src = psum[b][:, 0:ncol].rearrange("p (a b c) -> p a b c", b=2, c=64)
                    evac_copy(dst, src, [R_ps[b]], [R_vst[sl]])
            P.op("pool", lambda e, kt=kt: e.dma_start(out=vscr[kt * T:(kt + 1) * T, :].rearrange("(c p) n -> p c n", p=128), in_=vst),
                 reads=list(R_vst), writes=[R_scrV], dma_key="scrV")

        out_dmas = []
        qtiles = [("A", i) for i in range(n_q_a)] + [("B", u) for u in range(n_q_b)]

        def tile_params(job, ti):
            if job == "A":
                return dict(kt=ti, kt_lo=0, kt_hi=16, bmode="first" if ti == 0 else ("last" if ti == 15 else "int"), orow=ti * T,
                            kgate=lambda ktile: None)
            return dict(kt=18 + ti, kt_lo=16, kt_hi=24, bmode="q0" if ti == 0 else ("q3" if ti == 3 else "int"), orow=(QT_A + ti) * T,
                        kgate=lambda ktile: "L" if ktile < 18 else ("R" if ktile > 21 else None))

        def window_loads(job, ti):
            tp_ = tile_params(job, ti)
            kt, kt_lo, kt_hi = tp_["kt"], tp_["kt_lo"], tp_["kt_hi"]
            tok0 = kt * T
            lo_tok, hi_tok = kt_lo * T, kt_hi * T
            thunks = []

            def kload(win, rw, chs, t_lo, t_hi):
                a, b_ = max(t_lo, lo_tok), min(t_hi, hi_tok)
                src = kscr[chs[0]:chs[-1] + 1, :, a:b_].rearrange("c p t -> p c t")
                thunks.append(lambda WQ: P.op(WQ, lambda e: e.dma_start(out=win[:, :, a - t_lo:b_ - t_lo], in_=src),
                                              reads=[R_scrK, R_scrV], writes=[rw], dma_key="w" + rw.name))
            kload(K1w, R_K[0], [0, 1], tok0 - 128, tok0 + 640)
            kload(K2w, R_K[1], [2, 3], tok0 - 512, tok0 + 1024)
            kload(K3w, R_K[2], [4, 5], tok0 - 1024, tok0 + 1536)
            kload(KBw, R_K[3], [6, 7, 8, 9], tok0 - 256, tok0 + 768)

            def vgather(win, rv, wcol0, col0, ncol, row0, dims, p_lo=0, p_hi=128, pstride=1):
                if p_hi <= p_lo:
                    return
                nch = 1
                for (_, c_) in dims:
                    nch *= c_
                src = bass.AP(tensor=vscr.tensor, offset=(row0 + pstride * p_lo) * VROW + col0,
                              ap=[[pstride * VROW, p_hi - p_lo]] + [[rs * VROW, c_] for (rs, c_) in dims] + [[1, ncol]])
                dst = win[p_lo:p_hi, wcol0:wcol0 + nch * ncol]
                if len(dims) == 1:
                    dst = dst.rearrange("p (a n) -> p a n", n=ncol)
                elif len(dims) == 2:
                    dst = dst.rearrange("p (a b n) -> p a b n", b=dims[1][1], n=ncol)
                thunks.append(lambda WQ: P.op(WQ, lambda e: e.dma_start(out=dst, in_=src), reads=[R_scrK, R_scrV], writes=[rv], dma_key="w" + rv.name))

            def tile_ok(ktl):
                return kt_lo <= ktl < kt_hi
            ccs = [4 * kt - 1 + s_ for s_ in range(6) if tile_ok((4 * kt - 1 + s_) // 4)]
            vgather(V1w, R_V[0], (ccs[0] - (4 * kt - 1)) * 384, 0, 384, ccs[0] * 128, [(128, len(ccs))])
            for d in range(3):
                if tile_ok(kt - 1 + d):
                    vgather(V2w, R_V[1], 4 * d * 384, 384, 384, (kt - 1 + d) * T, [(1, 4)], pstride=4)
            base3 = tok0 - 1024
            i_lo = max(0, (lo_tok - base3) // 16)
            i_hi = min(128, (hi_tok - base3) // 16)
            for r0 in range(0, 16, 4):
                vgather(V3w, R_V[2], r0 * 384, 768, 384, base3 + r0, [(1, 4)], i_lo, i_hi, pstride=16)
            if tile_ok(kt + 2):
                vgather(V3b, R_V[4], 0, 768, 384, (kt + 2) * T, [(128, 4)])
            ccs = [4 * kt - 2 + s_ for s_ in range(8) if tile_ok((4 * kt - 2 + s_) // 4)]
            half_n = (len(ccs) + 1) // 2
            for part in (ccs[:half_n], ccs[half_n:]):
                if part:
                    vgather(VBw, R_V[3], (part[0] - (4 * kt - 2)) * 768, 1152, 768, part[0] * 128, [(128, len(part))])
            return thunks

        if qtiles:
            for th in window_loads(*qtiles[0]):
                th("sp")
        for qi, (job, ti) in enumerate(qtiles):
            tp_ = tile_params(job, ti)
            kt, kt_lo, kt_hi, bmode, orow, kgate = tp_["kt"], tp_["kt_lo"], tp_["kt_hi"], tp_["bmode"], tp_["orow"], tp_["kgate"]
            tok0 = kt * T

            def tile_ok(ktl, kt_lo=kt_lo, kt_hi=kt_hi):
                return kt_lo <= ktl < kt_hi

            make_xT(kt)
            load_rope(kt)
            proj_chunks(10, lambda c: (arena[:, 8 + c, :], R_ar[8 + c]))

            if qi == 0:
                dump("xT", arena[:, 0:8, :], [128, 8, T], BF16, R_ar[0:8])
                dump("Q", arena[:, 8:18, :], [128, 10, T], BF16, R_ar[8:18])
                dump("K3w", K3w[:], [128, 2, 2560], BF16, [R_K[2]])
                dump("KBw", KBw[:], [128, 4, 1024], BF16, [R_K[3]])
                dump("V1w", V1w[:], [128, 6 * 384], BF16, [R_V[0]])
                dump("V3w", V3w, [128, 16 * 384], BF16, [R_V[2]])
                dump("VBw", VBw[:], [128, 8 * 768], BF16, [R_V[3]])
                dump("tabB", tabB[:], [128, 8, NBS * 64], BF16, [R_tabB])
            DEPTH = 4
            pipe = []

            def vaug(win, chunk_off, pair, half):
                o = chunk_off + pair * 192 + 64 * half
                return win[:, o:o + 128]

            post_sched = []
            POST_LAG = 2

            def run_due(force=False):
                keep = []
                for item in post_sched:
                    if force or item[0] <= 0:
                        item[1]()
                    else:
                        keep.append(item)
                post_sched[:] = keep

            def pop_one():
                fn = pipe.pop(0)
                fn()
                for item in post_sched:
                    item[0] -= 1
                run_due()

            def submit(subs, mask_ops, obank, first_flag, post=None, sres=(), vres=()):
                i = cnt["P"] % NP
                cnt["P"] += 1
                sbk = i % NSB
                for j, (kap, qap, c0, n, g, vap, oc0) in enumerate(subs):
                    mm(psum[sbk][:, c0:c0 + n], kap, qap, j == 0, False, list(sres), [R_ps[sbk]], skip=True)
                j = 0
                while j < len(subs):
                    j2 = j
                    while j2 + 1 < len(subs) and subs[j2 + 1][4] == subs[j][4] and subs[j2 + 1][2] == subs[j2][2] + subs[j2][3]:
                        j2 += 1
                    c0 = subs[j][2]
                    c1 = subs[j2][2] + subs[j2][3]
                    gc = GATE_COL[subs[j][4]]
                    P.op("act", lambda e, i=i, sbk=sbk, c0=c0, c1=c1, gc=gc: e.activation(out=Pt[:, i, c0:c1], in_=psum[sbk][:, c0:c1], func=AF.Exp, scale=0.125, bias=gate[:, gc:gc + 1]),
                         reads=[R_ps[sbk], R_gate], writes=[R_Pt[i]])
                    j = j2 + 1
                for (c0, n, map_, mres) in mask_ops:
                    eng = "dve"
                    cnt["mul"] += 1
                    P.op(eng, lambda e, i=i, c0=c0, n=n, map_=map_: e.tensor_tensor(out=Pt[:, i, c0:c0 + n], in0=Pt[:, i, c0:c0 + n], in1=map_, op=ALU.mult),
                         reads=[R_Pt[i], mres], writes=[R_Pt[i]])

                for _ in range(N_WARM):
                    P.op("pe", lambda e: e.matmul(psum[DUMMY_BANK][:], lhsT=permb[:], rhs=amask4[:, 0, :], start=True, stop=True), reads=[], writes=[R_ps[DUMMY_BANK]])

                def pv(i=i, subs=subs, first_flag=first_flag, post=post):
                    ff = first_flag
                    for (kap, qap, c0, n, g, vap, oc0) in subs:
                        mm(psum[obank][:, oc0:oc0 + n], vap, Pt[:, i, c0:c0 + n], ff, False, list(vres) + [R_Pt[i]], [R_ps[obank]], skip=True)
                        ff = False
                    if post is not None:
                        for k_, st_ in enumerate(post()):
                            post_sched.append([POST_LAG + k_, st_])
                pipe.append(pv)
                while len(pipe) > DEPTH:
                    pop_one()

            def next_obank():
                b_ = OBANKS[cnt["ob"] % len(OBANKS)]
                cnt["ob"] += 1
                return b_

            if job == "A":
                g3gate = {0: "S64", 1: "S32", 15: "S96"}.get(ti)
            else:
                g3gate = {0: "L64", 1: "L32", 3: "R96"}.get(ti)

            for hs in range(4):
                half = hs % 2
                pr = slice(64 * half, 64 * half + 64)
                po = slice(64 * (1 - half), 64 * (1 - half) + 64)
                qch = hs // 2
                acc_ap, acc_res = (acc[:], R_acc) if hs % 2 == 0 else (ft[:, 2, :], R_ft[2])
                ob1 = next_obank()
                blocks = []
                for s_ in range(6):
                    cc = 4 * kt - 1 + s_
                    ktl = cc // 4
                    if not tile_ok(ktl):
                        continue
                    q_lo, q_hi = max(0, s_ - 2), min(3, s_)
                    n = 128 * (q_hi - q_lo + 1)
                    j_lo = q_lo - s_ + 2
                    koff = s_ * 128
                    subs = [(K1w[pr, qch, koff:koff + 128], arena[pr, 8 + qch, q_lo * 128:q_lo * 128 + n], 0, n, kgate(ktl),
                             vaug(V1w, s_ * 384, qch, half), q_lo * 128)]
                    blocks.append((subs, [(0, n, amaskr[:, j_lo * 128:j_lo * 128 + n], R_amask4)], [R_K[0], R_ar[8 + qch]], [R_V[0]]))
                if tile_ok(kt + 2):
                    for c_ in range(4):
                        koff = 2048 + c_ * 128
                        q0_ = 128 * c_
                        subs = [(K3w[pr, qch, koff:koff + 128], arena[pr, 12 + qch, q0_:T], 0, T - q0_, kgate(kt + 2), vaug(V3b, c_ * 384, qch, half), q0_)]
                        blocks.append((subs, [(0, T - q0_, amask[:, 896 + 512 * c_ + q0_:896 + 512 * (c_ + 1)], R_amask)], [R_K[2], R_ar[12 + qch]], [R_V[4]]))

                def post1(ob1=ob1, acc_ap=acc_ap, acc_res=acc_res):
                    return [lambda: P.op("act", lambda e: e.activation(out=acc_ap, in_=psum[ob1][:], func=AF.Copy), reads=[R_ps[ob1]], writes=[acc_res])]
                for bi, (subs, mops, sres, vres) in enumerate(blocks):
                    submit(subs, mops, ob1, bi == 0, post1 if bi == len(blocks) - 1 else None, sres, vres)
                ob2 = next_obank()
                blocks = []
                for d in (-1, 0, 1):
                    ktl = kt + d
                    if not tile_ok(ktl):
                        continue
                    kb0 = (ktl * T) - (tok0 - 512)
                    subs = []
                    for r4 in range(4):
                        sl = 4 * (d + 1) + r4
                        subs.append((K2w[pr, qch, kb0 + r4:kb0 + T:4], arena[pr, 10 + qch, r4:T:4], r4 * 128, 128, kgate(ktl),
                                     vaug(V2w, sl * 384, qch, half), r4 * 128))
                    blocks.append((subs, [(0, T, amask4[:, d + 1, :], R_amask4)], [R_K[1], R_ar[10 + qch]], [R_V[1]]))

                def post2(ob2=ob2, acc_ap=acc_ap, acc_res=acc_res):
                    return [lambda: P.op("dve", lambda e: e.tensor_tensor(out=acc_ap.rearrange("p (m r) -> p m r", r=4), in0=acc_ap.rearrange("p (m r) -> p m r", r=4),
                                                                          in1=psum[ob2][:].rearrange("p (r m) -> p m r", r=4), op=ALU.add),
                                         reads=[R_ps[ob2], acc_res], writes=[acc_res])]
                for bi, (subs, mops, sres, vres) in enumerate(blocks):
                    submit(subs, mops, ob2, bi == 0, post2 if bi == len(blocks) - 1 else None, sres, vres)
                ob3 = next_obank()
                subs = []
                for r in range(16):
                    subs.append((K3w[pr, qch, r:2048:16], arena[pr, 12 + qch, r:T:16], r * 32, 32, g3gate, vaug(V3w, r * 384, qch, half), r * 32))

                def post3(ob3=ob3, acc_ap=acc_ap, acc_res=acc_res, pr=pr, po=po, qch=qch, half=half):
                    def st0():
                        P.op("dve", lambda e: e.tensor_tensor(out=acc_ap.rearrange("p (m r) -> p m r", r=16), in0=acc_ap.rearrange("p (m r) -> p m r", r=16),
                                                              in1=psum[ob3][:].rearrange("p (r m) -> p m r", r=16), op=ALU.add),
                             reads=[R_ps[ob3], acc_res], writes=[acc_res])

                    def st1():
                        P.op("act", lambda e: e.activation(out=rcp[pr, :], in_=acc_ap[po, :], func=AF.Ln), reads=[acc_res], writes=[R_rcp[half]])
                        P.op("act", lambda e: e.activation(out=rcp[pr, :], in_=rcp[pr, :], func=AF.Exp, scale=-1.0), reads=[R_rcp[half]], writes=[R_rcp[half]])

                    def st2():
                        P.op("dve", lambda e: e.tensor_tensor(out=OA[pr, qch, :], in0=acc_ap[pr, :], in1=rcp[pr, :], op=ALU.mult),
                             reads=[acc_res, R_rcp[half]], writes=[R_OA[qch]])
                    return [st0, st1, st2]
                submit(subs, [(0, T, amask[:, 384:896], R_amask)], ob3, True, post3, [R_K[2], R_ar[12 + qch]], [R_V[2]])

            runs = _b_runs(bmode)
            for h in range(8):
                half = h % 2
                pr = slice(64 * half, 64 * half + 64)
                po = slice(64 * (1 - half), 64 * (1 - half) + 64)
                kch = h // 2
                ob = next_obank()
                blocks = []
                for (kcr, qr0, nr, sl0, g) in runs:
                    cc = 4 * kt + kcr
                    ktl = cc // 4
                    if not tile_ok(ktl):
                        continue
                    gname = g if g is not None else kgate(ktl)
                    koff = (kcr + 2) * 128
                    n = nr * 64
                    subs = [(KBw[pr, kch, koff:koff + 128], arena[pr, 14 + kch, qr0 * 64:qr0 * 64 + n], 0, n, gname,
                             vaug(VBw, (kcr + 2) * 768, kch, half), qr0 * 64)]
                    blocks.append((subs, [(0, n, tabB[:, h, sl0 * 64:sl0 * 64 + n], R_tabB)]))

                def postB(ob=ob, pr=pr, po=po, kch=kch, half=half):
                    def st0():
                        P.op("act", lambda e: e.activation(out=rcp[pr, :], in_=psum[ob][po, :], func=AF.Ln), reads=[R_ps[ob]], writes=[R_rcp[half]])
                        P.op("act", lambda e: e.activation(out=rcp[pr, :], in_=rcp[pr, :], func=AF.Exp, scale=-1.0), reads=[R_rcp[half]], writes=[R_rcp[half]])

                    def st1():
                        P.op("dve", lambda e: e.tensor_tensor(out=OB[pr, kch, :], in0=psum[ob][pr, :], in1=rcp[pr, :], op=ALU.mult),
                             reads=[R_ps[ob], R_rcp[half]], writes=[R_OB[kch]])
                    return [st0, st1]
                for bi, (subs, mops) in enumerate(blocks):
                    submit(subs, mops, ob, bi == 0, postB if bi == len(blocks) - 1 else None, [R_K[3], R_ar[14 + kch]], [R_V[3]])
            while pipe:
                pop_one()
            run_due(force=True)

            if qi == 0:
                dump("OA", OA[:], [128, 2, T], BF16, R_OA)
                dump("OB", OB[:], [128, 4, T], BF16, R_OB)
            wa_ap, rwa = load_L(OFF_WA)
            wb0, rwb0 = load_L(OFF_WB)
            wb1, rwb1 = load_L(OFF_WB + 2048)
            wa3 = wa_ap.rearrange("p (k c) -> p k c", c=1024)
            for jc in range(8):
                bs = 0 if jc % 2 == 0 else 4
                if jc % 2 == 0:
                    t0_ap, t0_res, t1_ap, t1_res = ft[:, 0, :], [R_ft[0]], ft[:, 1, :], [R_ft[1]]
                else:
                    t0_ap, t0_res, t1_ap, t1_res = ft[:, 2, :], [R_ft[2]], rcp[:], list(R_rcp)
                for k in range(2):
                    mm(psum[bs][:], wa3[:, k, jc * 128:(jc + 1) * 128], OA[:, k, :], k == 0, k == 1, [rwa, R_OA[k]], [R_ps[bs]])
                wbp, rwb = (wb0, rwb0) if jc < 4 else (wb1, rwb1)
                wb3 = wbp.rearrange("p (k c) -> p k c", c=512)
                for k in range(4):
                    mm(psum[bs + 1][:], wb3[:, k, (jc % 4) * 128:(jc % 4 + 1) * 128], OB[:, k, :], k == 0, k == 3, [rwb, R_OB[k]], [R_ps[bs + 1]])
                for gi in range(2):
                    wg, rwg = load_S(OFF_WFM + (20 + 8 * gi + jc) * 1024)
                    for k in range(8):
                        mm(psum[bs + 2 + gi][:], wg[:, k, :], arena[:, k, :], k == 0, k == 7, [rwg, R_ar[k]], [R_ps[bs + 2 + gi]])
                    P.op("act", lambda e, gi=gi, jc=jc, bs=bs: e.activation(out=sg[:, gi, :], in_=psum[bs + 2 + gi][:], func=AF.Sigmoid, bias=cols[:, 8 * gi + jc:8 * gi + jc + 1]),
                         reads=[R_ps[bs + 2 + gi], R_cols], writes=[R_sg[gi]])
                P.op("dve", lambda e, bs=bs, t0_ap=t0_ap: e.tensor_tensor(out=t0_ap, in0=psum[bs][:], in1=sg[:, 0, :], op=ALU.mult), reads=[R_ps[bs], R_sg[0]], writes=t0_res)
                P.op("dve", lambda e, bs=bs, t1_ap=t1_ap: e.tensor_tensor(out=t1_ap, in0=psum[bs + 1][:], in1=sg[:, 1, :], op=ALU.mult), reads=[R_ps[bs + 1], R_sg[1]], writes=t1_res)
                P.op("pool", lambda e, jc=jc, t0_ap=t0_ap, t1_ap=t1_ap: e.tensor_tensor(out=mrg[:, jc, :], in0=t0_ap, in1=t1_ap, op=ALU.add), reads=t0_res + t1_res, writes=[R_mrg[jc]])
                if jc == 0:
                    wthunks = window_loads(*qtiles[qi + 1]) if qi + 1 < len(qtiles) else []
                nth = (len(wthunks) + 6) // 7
                for th in wthunks[jc * nth:(jc + 1) * nth] if jc < 7 else wthunks[7 * nth:]:
                    th("sp")

            if qi == 0:
                dump("mrg", mrg[:], [128, 8, T], BF16, R_mrg)
            for tc in range(4):
                P.op("pool", lambda e, tc=tc, tok0=tok0: e.dma_start(out=xres[:, tc, :], in_=xs[tok0 + tc * 128:tok0 + (tc + 1) * 128, :]),
                     writes=[R_xres[tc]], dma_key=f"xres{tc}")

            def layer_norm_all(stats_done=False):
                for tc in range(4):
                    for hf in range(2):
                        if stats_done:
                            continue
                        P.op("dve", lambda e, hf=hf, tc=tc: e.bn_stats(out=stats[:, tc, hf, :], in_=xres[:, tc, hf * 512:(hf + 1) * 512]), reads=[R_xres[tc]], writes=[R_stats[tc]])
                    P.op("dve", lambda e, tc=tc: e.bn_aggr(out=mv[:, tc, :], in_=stats[:, tc].rearrange("p a b -> p (a b)")), reads=[R_stats[tc]], writes=[R_stats[tc]])
                P.op("act", lambda e: e.activation(out=rstd[:, 0:4], in_=mv[:, :, 1], func=AF.Sqrt, bias=gate[:, 5:6]), reads=list(R_stats) + [R_gate], writes=[R_rstd])
                P.op("dve", lambda e: e.reciprocal(out=rstd[:, 0:4], in_=rstd[:, 0:4]), reads=[R_rstd], writes=[R_rstd])
                for tc in range(4):
                    P.op("dve", lambda e, tc=tc: e.tensor_scalar(out=xres[:, tc, :], in0=xres[:, tc, :], scalar1=mv[:, tc, 0:1], scalar2=rstd[:, tc:tc + 1], op0=ALU.subtract, op1=ALU.mult),
                         reads=[R_xres[tc], R_stats[tc], R_rstd], writes=[R_xres[tc]])

            for chh in range(2):
                for kh in range(2):
                    wl, rw = load_L(OFF_WO + (2 * chh + kh) * 2048)
                    for tc in range(4):
                        b = 4 + tc
                        for k4 in range(4):
                            k = 4 * kh + k4
                            rhs = wl.rearrange("p (k c) -> p k c", c=512)[:, k4, :]
                            mm(psum[b][:], mrg[:, k, tc * 128:(tc + 1) * 128], rhs, k == 0, k == 7, [rw, R_mrg[k]], [R_ps[b]])
                for tc in range(4):
                    b = 4 + tc
                    P.op("dve", lambda e, tc=tc, b=b, chh=chh: e.scalar_tensor_tensor(out=xres[:, tc, chh * 512:(chh + 1) * 512], in0=xres[:, tc, chh * 512:(chh + 1) * 512],
                                                                                    scalar=ALPHA, in1=psum[b][:], op0=ALU.mult, op1=ALU.add),
                         reads=[R_xres[tc], R_ps[b]], writes=[R_xres[tc]])
                    P.op("dve", lambda e, tc=tc, chh=chh: e.bn_stats(out=stats[:, tc, chh, :], in_=xres[:, tc, chh * 512:(chh + 1) * 512]), reads=[R_xres[tc]], writes=[R_stats[tc]])
            layer_norm_all(stats_done=True)
            if qi == 0:
                dump("z1", xres[:], [128, 4, 1024], F32, R_xres)
            for tc in range(4):
                for fc in range(8):
                    b = fc
                    P.op("pe", lambda e, tc=tc, fc=fc, b=b: e.transpose(out=psum[b][:, tc * 128:(tc + 1) * 128], in_=xres[:, tc, fc * 128:(fc + 1) * 128], identity=ident[:]),
                         reads=[R_xres[tc], R_misc], writes=[R_ps[b]])
            for fc in range(8):
                if fc % 2 == 0:
                    P.op("act", lambda e, fc=fc: e.activation(out=mrg[:, fc, :], in_=psum[fc][:], func=AF.Identity, scale=cols[:, 48 + fc:49 + fc], bias=cols[:, 56 + fc:57 + fc]),
                         reads=[R_ps[fc], R_cols], writes=[R_mrg[fc]])
                else:
                    P.op("dve", lambda e, fc=fc: e.tensor_scalar(out=mrg[:, fc, :], in0=psum[fc][:], scalar1=cols[:, 48 + fc:49 + fc], scalar2=cols[:, 56 + fc:57 + fc], op0=ALU.mult, op1=ALU.add),
                         reads=[R_ps[fc], R_cols], writes=[R_mrg[fc]])
            for tc in range(4):
                P.op("pool", lambda e, tc=tc: e.tensor_tensor(out=xres[:, tc, :], in0=xres[:, tc, :], in1=reps[:, 0, :], op=ALU.mult), reads=[R_xres[tc], R_reps], writes=[R_xres[tc]])
                P.op("pool", lambda e, tc=tc: e.tensor_tensor(out=xres[:, tc, :], in0=xres[:, tc, :], in1=reps[:, 1, :], op=ALU.add), reads=[R_xres[tc], R_reps], writes=[R_xres[tc]])

            if qi == 0:
                dump("x1T", mrg[:], [128, 8, T], BF16, R_mrg)
                dump("x1res", xres[:], [128, 4, 1024], F32, R_xres)
            for hh in range(2):
                for kk in range(16):
                    hk = 16 * hh + kk
                    w1_, rw = load_S(OFF_W1 + hk * 1024)
                    b = kk % 2
                    for k in range(8):
                        mm(psum[b][:], w1_[:, k, :], mrg[:, k, :], k == 0, k == 7, [rw, R_mrg[k]], [R_ps[b]])
                    fi = 2 if kk % 2 == 0 else 0
                    P.op("dve", lambda e, b=b, hk=hk, fi=fi: e.tensor_scalar(out=ft[:, fi, :], in0=psum[b][:], scalar1=cols[:, 16 + hk:17 + hk], scalar2=0.0, op0=ALU.add, op1=ALU.max),
                         reads=[R_ps[b], R_cols], writes=[R_ft[fi]])
                    P.op("pool", lambda e, kk=kk, fi=fi: e.tensor_tensor(out=arena[:, kk, :], in0=ft[:, fi, :], in1=ft[:, fi, :], op=ALU.mult), reads=[R_ft[fi]], writes=[R_ar[kk]])
                for chh in range(2):
                    for kq in range(4):
                        wl, rw = load_L(OFF_W2 + ((hh * 2 + chh) * 4 + kq) * 2048)
                        for tc in range(4):
                            b = 4 + tc
                            for k4 in range(4):
                                k = 4 * kq + k4
                                rhs = wl.rearrange("p (k c) -> p k c", c=512)[:, k4, :]
                                mm(psum[b][:], arena[:, k, tc * 128:(tc + 1) * 128], rhs, k == 0, k == 15, [rw, R_ar[k]], [R_ps[b]])
                    for tc in range(4):
                        b = 4 + tc
                        P.op("dve", lambda e, tc=tc, b=b, chh=chh: e.tensor_tensor(out=xres[:, tc, chh * 512:(chh + 1) * 512], in0=psum[b][:], in1=xres[:, tc, chh * 512:(chh + 1) * 512], op=ALU.add),
                             reads=[R_xres[tc], R_ps[b]], writes=[R_xres[tc]])
                        if hh == 1:
                            P.op("dve", lambda e, tc=tc, chh=chh: e.bn_stats(out=stats[:, tc, chh, :], in_=xres[:, tc, chh * 512:(chh + 1) * 512]), reads=[R_xres[tc]], writes=[R_stats[tc]])
            if qi == 0:
                dump("h2", xres[:], [128, 4, 1024], F32, R_xres)
            layer_norm_all(stats_done=True)
            for tc in range(4):
                P.op("pool", lambda e, tc=tc: e.tensor_tensor(out=xres[:, tc, :], in0=xres[:, tc, :], in1=reps[:, 2, :], op=ALU.mult), reads=[R_xres[tc], R_reps], writes=[R_xres[tc]])
                P.op("pool", lambda e, tc=tc: e.tensor_tensor(out=xres[:, tc, :], in0=xres[:, tc, :], in1=reps[:, 3, :], op=ALU.add), reads=[R_xres[tc], R_reps], writes=[R_xres[tc]])
                od = P.op("pool", lambda e, tc=tc, orow=orow: e.dma_start(out=ys[orow + tc * 128:orow + (tc + 1) * 128, :], in_=xres[:, tc, :]),
                          reads=[R_xres[tc]], dma_key=f"out{tc}")
                out_dmas.append(od)

        if debug:
            nk = n_kv * T
            dump("kscr", kscr[:, :, 0:nk], [10, 128, nk], BF16, [R_scrK])
            dump("vscr", vscr[0:nk, :], [nk, VROW], BF16, [R_scrV])
        P.finalize(final_waits=out_dmas + dbg_dmas)
    return nc


_CACHE = {}


def kernel(x_prompt, x_sample, w_in, b_gate, w_branch_a, w_branch_b, w_out, rel_pos_bias,
           ln1_g, ln1_b, w_ff1, b_ff1, w_ff2, b_ff2, ln2_g, ln2_b):
    f = np.float32
    x_prompt = np.asarray(x_prompt, f)
    x_sample = np.asarray(x_sample, f)
    wall = _host_weights(np.asarray(w_in[0], f), np.asarray(w_branch_a[0], f), np.asarray(w_branch_b[0], f),
                         np.asarray(w_out[0], f), np.asarray(w_ff1[0], f), np.asarray(w_ff2[0], f))
    cols, rep, amask, tabs, misc = _host_consts(np.asarray(b_gate[0], f), np.asarray(rel_pos_bias[0], f),
                                                np.asarray(ln1_g[0], f), np.asarray(ln1_b[0], f), np.asarray(b_ff1[0], f),
                                                np.asarray(b_ff2[0], f), np.asarray(ln2_g[0], f), np.asarray(ln2_b[0], f))
    in_maps = []
    for c in range(8):
        b, q = c // 4, c % 4
        xs = np.zeros((NTOK, 1024), f)
        xs[:8192] = x_prompt[c]
        lo, hi = q * 2048 - 1024, q * 2048 + 3072
        s_lo, s_hi = max(lo, 0), min(hi, 8192)
        xs[8192 + (s_lo - lo):8192 + (s_hi - lo)] = x_sample[b, s_lo:s_hi]
        g = np.zeros((128, 16), f)
        g[:, 1] = 0.0 if q > 0 else NEG
        g[:, 2] = 0.0 if q == 0 else NEG
        g[:, 3] = 0.0 if q < 3 else NEG
        g[:, 4] = 0.0 if q == 3 else NEG
        g[:, 5] = LN_EPS
        g[0:64, 6] = NEG
        g[0:32, 7] = NEG
        g[96:128, 8] = NEG
        if q == 0:
            g[0:64, 9] = NEG
            g[0:32, 10] = NEG
        if q == 3:
            g[96:128, 11] = NEG
        rope = _rope_tables(q).reshape(NKV, 128, 2 * T)
        in_maps.append({"xs": xs, "wall": wall, "ccols": cols, "crep": rep, "camask": amask, "ctabs": tabs,
                        "cmisc": misc, "cgate": g, "crope": np.ascontiguousarray(rope)})
    if "nc" not in _CACHE:
        _CACHE["nc"] = build_program()
    res = run_bass_kernel_spmd(_CACHE["nc"], in_maps, core_ids=list(range(8)))
    y_prompt = np.zeros((8, 8192, 1024), f)
    y_sample = np.zeros((2, 8192, 1024), f)
    for c in range(8):
        ysc = res.results[c]["ys"]
        y_prompt[c] = ysc[:8192]
        b, q = c // 4, c % 4
        y_sample[b, q * 2048:(q + 1) * 2048] = ysc[8192:]
    return (y_prompt, y_sample)
```

```python
import contextlib
import numpy as np
import concourse.bass as bass
import concourse.mybir as mybir
from concourse.bass_utils import run_bass_kernel_spmd

F32 = mybir.dt.float32
BF16 = mybir.dt.bfloat16
AF = mybir.ActivationFunctionType
ALU = mybir.AluOpType

ENGS = ("pe", "act", "dve", "pool", "sp")
NEG = -30000.0
ALPHA = 2.0 ** 0.25
LN_EPS = 1e-5
T = 512
NKV = 24
NTOK = NKV * T
QT_A = 16
QT_B = 4


class Res:
    __slots__ = ("name", "w", "rs")

    def __init__(self, name):
        self.name = name
        self.w = None
        self.rs = []


class Instr:
    __slots__ = ("eng", "fn", "deps", "signal", "dma", "sem", "val")

    def __init__(self, eng, fn, dma):
        self.eng = eng
        self.fn = fn
        self.deps = []
        self.signal = False
        self.dma = dma
        self.sem = None
        self.val = None


class Prog:
    def __init__(self, nc):
        self.nc = nc
        self.lists = {e: [] for e in ENGS}

    def op(self, eng, fn, reads=(), writes=(), dma_key=None):
        ins = Instr(eng, fn, dma_key)
        deps = []
        for r in reads:
            if r.w is not None:
                deps.append(r.w)
        for w in writes:
            if w.w is not None:
                deps.append(w.w)
            deps.extend(w.rs)
        seen = set()
        for d in deps:
            if id(d) in seen:
                continue
            seen.add(id(d))
            if d.eng == eng and d.dma is None and ins.dma is None and eng == "pe":
                continue
            ins.deps.append(d)
        for r in reads:
            r.rs.append(ins)
        for w in writes:
            w.w = ins
            w.rs = []
        self.lists[eng].append(ins)
        return ins

    def finalize(self, final_waits=()):
        nc = self.nc
        for e in ENGS:
            for ins in self.lists[e]:
                for d in ins.deps:
                    d.signal = True
        with contextlib.ExitStack() as stack:
            eng_sem = {e: stack.enter_context(nc.semaphore(f"prog_{e}")) for e in ENGS}
            dma_h = {}
            for e in ENGS:
                for ins in self.lists[e]:
                    if ins.dma is not None and ins.dma not in dma_h:
                        dma_h[ins.dma] = [stack.enter_context(nc.semaphore(f"dma_{len(dma_h)}")), 0]
            for e in ENGS:
                cnt = 0
                for ins in self.lists[e]:
                    if ins.dma is not None:
                        h = dma_h[ins.dma]
                        h[1] += 16
                        ins.sem, ins.val = h[0], h[1]
                    elif ins.signal:
                        cnt += 1
                        ins.sem, ins.val = eng_sem[e], cnt
            block = stack.enter_context(nc.Block())
            engobj = {"pe": block.tensor, "act": block.scalar, "dve": block.vector,
                      "pool": block.gpsimd, "sp": block.sync}

            def make_body(e):
                def body(engine):
                    waited = {}
                    for ins in self.lists[e]:
                        for d in ins.deps:
                            key = id(d.sem)
                            if waited.get(key, 0) >= d.val:
                                continue
                            waited[key] = d.val
                            engine.wait_ge(d.sem, d.val)
                        bi = ins.fn(engine)
                        if ins.dma is not None:
                            bi.then_inc(ins.sem, 16)
                        elif ins.signal:
                            bi.then_inc(ins.sem, 1)
                    if e == "sp":
                        for d in final_waits:
                            if waited.get(id(d.sem), 0) >= d.val:
                                continue
                            waited[id(d.sem)] = d.val
                            engine.wait_ge(d.sem, d.val)
                return body

            for e in ENGS:
                engobj[e](make_body(e))


def _b_entries(ttype):
    out = {}
    for kcr in range(-2, 6):
        for qr in range(8):
            a = 2 * kcr
            r = qr
            if ttype == "int":
                wr = r - 4
                ok = lambda row: wr <= row < wr + 8
            elif ttype == "first":
                wr = max(r - 4, 0)
                ok = lambda row: row >= 0 and wr <= row < wr + 8
            else:
                wr = min(r - 4, 0)
                ok = lambda row: row < 8 and wr <= row < wr + 8
            v0, v1 = int(ok(a)), int(ok(a + 1))
            if v0 or v1:
                out[(kcr, qr)] = (r - a, v0, v1)
    return out


def _b_slots():
    inter = sorted(set(_b_entries("int").values()))
    rest = set()
    for tt in ("first", "last"):
        rest |= set(_b_entries(tt).values())
    rest = sorted(rest - set(inter))
    return inter + rest


B_SLOTS = _b_slots()
NBS = len(B_SLOTS)


def _b_runs(mode):
    items = []
    if mode in ("int", "first", "last"):
        ent = _b_entries(mode)
        for (kcr, qr), e in ent.items():
            items.append((kcr, qr, B_SLOTS.index(e), None))
    else:
        eI = _b_entries("int")
        eX = _b_entries("first" if mode == "q0" else "last")
        gI, gX = ("L", "F") if mode == "q0" else ("R", "Z")
        for kcr in range(-2, 6):
            for qr in range(8):
                a, b = eI.get((kcr, qr)), eX.get((kcr, qr))
                if a is not None and a == b:
                    items.append((kcr, qr, B_SLOTS.index(a), None))
                else:
                    if a is not None:
                        items.append((kcr, qr, B_SLOTS.index(a), gI))
                    if b is not None:
                        items.append((kcr, qr, B_SLOTS.index(b), gX))
    items.sort(key=lambda t: (t[0], str(t[3]), t[1]))
    runs = []
    for (kcr, qr, sl, g) in items:
        if runs:
            k0, q0, n0, s0, g0 = runs[-1]
            if k0 == kcr and g0 == g and q0 + n0 == qr and s0 + n0 == sl:
                runs[-1] = (k0, q0, n0 + 1, s0, g0)
                continue
        runs.append((kcr, qr, 1, sl, g))
    return runs


GATE_COL = {None: 0, "L": 1, "F": 2, "R": 3, "Z": 4, "S64": 6, "S32": 7, "S96": 8, "L64": 9, "L32": 10, "R96": 11}
NAM = 384 + 512 + 2048
VROW = 3 * 384 + 768

C_QA, C_KA, C_VA, C_QB, C_KB, C_VB, C_GA, C_GB = 0, 768, 1536, 2304, 2816, 3328, 3840, 4864


def _fm_chunk(w, col0):
    k = w.shape[0] // 128
    return w[:, col0:col0 + 128].reshape(k, 128, 128).transpose(1, 0, 2)


def _rhs_layout(w):
    k = w.shape[0] // 128
    return w.reshape(k, 128, w.shape[1]).transpose(1, 0, 2)


def _host_weights(w_in, w_a, w_b, w_o, w1, w2):
    f = np.float32
    chunks = []
    for c0 in [C_KA + 128 * i for i in range(6)] + [C_KB + 128 * i for i in range(4)]:
        chunks.append(_fm_chunk(w_in, c0))
    for c0 in [C_QA + 128 * i for i in range(6)] + [C_QB + 128 * i for i in range(4)]:
        chunks.append(_fm_chunk(w_in, c0))
    for c0 in [C_GA + 128 * i for i in range(8)] + [C_GB + 128 * i for i in range(8)]:
        chunks.append(_fm_chunk(w_in, c0))
    wfm = np.stack(chunks, axis=1).reshape(128, -1)
    wv = np.concatenate([_rhs_layout(w_in[:, C_VA + 256 * g:C_VA + 256 * (g + 1)]).reshape(128, -1) for g in range(3)]
                        + [_rhs_layout(w_in[:, C_VB:C_VB + 512]).reshape(128, -1)], axis=1)
    wa = _rhs_layout(w_a).reshape(128, -1)
    wb = _rhs_layout(w_b)
    wb = np.concatenate([wb[:, :, 0:512].reshape(128, -1), wb[:, :, 512:1024].reshape(128, -1)], axis=1)
    wo = _rhs_layout(w_o)
    pieces = []
    for ch in range(2):
        for kh in range(2):
            pieces.append(wo[:, 4 * kh:4 * kh + 4, 512 * ch:512 * ch + 512].reshape(128, -1))
    wo = np.concatenate(pieces, axis=1)
    w1c = np.stack([_fm_chunk(w1, 128 * i) for i in range(32)], axis=1).reshape(128, -1)
    w2l = _rhs_layout(w2)
    pieces = []
    for hh in range(2):
        for ch in range(2):
            for kq in range(4):
                k0 = 16 * hh + 4 * kq
                pieces.append(w2l[:, k0:k0 + 4, 512 * ch:512 * ch + 512].reshape(128, -1))
    w2p = np.concatenate(pieces, axis=1)
    wall = np.concatenate([wfm, wv, wa, wb, wo, w1c, w2p], axis=1).astype(f)
    return np.ascontiguousarray(wall)


OFF_WFM = 0
OFF_WV = 36 * 1024
OFF_WA = OFF_WV + 3 * 2048 + 4096
OFF_WB = OFF_WA + 2048
OFF_WO = OFF_WB + 4096
OFF_W1 = OFF_WO + 8192
OFF_W2 = OFF_W1 + 32 * 1024
W_TOT = OFF_W2 + 32 * 1024


def _host_consts(b_gate, rpb, ln1_g, ln1_b, b_ff1, b_ff2, ln2_g, ln2_b):
    f = np.float32
    cols = np.zeros((128, 64), f)
    cols[:, 0:8] = b_gate[0].reshape(8, 128).T
    cols[:, 8:16] = b_gate[1].reshape(8, 128).T
    cols[:, 16:48] = b_ff1.reshape(32, 128).T
    cols[:, 48:56] = ln1_g.reshape(8, 128).T
    cols[:, 56:64] = ln1_b.reshape(8, 128).T
    rep = np.stack([np.broadcast_to(v[None, :], (128, 1024)) for v in (ln1_g, ln1_b, b_ff2, ln2_g, ln2_b)], axis=1)
    rep = np.ascontiguousarray(rep.reshape(128, 5 * 1024)).astype(f)
    i = np.arange(128)[:, None]
    j = np.arange(128)[None, :]
    ms = []
    for d in (-1, 0, 1):
        ms.append((np.abs(128 * d + i - j) <= 64).astype(f))
    mq = np.arange(32)[None, :]
    ma3 = np.tile((i >= mq).astype(f), (1, 16))
    ms.append(ma3)
    t = np.arange(512)[None, :]
    for c in range(4):
        ms.append((((i % 16) == (t % 16)) & ((8 * c + i // 16) <= (t // 16))).astype(f))
    amask = np.concatenate(ms, axis=1)
    kcol = np.arange(64)[:, None]
    qc = np.arange(64)[None, :]
    wc = np.clip(qc - 8, 0, 48)
    okc = (kcol >= wc) & (kcol < wc + 16)
    dc = np.clip(kcol - qc + 15, 0, 30)
    tabs = np.full((128, 8, NBS * 64), NEG, f)
    for s, (delta, v0, v1) in enumerate(B_SLOTS):
        for krl, v in ((0, v0), (1, v1)):
            if not v:
                continue
            dr = krl - delta + 7
            assert 0 <= dr <= 14
            for h in range(8):
                vals = rpb[h, dr][dc]
                tabs[krl * 64:(krl + 1) * 64, h, s * 64:(s + 1) * 64] = np.where(okc, vals, NEG)
    tabs = np.ascontiguousarray(tabs.reshape(128, -1))
    ident = np.eye(128, dtype=f)
    perm = np.zeros((128, 128), f)
    for m in range(128):
        d = m % 64
        if d < 8:
            perm[m + 8, m] = 1.0
        elif d < 16:
            perm[m - 8, m] = 1.0
    misc = np.concatenate([ident, perm], axis=1)
    return cols, rep, amask, tabs, misc


def _rope_tables(q):
    f = np.float32
    inv = (np.float32(500000.0) ** (-np.arange(8, dtype=f) / np.float32(8))).astype(f)
    pos = np.zeros((NKV, T), f)
    for kt in range(16):
        pos[kt] = kt * T + np.arange(T)
    for u in range(8):
        pos[16 + u] = q * 2048 - 1024 + u * T + np.arange(T)
    ang = pos[:, None, :] * inv[None, :, None]
    cs, sn = np.cos(ang).astype(f), np.sin(ang).astype(f)
    tab = np.zeros((NKV, 128, 2, T), f)
    tab[:, :, 0, :] = 1.0
    for half in (0, 64):
        tab[:, half + 0:half + 8, 0, :] = cs
        tab[:, half + 8:half + 16, 0, :] = cs
        tab[:, half + 0:half + 8, 1, :] = -sn
        tab[:, half + 8:half + 16, 1, :] = sn
    return tab


def build_program(n_q_a=QT_A, n_q_b=QT_B, n_kv=NKV, debug=False):
    nc = bass.Bass("TRN2", target_bir_lowering=False)
    xs = nc.dram_tensor("xs", [NTOK, 1024], F32, kind="ExternalInput").ap()
    wall = nc.dram_tensor("wall", [128, W_TOT], F32, kind="ExternalInput").ap()
    ccols = nc.dram_tensor("ccols", [128, 64], F32, kind="ExternalInput").ap()
    crep = nc.dram_tensor("crep", [128, 5 * 1024], F32, kind="ExternalInput").ap()
    camask = nc.dram_tensor("camask", [128, NAM], F32, kind="ExternalInput").ap()
    ctabs = nc.dram_tensor("ctabs", [128, 8 * NBS * 64], F32, kind="ExternalInput").ap()
    cmisc = nc.dram_tensor("cmisc", [128, 256], F32, kind="ExternalInput").ap()
    cgate = nc.dram_tensor("cgate", [128, 16], F32, kind="ExternalInput").ap()
    crope = nc.dram_tensor("crope", [NKV, 128, 2 * T], F32, kind="ExternalInput").ap()
    ys = nc.dram_tensor("ys", [(QT_A + QT_B) * T, 1024], F32, kind="ExternalOutput").ap()
    wbf = nc.dram_tensor("wbf", [128, W_TOT], BF16, kind="Internal").ap()
    kscr = nc.dram_tensor("kscr", [10, 128, NTOK], BF16, kind="Internal").ap()
    vscr = nc.dram_tensor("vscr", [NTOK, VROW], BF16, kind="Internal").ap()

    P = Prog(nc)
    with contextlib.ExitStack() as st:
        def sb(name, shape, dt):
            return st.enter_context(nc.sbuf_tensor(name, shape, dt))

        NS, NL = 4, 3
        wS = sb("wS", [128, NS, 1024], BF16)
        wL = sb("wL", [128, NL, 2048], BF16)
        R_wS = [Res(f"wS{i}") for i in range(NS)]
        R_wL = [Res(f"wL{i}") for i in range(NL)]
        xst = sb("xst", [128, 2, 1024], F32)
        R_xst = [Res("xst0"), Res("xst1")]
        xres = sb("xres", [128, 4, 1024], F32)
        R_xres = [Res(f"xres{i}") for i in range(4)]
        arena = sb("arena", [128, 18, T], BF16)
        R_ar = [Res(f"ar{i}") for i in range(18)]
        mrg = sb("mrg", [128, 8, T], BF16)
        R_mrg = [Res(f"mrg{i}") for i in range(8)]
        OA = sb("OA", [128, 2, T], BF16)
        OB = sb("OB", [128, 4, T], BF16)
        R_OA = [Res("OA0"), Res("OA1")]
        R_OB = [Res(f"OB{i}") for i in range(4)]
        acc = sb("acc", [128, T], F32)
        R_acc = Res("acc")
        rcp = sb("rcp", [128, T], F32)
        R_rcp = [Res("rcp0"), Res("rcp1")]
        K1w = sb("K1w", [128, 2, 768], BF16)
        K2w = sb("K2w", [128, 2, 1536], BF16)
        K3w = sb("K3w", [128, 2, 2560], BF16)
        KBw = sb("KBw", [128, 4, 1024], BF16)
        R_K = [Res("K1w"), Res("K2w"), Res("K3w"), Res("KBw")]
        V1w = sb("V1w", [128, 6 * 384], BF16)
        Vbig = sb("Vbig", [128, 28 * 384], BF16)
        V2w = Vbig[:, 0:12 * 384]
        V3w = Vbig[:, 12 * 384:28 * 384]
        V3b = sb("V3b", [128, 4 * 384], BF16)
        VBw = sb("VBw", [128, 8 * 768], BF16)
        R_V = [Res("V1w"), Res("V2w"), Res("V3w"), Res("VBw"), Res("V3b")]
        NP = 5
        NSB = 5
        OBANKS = [5, 6, 7]
        DUMMY_BANK = 4
        N_WARM = 0
        Pt = sb("Pt", [128, NP, T], BF16)
        R_Pt = [Res(f"Pt{i}") for i in range(NP)]
        ft = sb("ft", [128, 3, T], F32)
        R_ft = [Res(f"ft{i}") for i in range(3)]
        sg = sb("sg", [128, 2, T], BF16)
        R_sg = [Res("sg0"), Res("sg1")]
        amask = sb("amask", [128, NAM], BF16)
        amask4 = sb("amask4", [128, 3, T], BF16)
        amaskr = sb("amaskr", [128, 384], BF16)
        R_amask = Res("amask")
        R_amask4 = Res("amask4")
        tabB = sb("tabB", [128, 8, NBS * 64], BF16)
        R_tabB = Res("tabB")
        reps = sb("reps", [128, 4, 1024], F32)
        R_reps = Res("reps")
        rope = sb("rope", [128, 2, T], F32)
        R_rope = Res("rope")
        cols = sb("cols", [128, 64], F32)
        R_cols = Res("cols")
        gate = sb("gate", [128, 16], F32)
        R_gate = Res("gate")
        ident = sb("ident", [128, 128], F32)
        permb = sb("permb", [128, 128], BF16)
        R_misc = Res("misc")
        stats = sb("stats", [128, 4, 2, 6], F32)
        mv = sb("mv", [128, 4, 2], F32)
        rstd = sb("rstd", [128, 4], F32)
        R_stats = [Res(f"st{i}") for i in range(4)]
        R_rstd = Res("rstd")
        kst = arena[:, 8:18, :]
        R_kst = R_ar[8:18]
        vst = Vbig[:, 0:4 * VROW].rearrange("p (c n) -> p c n", n=VROW)
        R_vst = [Res(f"vst{i}") for i in range(4)]
        psum = [st.enter_context(nc.psum_tensor(f"ps{i}", [128, T], F32)) for i in range(8)]
        R_ps = [Res(f"ps{i}") for i in range(8)]
        R_wbf = Res("wbf")
        R_scrK = Res("scrK")
        R_scrV = Res("scrV")
        R_kscr = [Res(f"kscr{kt}") for kt in range(NKV)]
        R_vscr = [Res(f"vscr{kt}") for kt in range(NKV)]

        cnt = {"dma": 0, "S": 0, "L": 0, "P": 0, "ev": 0, "mul": 0, "ob": 0}
        dbg_dmas = []

        def dump(name, ap, shape, dt, reads):
            if not debug:
                return
            d = nc.dram_tensor("dbg_" + name, list(shape), dt, kind="ExternalOutput").ap()
            dbg_dmas.append(P.op("sp", lambda e: e.dma_start(out=d, in_=ap), reads=reads, dma_key="dbg_" + name))

        def dkey(prefix):
            cnt["dma"] += 1
            return f"{prefix}{cnt['dma'] % 6}"

        for a0 in range(0, W_TOT, 16384):
            a1 = min(a0 + 16384, W_TOT)
            P.op("pool", lambda e, a0=a0, a1=a1: e.dma_start(
                out=wbf[:, a0:a1].rearrange("p (n f) -> p n f", f=2048),
                in_=wall[:, a0:a1].rearrange("p (n f) -> p n f", f=2048)),
                writes=[R_wbf], dma_key="wcast")
        P.op("sp", lambda e: e.dma_start(out=cols[:], in_=ccols), writes=[R_cols], dma_key="c0")
        P.op("sp", lambda e: e.dma_start(out=gate[:], in_=cgate), writes=[R_gate], dma_key="c1")
        P.op("sp", lambda e: e.dma_start(out=ident[:], in_=cmisc[:, 0:128]), writes=[R_misc], dma_key="c2")
        P.op("pool", lambda e: e.dma_start(out=permb[:], in_=cmisc[:, 128:256]), writes=[R_misc], dma_key="c3")
        P.op("pool", lambda e: e.dma_start(out=amask[:].rearrange("p (a b) -> p a b", b=128), in_=camask.rearrange("p (a b) -> p a b", b=128)), writes=[R_amask], dma_key="c4")
        for d_ in range(3):
            for rep_ in range(4):
                P.op("pool", lambda e, d_=d_, rep_=rep_: e.tensor_copy(out=amask4[:, d_, rep_ * 128:(rep_ + 1) * 128], in_=amask[:, d_ * 128:(d_ + 1) * 128]),
                     reads=[R_amask], writes=[R_amask4])
        for j_ in range(3):
            P.op("pool", lambda e, j_=j_: e.tensor_copy(out=amaskr[:, j_ * 128:(j_ + 1) * 128], in_=amask[:, (2 - j_) * 128:(3 - j_) * 128]),
                 reads=[R_amask], writes=[R_amask4])
        P.op("sp", lambda e: e.dma_start(out=reps[:, 0, :], in_=crep[:, 0:1024]), writes=[R_reps], dma_key="c5_0")
        P.op("sp", lambda e: e.dma_start(out=reps[:, 1, :], in_=crep[:, 1024:2048]), writes=[R_reps], dma_key="c5_1")
        P.op("sp", lambda e: e.dma_start(out=xres[:, 0, :], in_=crep[:, 2048:3072]), writes=[R_xres[0]], dma_key="c5_2")
        P.op("sp", lambda e: e.dma_start(out=reps[:, 2:4, :].rearrange("p a b -> p (a b)"), in_=crep[:, 3072:5120]), writes=[R_reps], dma_key="c5_3")
        P.op("dve", lambda e: e.tensor_scalar(out=reps[:, 0, :], in0=reps[:, 0, :], scalar1=ALPHA, scalar2=None, op0=ALU.mult), reads=[R_reps], writes=[R_reps])
        P.op("dve", lambda e: e.scalar_tensor_tensor(out=reps[:, 1, :], in0=reps[:, 1, :], scalar=ALPHA, in1=xres[:, 0, :], op0=ALU.mult, op1=ALU.add),
             reads=[R_reps, R_xres[0]], writes=[R_reps])
        for h in range(8):
            w = NBS * 64
            r = R_xres[1 + (h % 2)]
            P.op("sp", lambda e, h=h, w=w: e.dma_start(out=xres[:, 1 + (h % 2), 0:w], in_=ctabs[:, h * w:(h + 1) * w]), writes=[r], dma_key=f"tb{h % 2}")
            P.op("act", lambda e, h=h, w=w: e.activation(out=tabB[:, h, :], in_=xres[:, 1 + (h % 2), 0:w], func=AF.Exp), reads=[r], writes=[R_tabB])
        for (w_, r_) in ((K1w, R_K[0]), (K2w, R_K[1]), (K3w, R_K[2]), (KBw, R_K[3])):
            P.op("pool", lambda e, w_=w_: e.memset(w_[:].rearrange("p a b -> p (a b)"), 0.0), writes=[r_])
        for (w_, r_) in ((V1w, [R_V[0]]), (Vbig, [R_V[1], R_V[2]]), (VBw, [R_V[3]]), (V3b, [R_V[4]])):
            P.op("pool", lambda e, w_=w_: e.memset(w_[:], 0.0), writes=r_)
        for sl in range(4):
            P.op("pool", lambda e, sl=sl: e.memset(vst[:, sl, :].rearrange("p (a b c) -> p a b c", b=3, c=64)[:, :, 1, :], 1.0),
                 reads=[R_V[1], R_V[2]], writes=[R_vst[sl]])

        def load_S(off):
            i = cnt["S"] % NS
            cnt["S"] += 1
            P.op("sp", lambda e, i=i, off=off: e.dma_start(out=wS[:, i, :], in_=wbf[:, off:off + 1024]),
                 reads=[R_wbf], writes=[R_wS[i]], dma_key=f"wS{i}")
            return wS[:, i, :].rearrange("p (k c) -> p k c", c=128), R_wS[i]

        def load_L(off):
            i = cnt["L"] % NL
            cnt["L"] += 1
            P.op("sp", lambda e, i=i, off=off: e.dma_start(out=wL[:, i, :], in_=wbf[:, off:off + 2048]),
                 reads=[R_wbf], writes=[R_wL[i]], dma_key=f"wL{i}")
            return wL[:, i, :], R_wL[i]

        def mm(out, lhsT, rhs, start, stop, reads, writes, skip=False):
            P.op("pe", lambda e: e.matmul(out, lhsT=lhsT, rhs=rhs, start=start, stop=stop, skip_group_check=skip),
                 reads=reads, writes=writes)

        def evac_copy(out, in_, reads, writes):
            k = cnt["ev"] % 2
            cnt["ev"] += 1
            if k == 0:
                P.op("act", lambda e: e.activation(out=out, in_=in_, func=AF.Copy), reads=reads, writes=writes)
            else:
                P.op("dve", lambda e: e.tensor_copy(out=out, in_=in_), reads=reads, writes=writes)

        qt_list = [("A", i) for i in range(n_q_a)] + [("B", u) for u in range(n_q_b)]
        xseq = list(range(n_kv)) + [(i if j_ == "A" else 18 + i) for (j_, i) in qt_list]
        xstate = {"i": 0}

        mrg32 = mrg[:].rearrange("p a b -> p (a b)").bitcast(F32).rearrange("p (s n) -> p s n", n=1024)
        ring1 = [(xres[:, i, :], [R_xres[i]]) for i in range(4)] + [(xst[:, i, :], [R_xst[i]]) for i in range(2)]
        ring2 = [(xst[:, 0, :], [R_xst[0]]), (xst[:, 1, :], [R_xst[1]]), (mrg32[:, 0, :], list(R_mrg[0:4])), (mrg32[:, 1, :], list(R_mrg[4:8]))]
        n1 = 4 * n_kv

        def xslot(g):
            if g < n1:
                return ring1[g % 6], g % 6
            k_ = (g - n1) % 4
            return ring2[k_], (4 + k_ if k_ < 2 else 4 + k_)
        xstate["req"] = 0
        xocc = {}

        def xrequest_upto(g, own=False):
            total = 4 * len(xseq)
            while xstate["req"] < total:
                r = xstate["req"]
                (ap_, res_), sid = xslot(r)
                prev = xocc.get(sid)
                if prev is not None and prev >= g:
                    break
                if sid >= 6 and not (own and (r // 4) == (g // 4)):
                    break
                xocc[sid] = r
                kt_, tc = xseq[r // 4], r % 4
                P.op("sp", lambda e, ap_=ap_, kt_=kt_, tc=tc: e.dma_start(out=ap_, in_=xs[kt_ * T + tc * 128: kt_ * T + (tc + 1) * 128, :]),
                     writes=res_, dma_key="x" + res_[0].name)
                xstate["req"] += 1

        def make_xT(kt):
            idx = xstate["i"]
            xstate["i"] += 1
            assert xseq[idx] == kt
            xrequest_upto(4 * idx, own=True)
            for rnd in range(2):
                for tc in range(4):
                    g = 4 * idx + tc
                    (ap_, res_), sid = xslot(g)
                    for f4 in range(4):
                        fc = 4 * rnd + f4
                        P.op("pe", lambda e, tc=tc, ap_=ap_, fc=fc, f4=f4: e.transpose(out=psum[f4][:, tc * 128:(tc + 1) * 128], in_=ap_[:, fc * 128:(fc + 1) * 128], identity=ident[:]),
                             reads=res_ + [R_misc], writes=[R_ps[f4]])
                for f4 in range(4):
                    fc = 4 * rnd + f4
                    if idx >= n_kv:
                        P.op("act", lambda e, fc=fc, f4=f4: e.activation(out=arena[:, fc, :], in_=psum[f4][:], func=AF.Copy), reads=[R_ps[f4]], writes=[R_ar[fc]])
                    else:
                        evac_copy(arena[:, fc, :], psum[f4][:], [R_ps[f4]], [R_ar[fc]])
            xrequest_upto(4 * (idx + 1))

        def rope_chunk(src_bank, dst_ap, dst_res, bankB, add_eng="pool"):
            i = cnt["P"] % NP
            cnt["P"] += 1
            P.op("act", lambda e: e.activation(out=Pt[:, i, :], in_=psum[src_bank][:], func=AF.Copy), reads=[R_ps[src_bank]], writes=[R_Pt[i]])

            def fin():
                mm(psum[bankB][:], permb[:], Pt[:, i, :], True, True, [R_Pt[i], R_misc], [R_ps[bankB]])
                P.op("dve", lambda e: e.tensor_tensor(out=ft[:, 0, :], in0=psum[bankB][:], in1=rope[:, 1, :], op=ALU.mult), reads=[R_ps[bankB], R_rope], writes=[R_ft[0]])
                P.op("dve", lambda e: e.tensor_tensor(out=ft[:, 1, :], in0=psum[src_bank][:], in1=rope[:, 0, :], op=ALU.mult), reads=[R_ps[src_bank], R_rope], writes=[R_ft[1]])
                P.op(add_eng, lambda e: e.tensor_tensor(out=dst_ap, in0=ft[:, 0, :], in1=ft[:, 1, :], op=ALU.add), reads=[R_ft[0], R_ft[1]], writes=[dst_res])
            return fin

        def proj_chunks(w_base, dst_fn, resident=None):
            pend = None
            for c in range(10):
                if resident is not None:
                    wv_, rw = resident[c]
                else:
                    wv_, rw = load_S(OFF_WFM + (w_base + c) * 1024)
                b = c % 4
                for k in range(8):
                    mm(psum[b][:], wv_[:, k, :], arena[:, k, :], k == 0, k == 7, [rw, R_ar[k]], [R_ps[b]])
                if pend is not None:
                    pend()
                    pend = None
                dst_ap, dst_res = dst_fn(c)
                if c < 6:
                    pend = rope_chunk(b, dst_ap, dst_res, 4 + (c % 4), add_eng=("pool" if resident is not None else "dve"))
                else:
                    evac_copy(dst_ap, psum[b][:], [R_ps[b]], [dst_res])
            if pend is not None:
                pend()

        def load_rope(kt):
            P.op("sp", lambda e: e.dma_start(out=rope[:].rearrange("p a b -> p (a b)"), in_=crope[kt]), writes=[R_rope], dma_key="rope")

        k3f = K3w[:].rearrange("p a b -> p (a b)")
        kbf = KBw[:].rearrange("p a b -> p (a b)")
        k2f = K2w[:].rearrange("p a b -> p (a b)")
        mrgf = mrg[:].rearrange("p a b -> p (a b)")
        kres_slots = [(k3f[:, i * 1024:(i + 1) * 1024], R_K[2]) for i in range(5)] + \
                     [(kbf[:, i * 1024:(i + 1) * 1024], R_K[3]) for i in range(4)] + [(k2f[:, 0:1024], R_K[1])]
        wk_res = []
        for c in range(10):
            ap_, r_ = kres_slots[c]
            P.op("sp", lambda e, ap_=ap_, c=c: e.dma_start(out=ap_, in_=wbf[:, OFF_WFM + c * 1024:OFF_WFM + (c + 1) * 1024]),
                 reads=[R_wbf], writes=[r_], dma_key=f"p1k{c}")
            wk_res.append((ap_.rearrange("p (k c) -> p k c", c=128), r_))
        vres_slots = [(VBw[:, 0:2048], [R_V[3]]), (VBw[:, 2048:4096], [R_V[3]]), (VBw[:, 4096:6144], [R_V[3]]),
                      (V1w[:, 0:2048], [R_V[0]]), (mrgf[:, 0:2048], list(R_mrg[0:4]))]
        wv_res = []
        for pc in range(5):
            ap_, r_ = vres_slots[pc]
            P.op("sp", lambda e, ap_=ap_, pc=pc: e.dma_start(out=ap_, in_=wbf[:, OFF_WV + pc * 2048:OFF_WV + (pc + 1) * 2048]),
                 reads=[R_wbf], writes=r_, dma_key=f"p1v{pc}")
            wv_res.append((ap_, r_))
        for kt in range(n_kv):
            make_xT(kt)
            load_rope(kt)
            proj_chunks(0, lambda c: (kst[:, c, :], R_kst[c]), resident=wk_res)
            P.op("pool", lambda e, kt=kt: e.dma_start(out=kscr[:, :, kt * T:(kt + 1) * T].rearrange("c p t -> p c t"), in_=kst),
                 reads=list(R_kst), writes=[R_scrK], dma_key="scrK")
            for g in range(4):
                ncol = 512 if g == 3 else 256
                woff = OFF_WV + (g * 2048 if g < 3 else 3 * 2048)
                pieces = [wv_res[g]] if g < 3 else [wv_res[3], wv_res[4]]
                for ch in range(4):
                    b = 4 + (ch % 2) + 2 * (g % 2)
                    for k in range(8):
                        lt = arena[:, k, ch * 128:(ch + 1) * 128]
                        if g < 3:
                            wl, rw = pieces[0]
                            rhs = wl.rearrange("p (k c) -> p k c", c=256)[:, k, :]
                        else:
                            wl, rw = pieces[k // 4]
                            rhs = wl.rearrange("p (k c) -> p k c", c=512)[:, k % 4, :]
                        mm(psum[b][:, 0:ncol], lt, rhs, k == 0, k == 7, list(rw) + [R_ar[k]], [R_ps[b]])
                    npair = ncol // 128
                    c0 = g * 384
                    sl = ch
                    dst = vst[:, sl, c0:c0 + npair * 192].rearrange("p (a b c) -> p a b c", b=3, c=64)[:, :, 0:3:2, :]
                    src = psum[b][:, 0:ncol].rearrange("p (a b c) -> p a b c", b=2, c=64)
                    evac_copy(dst, src, [R_ps[b]], [R_vst[sl]])
            P.op("pool", lambda e, kt=kt: e.dma_start(out=vscr[kt * T:(kt + 1) * T, :].rearrange("(c p) n -> p c n", p=128), in_=vst),
                 reads=list(R_vst), writes=[R_scrV], dma_key="scrV")

        out_dmas = []
        qtiles = [("A", i) for i in range(n_q_a)] + [("B", u) for u in range(n_q_b)]

        def tile_params(job, ti):
            if job == "A":
                return dict(kt=ti, kt_lo=0, kt_hi=16, bmode="first" if ti == 0 else ("last" if ti == 15 else "int"), orow=ti * T,
                            kgate=lambda ktile: None)
            return dict(kt=18 + ti, kt_lo=16, kt_hi=24, bmode="q0" if ti == 0 else ("q3" if ti == 3 else "int"), orow=(QT_A + ti) * T,
                        kgate=lambda ktile: "L" if ktile < 18 else ("R" if ktile > 21 else None))

        def window_loads(job, ti):
            tp_ = tile_params(job, ti)
            kt, kt_lo, kt_hi = tp_["kt"], tp_["kt_lo"], tp_["kt_hi"]
            tok0 = kt * T
            lo_tok, hi_tok = kt_lo * T, kt_hi * T
            thunks = []

            def kload(win, rw, chs, t_lo, t_hi):
                a, b_ = max(t_lo, lo_tok), min(t_hi, hi_tok)
                src = kscr[chs[0]:chs[-1] + 1, :, a:b_].rearrange("c p t -> p c t")
                thunks.append(lambda WQ: P.op(WQ, lambda e: e.dma_start(out=win[:, :, a - t_lo:b_ - t_lo], in_=src),
                                              reads=[R_scrK, R_scrV], writes=[rw], dma_key="w" + rw.name))
            kload(K1w, R_K[0], [0, 1], tok0 - 128, tok0 + 640)
            kload(K2w, R_K[1], [2, 3], tok0 - 512, tok0 + 1024)
            kload(K3w, R_K[2], [4, 5], tok0 - 1024, tok0 + 1536)
            kload(KBw, R_K[3], [6, 7, 8, 9], tok0 - 256, tok0 + 768)

            def vgather(win, rv, wcol0, col0, ncol, row0, dims, p_lo=0, p_hi=128, pstride=1):
                if p_hi <= p_lo:
                    return
                nch = 1
                for (_, c_) in dims:
                    nch *= c_
                src = bass.AP(tensor=vscr.tensor, offset=(row0 + pstride * p_lo) * VROW + col0,
                              ap=[[pstride * VROW, p_hi - p_lo]] + [[rs * VROW, c_] for (rs, c_) in dims] + [[1, ncol]])
                dst = win[p_lo:p_hi, wcol0:wcol0 + nch * ncol]
                if len(dims) == 1:
                    dst = dst.rearrange("p (a n) -> p a n", n=ncol)
                elif len(dims) == 2:
                    dst = dst.rearrange("p (a b n) -> p a b n", b=dims[1][1], n=ncol)
                thunks.append(lambda WQ: P.op(WQ, lambda e: e.dma_start(out=dst, in_=src), reads=[R_scrK, R_scrV], writes=[rv], dma_key="w" + rv.name))

            def tile_ok(ktl):
                return kt_lo <= ktl < kt_hi
            ccs = [4 * kt - 1 + s_ for s_ in range(6) if tile_ok((4 * kt - 1 + s_) // 4)]
            vgather(V1w, R_V[0], (ccs[0] - (4 * kt - 1)) * 384, 0, 384, ccs[0] * 128, [(128, len(ccs))])
            for d in range(3):
                if tile_ok(kt - 1 + d):
                    vgather(V2w, R_V[1], 4 * d * 384, 384, 384, (kt - 1 + d) * T, [(1, 4)], pstride=4)
            base3 = tok0 - 1024
            i_lo = max(0, (lo_tok - base3) // 16)
            i_hi = min(128, (hi_tok - base3) // 16)
            for r0 in range(0, 16, 4):
                vgather(V3w, R_V[2], r0 * 384, 768, 384, base3 + r0, [(1, 4)], i_lo, i_hi, pstride=16)
            if tile_ok(kt + 2):
                vgather(V3b, R_V[4], 0, 768, 384, (kt + 2) * T, [(128, 4)])
            ccs = [4 * kt - 2 + s_ for s_ in range(8) if tile_ok((4 * kt - 2 + s_) // 4)]
            half_n = (len(ccs) + 1) // 2
            for part in (ccs[:half_n], ccs[half_n:]):
                if part:
                    vgather(VBw, R_V[3], (part[0] - (4 * kt - 2)) * 768, 1152, 768, part[0] * 128, [(128, len(part))])
            return thunks

        if qtiles:
            for th in window_loads(*qtiles[0]):
                th("sp")
        for qi, (job, ti) in enumerate(qtiles):
            tp_ = tile_params(job, ti)
            kt, kt_lo, kt_hi, bmode, orow, kgate = tp_["kt"], tp_["kt_lo"], tp_["kt_hi"], tp_["bmode"], tp_["orow"], tp_["kgate"]
            tok0 = kt * T

            def tile_ok(ktl, kt_lo=kt_lo, kt_hi=kt_hi):
                return kt_lo <= ktl < kt_hi

            make_xT(kt)
            load_rope(kt)
            proj_chunks(10, lambda c: (arena[:, 8 + c, :], R_ar[8 + c]))

            if qi == 0:
                dump("xT", arena[:, 0:8, :], [128, 8, T], BF16, R_ar[0:8])
                dump("Q", arena[:, 8:18, :], [128, 10, T], BF16, R_ar[8:18])
                dump("K3w", K3w[:], [128, 2, 2560], BF16, [R_K[2]])
                dump("KBw", KBw[:], [128, 4, 1024], BF16, [R_K[3]])
                dump("V1w", V1w[:], [128, 6 * 384], BF16, [R_V[0]])
                dump("V3w", V3w, [128, 16 * 384], BF16, [R_V[2]])
                dump("VBw", VBw[:], [128, 8 * 768], BF16, [R_V[3]])
                dump("tabB", tabB[:], [128, 8, NBS * 64], BF16, [R_tabB])
            DEPTH = 4
            pipe = []

            def vaug(win, chunk_off, pair, half):
                o = chunk_off + pair * 192 + 64 * half
                return win[:, o:o + 128]

            post_sched = []
            POST_LAG = 2

            def run_due(force=False):
                keep = []
                for item in post_sched:
                    if force or item[0] <= 0:
                        item[1]()
                    else:
                        keep.append(item)
                post_sched[:] = keep

            def pop_one():
                fn = pipe.pop(0)
                fn()
                for item in post_sched:
                    item[0] -= 1
                run_due()

            def submit(subs, mask_ops, obank, first_flag, post=None, sres=(), vres=()):
                i = cnt["P"] % NP
                cnt["P"] += 1
                sbk = i % NSB
                for j, (kap, qap, c0, n, g, vap, oc0) in enumerate(subs):
                    mm(psum[sbk][:, c0:c0 + n], kap, qap, j == 0, False, list(sres), [R_ps[sbk]], skip=True)
                j = 0
                while j < len(subs):
                    j2 = j
                    while j2 + 1 < len(subs) and subs[j2 + 1][4] == subs[j][4] and subs[j2 + 1][2] == subs[j2][2] + subs[j2][3]:
                        j2 += 1
                    c0 = subs[j][2]
                    c1 = subs[j2][2] + subs[j2][3]
                    gc = GATE_COL[subs[j][4]]
                    P.op("act", lambda e, i=i, sbk=sbk, c0=c0, c1=c1, gc=gc: e.activation(out=Pt[:, i, c0:c1], in_=psum[sbk][:, c0:c1], func=AF.Exp, scale=0.125, bias=gate[:, gc:gc + 1]),
                         reads=[R_ps[sbk], R_gate], writes=[R_Pt[i]])
                    j = j2 + 1
                for (c0, n, map_, mres) in mask_ops:
                    eng = "dve"
                    cnt["mul"] += 1
                    P.op(eng, lambda e, i=i, c0=c0, n=n, map_=map_: e.tensor_tensor(out=Pt[:, i, c0:c0 + n], in0=Pt[:, i, c0:c0 + n], in1=map_, op=ALU.mult),
                         reads=[R_Pt[i], mres], writes=[R_Pt[i]])

                for _ in range(N_WARM):
                    P.op("pe", lambda e: e.matmul(psum[DUMMY_BANK][:], lhsT=permb[:], rhs=amask4[:, 0, :], start=True, stop=True), reads=[], writes=[R_ps[DUMMY_BANK]])

                def pv(i=i, subs=subs, first_flag=first_flag, post=post):
                    ff = first_flag
                    for (kap, qap, c0, n, g, vap, oc0) in subs:
                        mm(psum[obank][:, oc0:oc0 + n], vap, Pt[:, i, c0:c0 + n], ff, False, list(vres) + [R_Pt[i]], [R_ps[obank]], skip=True)
                        ff = False
                    if post is not None:
                        for k_, st_ in enumerate(post()):
                            post_sched.append([POST_LAG + k_, st_])
                pipe.append(pv)
                while len(pipe) > DEPTH:
                    pop_one()

            def next_obank():
                b_ = OBANKS[cnt["ob"] % len(OBANKS)]
                cnt["ob"] += 1
                return b_

            if job == "A":
                g3gate = {0: "S64", 1: "S32", 15: "S96"}.get(ti)
            else:
                g3gate = {0: "L64", 1: "L32", 3: "R96"}.get(ti)

            for hs in range(4):
                half = hs % 2
                pr = slice(64 * half, 64 * half + 64)
                po = slice(64 * (1 - half), 64 * (1 - half) + 64)
                qch = hs // 2
                acc_ap, acc_res = (acc[:], R_acc) if hs % 2 == 0 else (ft[:, 2, :], R_ft[2])
                ob1 = next_obank()
                blocks = []
                for s_ in range(6):
                    cc = 4 * kt - 1 + s_
                    ktl = cc // 4
                    if not tile_ok(ktl):
                        continue
                    q_lo, q_hi = max(0, s_ - 2), min(3, s_)
                    n = 128 * (q_hi - q_lo + 1)
                    j_lo = q_lo - s_ + 2
                    koff = s_ * 128
                    subs = [(K1w[pr, qch, koff:koff + 128], arena[pr, 8 + qch, q_lo * 128:q_lo * 128 + n], 0, n, kgate(ktl),
                             vaug(V1w, s_ * 384, qch, half), q_lo * 128)]
                    blocks.append((subs, [(0, n, amaskr[:, j_lo * 128:j_lo * 128 + n], R_amask4)], [R_K[0], R_ar[8 + qch]], [R_V[0]]))
                if tile_ok(kt + 2):
                    for c_ in range(4):
                        koff = 2048 + c_ * 128
                        q0_ = 128 * c_
                        subs = [(K3w[pr, qch, koff:koff + 128], arena[pr, 12 + qch, q0_:T], 0, T - q0_, kgate(kt + 2), vaug(V3b, c_ * 384, qch, half), q0_)]
                        blocks.append((subs, [(0, T - q0_, amask[:, 896 + 512 * c_ + q0_:896 + 512 * (c_ + 1)], R_amask)], [R_K[2], R_ar[12 + qch]], [R_V[4]]))

                def post1(ob1=ob1, acc_ap=acc_ap, acc_res=acc_res):
                    return [lambda: P.op("act", lambda e: e.activation(out=acc_ap, in_=psum[ob1][:], func=AF.Copy), reads=[R_ps[ob1]], writes=[acc_res])]
                for bi, (subs, mops, sres, vres) in enumerate(blocks):
                    submit(subs, mops, ob1, bi == 0, post1 if bi == len(blocks) - 1 else None, sres, vres)
                ob2 = next_obank()
                blocks = []
                for d in (-1, 0, 1):
                    ktl = kt + d
                    if not tile_ok(ktl):
                        continue
                    kb0 = (ktl * T) - (tok0 - 512)
                    subs = []
                    for r4 in range(4):
                        sl = 4 * (d + 1) + r4
                        subs.append((K2w[pr, qch, kb0 + r4:kb0 + T:4], arena[pr, 10 + qch, r4:T:4], r4 * 128, 128, kgate(ktl),
                                     vaug(V2w, sl * 384, qch, half), r4 * 128))
                    blocks.append((subs, [(0, T, amask4[:, d + 1, :], R_amask4)], [R_K[1], R_ar[10 + qch]], [R_V[1]]))

                def post2(ob2=ob2, acc_ap=acc_ap, acc_res=acc_res):
                    return [lambda: P.op("dve", lambda e: e.tensor_tensor(out=acc_ap.rearrange("p (m r) -> p m r", r=4), in0=acc_ap.rearrange("p (m r) -> p m r", r=4),
                                                                          in1=psum[ob2][:].rearrange("p (r m) -> p m r", r=4), op=ALU.add),
                                         reads=[R_ps[ob2], acc_res], writes=[acc_res])]
                for bi, (subs, mops, sres, vres) in enumerate(blocks):
                    submit(subs, mops, ob2, bi == 0, post2 if bi == len(blocks) - 1 else None, sres, vres)
                ob3 = next_obank()
                subs = []
                for r in range(16):
                    subs.append((K3w[pr, qch, r:2048:16], arena[pr, 12 + qch, r:T:16], r * 32, 32, g3gate, vaug(V3w, r * 384, qch, half), r * 32))

                def post3(ob3=ob3, acc_ap=acc_ap, acc_res=acc_res, pr=pr, po=po, qch=qch, half=half):
                    def st0():
                        P.op("dve", lambda e: e.tensor_tensor(out=acc_ap.rearrange("p (m r) -> p m r", r=16), in0=acc_ap.rearrange("p (m r) -> p m r", r=16),
                                                              in1=psum[ob3][:].rearrange("p (r m) -> p m r", r=16), op=ALU.add),
                             reads=[R_ps[ob3], acc_res], writes=[acc_res])

                    def st1():
                        P.op("act", lambda e: e.activation(out=rcp[pr, :], in_=acc_ap[po, :], func=AF.Ln), reads=[acc_res], writes=[R_rcp[half]])
                        P.op("act", lambda e: e.activation(out=rcp[pr, :], in_=rcp[pr, :], func=AF.Exp, scale=-1.0), reads=[R_rcp[half]], writes=[R_rcp[half]])

                    def st2():
                        P.op("dve", lambda e: e.tensor_tensor(out=OA[pr, qch, :], in0=acc_ap[pr, :], in1=rcp[pr, :], op=ALU.mult),
                             reads=[acc_res, R_rcp[half]], writes=[R_OA[qch]])
                    return [st0, st1, st2]
                submit(subs, [(0, T, amask[:, 384:896], R_amask)], ob3, True, post3, [R_K[2], R_ar[12 + qch]], [R_V[2]])

            runs = _b_runs(bmode)
            for h in range(8):
                half = h % 2
                pr = slice(64 * half, 64 * half + 64)
                po = slice(64 * (1 - half), 64 * (1 - half) + 64)
                kch = h // 2
                ob = next_obank()
                blocks = []
                for (kcr, qr0, nr, sl0, g) in runs:
                    cc = 4 * kt + kcr
                    ktl = cc // 4
                    if not tile_ok(ktl):
                        continue
                    gname = g if g is not None else kgate(ktl)
                    koff = (kcr + 2) * 128
                    n = nr * 64
                    subs = [(KBw[pr, kch, koff:koff + 128], arena[pr, 14 + kch, qr0 * 64:qr0 * 64 + n], 0, n, gname,
                             vaug(VBw, (kcr + 2) * 768, kch, half), qr0 * 64)]
                    blocks.append((subs, [(0, n, tabB[:, h, sl0 * 64:sl0 * 64 + n], R_tabB)]))

                def postB(ob=ob, pr=pr, po=po, kch=kch, half=half):
                    def st0():
                        P.op("act", lambda e: e.activation(out=rcp[pr, :], in_=psum[ob][po, :], func=AF.Ln), reads=[R_ps[ob]], writes=[R_rcp[half]])
                        P.op("act", lambda e: e.activation(out=rcp[pr, :], in_=rcp[pr, :], func=AF.Exp, scale=-1.0), reads=[R_rcp[half]], writes=[R_rcp[half]])

                    def st1():
                        P.op("dve", lambda e: e.tensor_tensor(out=OB[pr, kch, :], in0=psum[ob][pr, :], in1=rcp[pr, :], op=ALU.mult),
                             reads=[R_ps[ob], R_rcp[half]], writes=[R_OB[kch]])
                    return [st0, st1]
                for bi, (subs, mops) in enumerate(blocks):
                    submit(subs, mops, ob, bi == 0, postB if bi == len(blocks) - 1 else None, [R_K[3], R_ar[14 + kch]], [R_V[3]])
            while pipe:
                pop_one()
            run_due(force=True)

            if qi == 0:
                dump("OA", OA[:], [128, 2, T], BF16, R_OA)
                dump("OB", OB[:], [128, 4, T], BF16, R_OB)
            wa_ap, rwa = load_L(OFF_WA)
            wb0, rwb0 = load_L(OFF_WB)
            wb1, rwb1 = load_L(OFF_WB + 2048)
            wa3 = wa_ap.rearrange("p (k c) -> p k c", c=1024)
            for jc in range(8):
                bs = 0 if jc % 2 == 0 else 4
                if jc % 2 == 0:
                    t0_ap, t0_res, t1_ap, t1_res = ft[:, 0, :], [R_ft[0]], ft[:, 1, :], [R_ft[1]]
                else:
                    t0_ap, t0_res, t1_ap, t1_res = ft[:, 2, :], [R_ft[2]], rcp[:], list(R_rcp)
                for k in range(2):
                    mm(psum[bs][:], wa3[:, k, jc * 128:(jc + 1) * 128], OA[:, k, :], k == 0, k == 1, [rwa, R_OA[k]], [R_ps[bs]])
                wbp, rwb = (wb0, rwb0) if jc < 4 else (wb1, rwb1)
                wb3 = wbp.rearrange("p (k c) -> p k c", c=512)
                for k in range(4):
                    mm(psum[bs + 1][:], wb3[:, k, (jc % 4) * 128:(jc % 4 + 1) * 128], OB[:, k, :], k == 0, k == 3, [rwb, R_OB[k]], [R_ps[bs + 1]])
                for gi in range(2):
                    wg, rwg = load_S(OFF_WFM + (20 + 8 * gi + jc) * 1024)
                    for k in range(8):
                        mm(psum[bs + 2 + gi][:], wg[:, k, :], arena[:, k, :], k == 0, k == 7, [rwg, R_ar[k]], [R_ps[bs + 2 + gi]])
                    P.op("act", lambda e, gi=gi, jc=jc, bs=bs: e.activation(out=sg[:, gi, :], in_=psum[bs + 2 + gi][:], func=AF.Sigmoid, bias=cols[:, 8 * gi + jc:8 * gi + jc + 1]),
                         reads=[R_ps[bs + 2 + gi], R_cols], writes=[R_sg[gi]])
                P.op("dve", lambda e, bs=bs, t0_ap=t0_ap: e.tensor_tensor(out=t0_ap, in0=psum[bs][:], in1=sg[:, 0, :], op=ALU.mult), reads=[R_ps[bs], R_sg[0]], writes=t0_res)
                P.op("dve", lambda e, bs=bs, t1_ap=t1_ap: e.tensor_tensor(out=t1_ap, in0=psum[bs + 1][:], in1=sg[:, 1, :], op=ALU.mult), reads=[R_ps[bs + 1], R_sg[1]], writes=t1_res)
                P.op("dve", lambda e, jc=jc, t0_ap=t0_ap, t1_ap=t1_ap: e.tensor_tensor(out=mrg[:, jc, :], in0=t0_ap, in1=t1_ap, op=ALU.add), reads=t0_res + t1_res, writes=[R_mrg[jc]])
                if jc == 0:
                    wthunks = window_loads(*qtiles[qi + 1]) if qi + 1 < len(qtiles) else []
                nth = (len(wthunks) + 6) // 7
                for th in wthunks[jc * nth:(jc + 1) * nth] if jc < 7 else wthunks[7 * nth:]:
                    th("sp")

            if qi == 0:
                dump("mrg", mrg[:], [128, 8, T], BF16, R_mrg)
            for tc in range(4):
                P.op("pool", lambda e, tc=tc, tok0=tok0: e.dma_start(out=xres[:, tc, :], in_=xs[tok0 + tc * 128:tok0 + (tc + 1) * 128, :]),
                     writes=[R_xres[tc]], dma_key=f"xres{tc}")

            def layer_norm_all(stats_done=False):
                for tc in range(4):
                    for hf in range(2):
                        if stats_done:
                            continue
                        P.op("dve", lambda e, hf=hf, tc=tc: e.bn_stats(out=stats[:, tc, hf, :], in_=xres[:, tc, hf * 512:(hf + 1) * 512]), reads=[R_xres[tc]], writes=[R_stats[tc]])
                    P.op("dve", lambda e, tc=tc: e.bn_aggr(out=mv[:, tc, :], in_=stats[:, tc].rearrange("p a b -> p (a b)")), reads=[R_stats[tc]], writes=[R_stats[tc]])
                P.op("act", lambda e: e.activation(out=rstd[:, 0:4], in_=mv[:, :, 1], func=AF.Sqrt, bias=gate[:, 5:6]), reads=list(R_stats) + [R_gate], writes=[R_rstd])
                P.op("dve", lambda e: e.reciprocal(out=rstd[:, 0:4], in_=rstd[:, 0:4]), reads=[R_rstd], writes=[R_rstd])
                for tc in range(4):
                    P.op("dve", lambda e, tc=tc: e.tensor_scalar(out=xres[:, tc, :], in0=xres[:, tc, :], scalar1=mv[:, tc, 0:1], scalar2=rstd[:, tc:tc + 1], op0=ALU.subtract, op1=ALU.mult),
                         reads=[R_xres[tc], R_stats[tc], R_rstd], writes=[R_xres[tc]])

            for chh in range(2):
                for kh in range(2):
                    wl, rw = load_L(OFF_WO + (2 * chh + kh) * 2048)
                    for tc in range(4):
                        b = 4 + tc
                        for k4 in range(4):
                            k = 4 * kh + k4
                            rhs = wl.rearrange("p (k c) -> p k c", c=512)[:, k4, :]
                            mm(psum[b][:], mrg[:, k, tc * 128:(tc + 1) * 128], rhs, k == 0, k == 7, [rw, R_mrg[k]], [R_ps[b]])
                for tc in range(4):
                    b = 4 + tc
                    P.op("dve", lambda e, tc=tc, b=b, chh=chh: e.scalar_tensor_tensor(out=xres[:, tc, chh * 512:(chh + 1) * 512], in0=xres[:, tc, chh * 512:(chh + 1) * 512],
                                                                                    scalar=ALPHA, in1=psum[b][:], op0=ALU.mult, op1=ALU.add),
                         reads=[R_xres[tc], R_ps[b]], writes=[R_xres[tc]])
                    P.op("dve", lambda e, tc=tc, chh=chh: e.bn_stats(out=stats[:, tc, chh, :], in_=xres[:, tc, chh * 512:(chh + 1) * 512]), reads=[R_xres[tc]], writes=[R_stats[tc]])
            layer_norm_all(stats_done=True)
            if qi == 0:
                dump("z1", xres[:], [128, 4, 1024], F32, R_xres)
            for tc in range(4):
                for fc in range(8):
                    b = fc
                    P.op("pe", lambda e, tc=tc, fc=fc, b=b: e.transpose(out=psum[b][:, tc * 128:(tc + 1) * 128], in_=xres[:, tc, fc * 128:(fc + 1) * 128], identity=ident[:]),
                         reads=[R_xres[tc], R_misc], writes=[R_ps[b]])
            for fc in range(8):
                if fc % 2 == 0:
                    P.op("act", lambda e, fc=fc: e.activation(out=mrg[:, fc, :], in_=psum[fc][:], func=AF.Identity, scale=cols[:, 48 + fc:49 + fc], bias=cols[:, 56 + fc:57 + fc]),
                         reads=[R_ps[fc], R_cols], writes=[R_mrg[fc]])
                else:
                    P.op("dve", lambda e, fc=fc: e.tensor_scalar(out=mrg[:, fc, :], in0=psum[fc][:], scalar1=cols[:, 48 + fc:49 + fc], scalar2=cols[:, 56 + fc:57 + fc], op0=ALU.mult, op1=ALU.add),
                         reads=[R_ps[fc], R_cols], writes=[R_mrg[fc]])
            for tc in range(4):
                P.op("pool", lambda e, tc=tc: e.tensor_tensor(out=xres[:, tc, :], in0=xres[:, tc, :], in1=reps[:, 0, :], op=ALU.mult), reads=[R_xres[tc], R_reps], writes=[R_xres[tc]])
                P.op("pool", lambda e, tc=tc: e.tensor_tensor(out=xres[:, tc, :], in0=xres[:, tc, :], in1=reps[:, 1, :], op=ALU.add), reads=[R_xres[tc], R_reps], writes=[R_xres[tc]])

            if qi == 0:
                dump("x1T", mrg[:], [128, 8, T], BF16, R_mrg)
                dump("x1res", xres[:], [128, 4, 1024], F32, R_xres)
            for hh in range(2):
                for kk in range(16):
                    hk = 16 * hh + kk
                    w1_, rw = load_S(OFF_W1 + hk * 1024)
                    b = kk % 2
                    for k in range(8):
                        mm(psum[b][:], w1_[:, k, :], mrg[:, k, :], k == 0, k == 7, [rw, R_mrg[k]], [R_ps[b]])
                    fi = 2 if kk % 2 == 0 else 0
                    P.op("dve", lambda e, b=b, hk=hk, fi=fi: e.tensor_scalar(out=ft[:, fi, :], in0=psum[b][:], scalar1=cols[:, 16 + hk:17 + hk], scalar2=0.0, op0=ALU.add, op1=ALU.max),
                         reads=[R_ps[b], R_cols], writes=[R_ft[fi]])
                    P.op("pool", lambda e, kk=kk, fi=fi: e.tensor_tensor(out=arena[:, kk, :], in0=ft[:, fi, :], in1=ft[:, fi, :], op=ALU.mult), reads=[R_ft[fi]], writes=[R_ar[kk]])
                for chh in range(2):
                    for kq in range(4):
                        wl, rw = load_L(OFF_W2 + ((hh * 2 + chh) * 4 + kq) * 2048)
                        for tc in range(4):
                            b = 4 + tc
                            for k4 in range(4):
                                k = 4 * kq + k4
                                rhs = wl.rearrange("p (k c) -> p k c", c=512)[:, k4, :]
                                mm(psum[b][:], arena[:, k, tc * 128:(tc + 1) * 128], rhs, k == 0, k == 15, [rw, R_ar[k]], [R_ps[b]])
                    for tc in range(4):
                        b = 4 + tc
                        P.op("dve", lambda e, tc=tc, b=b, chh=chh: e.tensor_tensor(out=xres[:, tc, chh * 512:(chh + 1) * 512], in0=psum[b][:], in1=xres[:, tc, chh * 512:(chh + 1) * 512], op=ALU.add),
                             reads=[R_xres[tc], R_ps[b]], writes=[R_xres[tc]])
                        if hh == 1:
                            P.op("dve", lambda e, tc=tc, chh=chh: e.bn_stats(out=stats[:, tc, chh, :], in_=xres[:, tc, chh * 512:(chh + 1) * 512]), reads=[R_xres[tc]], writes=[R_stats[tc]])
            if qi == 0:
                dump("h2", xres[:], [128, 4, 1024], F32, R_xres)
            layer_norm_all(stats_done=True)
            for tc in range(4):
                P.op("pool", lambda e, tc=tc: e.tensor_tensor(out=xres[:, tc, :], in0=xres[:, tc, :], in1=reps[:, 2, :], op=ALU.mult), reads=[R_xres[tc], R_reps], writes=[R_xres[tc]])
                P.op("pool", lambda e, tc=tc: e.tensor_tensor(out=xres[:, tc, :], in0=xres[:, tc, :], in1=reps[:, 3, :], op=ALU.add), reads=[R_xres[tc], R_reps], writes=[R_xres[tc]])
                od = P.op("pool", lambda e, tc=tc, orow=orow: e.dma_start(out=ys[orow + tc * 128:orow + (tc + 1) * 128, :], in_=xres[:, tc, :]),
                          reads=[R_xres[tc]], dma_key=f"out{tc}")
                out_dmas.append(od)

        if debug:
            nk = n_kv * T
            dump("kscr", kscr[:, :, 0:nk], [10, 128, nk], BF16, [R_scrK])
            dump("vscr", vscr[0:nk, :], [nk, VROW], BF16, [R_scrV])
        P.finalize(final_waits=out_dmas + dbg_dmas)
    return nc


_CACHE = {}


def kernel(x_prompt, x_sample, w_in, b_gate, w_branch_a, w_branch_b, w_out, rel_pos_bias,
           ln1_g, ln1_b, w_ff1, b_ff1, w_ff2, b_ff2, ln2_g, ln2_b):
    f = np.float32
    x_prompt = np.asarray(x_prompt, f)
    x_sample = np.asarray(x_sample, f)
    wall = _host_weights(np.asarray(w_in[0], f), np.asarray(w_branch_a[0], f), np.asarray(w_branch_b[0], f),
                         np.asarray(w_out[0], f), np.asarray(w_ff1[0], f), np.asarray(w_ff2[0], f))
    cols, rep, amask, tabs, misc = _host_consts(np.asarray(b_gate[0], f), np.asarray(rel_pos_bias[0], f),
                                                np.asarray(ln1_g[0], f), np.asarray(ln1_b[0], f), np.asarray(b_ff1[0], f),
                                                np.asarray(b_ff2[0], f), np.asarray(ln2_g[0], f), np.asarray(ln2_b[0], f))
    in_maps = []
    for c in range(8):
        b, q = c // 4, c % 4
        xs = np.zeros((NTOK, 1024), f)
        xs[:8192] = x_prompt[c]
        lo, hi = q * 2048 - 1024, q * 2048 + 3072
        s_lo, s_hi = max(lo, 0), min(hi, 8192)
        xs[8192 + (s_lo - lo):8192 + (s_hi - lo)] = x_sample[b, s_lo:s_hi]
        g = np.zeros((128, 16), f)
        g[:, 1] = 0.0 if q > 0 else NEG
        g[:, 2] = 0.0 if q == 0 else NEG
        g[:, 3] = 0.0 if q < 3 else NEG
        g[:, 4] = 0.0 if q == 3 else NEG
        g[:, 5] = LN_EPS
        g[0:64, 6] = NEG
        g[0:32, 7] = NEG
        g[96:128, 8] = NEG
        if q == 0:
            g[0:64, 9] = NEG
            g[0:32, 10] = NEG
        if q == 3:
            g[96:128, 11] = NEG
        rope = _rope_tables(q).reshape(NKV, 128, 2 * T)
        in_maps.append({"xs": xs, "wall": wall, "ccols": cols, "crep": rep, "camask": amask, "ctabs": tabs,
                        "cmisc": misc, "cgate": g, "crope": np.ascontiguousarray(rope)})
    if "nc" not in _CACHE:
        _CACHE["nc"] = build_program()
    res = run_bass_kernel_spmd(_CACHE["nc"], in_maps, core_ids=list(range(8)))
    y_prompt = np.zeros((8, 8192, 1024), f)
    y_sample = np.zeros((2, 8192, 1024), f)
    for c in range(8):
        ysc = res.results[c]["ys"]
        y_prompt[c] = ysc[:8192]
        b, q = c // 4, c % 4
        y_sample[b, q * 2048:(q + 1) * 2048] = ysc[8192:]
    return (y_prompt, y_sample)
```
